# Optimizing a Trainium2 kernel written in Bass

```python
import jax
import jax.numpy as jnp
from jax import lax
import numpy as np

D_MODEL = 1024
BATCH = 4
SEQ = 4096
DEPTH = 4
DEC_BATCH = 32
DEC_SEQ = 1
PAST_LEN = 8192
PAGE_SIZE = 128

N_MIXERS = 3
ATTN_GROUPS = ((128, 1), (512, 4), (2048, 16))
N_GROUPS = len(ATTN_GROUPS)
HEAD_DIM = 64
N_HEADS = D_MODEL // HEAD_DIM
ATTN_SCALE = HEAD_DIM ** -0.5
ROPE_THETA = 10000.0
POOL_WINDOWS = (2, 4, 8, 16)
POOL_GROUP = D_MODEL // len(POOL_WINDOWS)
POOL_PAST = max(POOL_WINDOWS) - 1
CONV_W = 3
D_FF = ((8 * D_MODEL // 3 + 127) // 128) * 128
ALPHA = (2.0 * DEPTH) ** 0.25
BETA = (8.0 * DEPTH) ** -0.25
LN_EPS = 1e-5
N_ATTN = len(range(0, DEPTH, N_MIXERS))
N_POOL = len(range(1, DEPTH, N_MIXERS))
N_SCONV = len(range(2, DEPTH, N_MIXERS))
F32 = jnp.float32

kernel_name = "dilated_pool_shortconv_hybrid_step"


def layer_norm(x, g, b):
    xf = x.astype(F32)
    mu = xf.mean(-1, keepdims=True)
    var = jnp.square(xf - mu).mean(-1, keepdims=True)
    return ((xf - mu) * lax.rsqrt(var + LN_EPS) * g + b).astype(x.dtype)


def rope(x, pos):
    half = HEAD_DIM // 2
    inv = ROPE_THETA ** (-jnp.arange(half, dtype=F32) / half)
    ang = pos.astype(F32)[:, None] * inv[None, :]
    cos = jnp.cos(ang)[None, :, None, None, :]
    sin = jnp.sin(ang)[None, :, None, None, :]
    xf = x.astype(F32)
    x1, x2 = xf[..., :half], xf[..., half:]
    return jnp.concatenate([x1 * cos - x2 * sin, x2 * cos + x1 * sin], -1).astype(x.dtype)


def causal_dwconv(z, prev, k):
    zp = jnp.concatenate([prev.astype(z.dtype), z], axis=1)
    T = z.shape[1]
    y = zp[:, 0:T] * k[0]
    for j in range(1, CONV_W):
        y = y + zp[:, j:j + T] * k[j]
    return y, zp[:, -(CONV_W - 1):]


def attn_qkv(x, w_qkv, pos0):
    B, T, _ = x.shape
    qkv = (x @ w_qkv).reshape(B, T, N_GROUPS, 3, N_HEADS, HEAD_DIM)
    pos = pos0 + jnp.arange(T)
    return rope(qkv[:, :, :, 0], pos), rope(qkv[:, :, :, 1], pos), qkv[:, :, :, 2]


def dilated_attn_prompt(q, k, v, window, dil):
    B, S, H, Dh = q.shape
    L = window // dil
    n = S // dil
    nb = -(-n // L)
    n_pad = nb * L
    Z = B * dil

    def to_res(t):
        t = t.reshape(B, n, dil, H, Dh).transpose(0, 2, 1, 3, 4).reshape(Z, n, H, Dh)
        return jnp.pad(t, ((0, 0), (0, n_pad - n), (0, 0), (0, 0)))

    def band(t):
        tp = jnp.pad(t, ((0, 0), (L, 0), (0, 0), (0, 0))).reshape(Z, nb + 1, L, H, Dh)
        return jnp.concatenate([tp[:, :-1], tp[:, 1:]], axis=2)

    qb = to_res(q).reshape(Z, nb, L, H, Dh)
    kb = band(to_res(k))
    vb = band(to_res(v)).astype(F32)
    s = jnp.einsum('znqhd,znkhd->znhqk', qb, kb, preferred_element_type=F32) * ATTN_SCALE
    qi = jnp.arange(L)[:, None]
    km = jnp.arange(2 * L)[None, :]
    delta = qi + L - km
    key_idx = jnp.arange(nb)[:, None, None] * L - L + km[None]
    mask = (delta >= 0) & (delta <= L) & (key_idx >= 0)
    s = jnp.where(mask[None, :, None], s, -jnp.inf)
    lse = jax.nn.logsumexp(s, axis=-1)
    p = jnp.exp(s - lse[..., None])
    o = jnp.einsum('znhqk,znkhd->znqhd', p, vb)
    o = o.reshape(B, dil, n_pad, H, Dh)[:, :, :n].transpose(0, 2, 1, 3, 4).reshape(B, S, H, Dh)
    lse = lse.transpose(0, 1, 3, 2).reshape(B, dil, n_pad, H)[:, :, :n]
    lse = lse.transpose(0, 2, 1, 3).reshape(B, S, H)
    return o, lse


def dilated_attn_step(q, kv_new, buf, window, dil):
    Lb = buf.shape[1]
    T = q.shape[1]
    ctx = jnp.concatenate([buf.astype(kv_new.dtype), kv_new], axis=1)
    nk = window // dil + 1
    idx = Lb + jnp.arange(T)[:, None] - dil * jnp.arange(nk)[None, :]
    valid = idx >= 0
    g = jnp.take(ctx, jnp.maximum(idx, 0), axis=1)
    s = jnp.einsum('bthd,btkhd->bthk', q, g[:, :, :, 0], preferred_element_type=F32) * ATTN_SCALE
    s = jnp.where(valid[None, :, None, :], s, -jnp.inf)
    lse = jax.nn.logsumexp(s, axis=-1)
    p = jnp.exp(s - lse[..., None])
    o = jnp.einsum('bthk,btkhd->bthd', p, g[:, :, :, 1].astype(F32))
    return o, lse


def merge_groups(outs, lses):
    o = jnp.stack(outs, axis=2)
    w = jax.nn.softmax(jnp.stack(lses, axis=2), axis=2)
    B, T = o.shape[0], o.shape[1]
    return jnp.einsum('btghd,btgh->bthd', o, w).reshape(B, T, N_HEADS * HEAD_DIM)


def pool_mixer(x, prev, w_in, w_grp, scale, w_out, pos0):
    B, T, _ = x.shape
    u = x @ w_in
    ctx = jnp.concatenate([prev.astype(u.dtype), u], axis=1)
    csum = jnp.concatenate([jnp.zeros((B, 1, D_MODEL), F32), jnp.cumsum(ctx.astype(F32), axis=1)], axis=1)
    pos = pos0 + jnp.arange(T)
    means = []
    for gi, w in enumerate(POOL_WINDOWS):
        cs = csum[..., gi * POOL_GROUP:(gi + 1) * POOL_GROUP]
        win_sum = cs[:, POOL_PAST + 1:] - cs[:, POOL_PAST + 1 - w:POOL_PAST + 1 - w + T]
        cnt = jnp.minimum(w, pos + 1).astype(F32)[None, :, None]
        means.append(win_sum / cnt)
    pooled = jnp.concatenate(means, axis=-1) - u.astype(F32)
    z = jnp.einsum('btgc,gcd->btgd', pooled.reshape(B, T, len(POOL_WINDOWS), POOL_GROUP), w_grp.astype(F32))
    z = (z.reshape(B, T, D_MODEL) * scale).astype(x.dtype)
    return z @ w_out, ctx[:, -POOL_PAST:]


def short_conv_mixer(x, prev, w_in, k, w_out):
    gb, gc, h = jnp.split(x @ w_in, 3, axis=-1)
    zc, new = causal_dwconv(gc * h, prev, k)
    return (gb * zc) @ w_out, new


def conv_ffn(x, prev, w_up, k, w_down):
    a, b = jnp.split(x @ w_up, 2, axis=-1)
    ac, new = causal_dwconv(a, prev, k)
    return (jax.nn.gelu(ac, approximate=False) * b) @ w_down, new


def trunk(x, pos0, kv_bufs, pool_st, sconv_st, ffn_st,
          attn_w_qkv, attn_w_o, pool_w_in, pool_w_grp, pool_scale, pool_w_out,
          sconv_w_in, sconv_k, sconv_w_out, ffn_w_up, ffn_k, ffn_w_down, ln_g, ln_b):
    B, T, _ = x.shape
    new_kv = [[] for _ in ATTN_GROUPS]
    new_pool, new_sconv, new_ffn = [], [], []
    ia = ib = ic = 0
    for i in range(DEPTH):
        kind = i % N_MIXERS
        if kind == 0:
            q, k, v = attn_qkv(x, attn_w_qkv[ia], pos0)
            outs, lses = [], []
            for g, (win, dil) in enumerate(ATTN_GROUPS):
                qg, kg, vg = q[:, :, g], k[:, :, g], v[:, :, g]
                kv_new = jnp.stack([kg, vg], axis=2)
                if kv_bufs is None:
                    o, l = dilated_attn_prompt(qg, kg, vg, win, dil)
                    new_kv[g].append(kv_new[:, -min(win, T):])
                else:
                    buf = kv_bufs[g][ia]
                    o, l = dilated_attn_step(qg, kv_new, buf, win, dil)
                    new_kv[g].append(jnp.concatenate([buf.astype(kv_new.dtype), kv_new], axis=1)[:, -buf.shape[1]:])
                outs.append(o)
                lses.append(l)
            mix = merge_groups(outs, lses).astype(x.dtype) @ attn_w_o[ia]
            ia += 1
        elif kind == 1:
            prev = jnp.zeros((B, POOL_PAST, D_MODEL), x.dtype) if pool_st is None else pool_st[ib]
            mix, st = pool_mixer(x, prev, pool_w_in[ib], pool_w_grp[ib], pool_scale[ib], pool_w_out[ib], pos0)
            new_pool.append(st)
            ib += 1
        else:
            prev = jnp.zeros((B, CONV_W - 1, D_MODEL), x.dtype) if sconv_st is None else sconv_st[ic]
            mix, st = short_conv_mixer(x, prev, sconv_w_in[ic], sconv_k[ic], sconv_w_out[ic])
            new_sconv.append(st)
            ic += 1
        x = layer_norm(ALPHA * x + mix, ln_g[i, 0], ln_b[i, 0])
        prev = jnp.zeros((B, CONV_W - 1, D_FF), x.dtype) if ffn_st is None else ffn_st[i]
        f, st = conv_ffn(x, prev, ffn_w_up[i], ffn_k[i], ffn_w_down[i])
        new_ffn.append(st)
        x = layer_norm(ALPHA * x + f, ln_g[i, 1], ln_b[i, 1])
    kv_out = [jnp.stack(lst) for lst in new_kv]
    return x, kv_out, jnp.stack(new_pool), jnp.stack(new_sconv), jnp.stack(new_ffn)


def setup_inputs(seed: int = 0) -> dict:
    key = jax.random.key(seed)
    ks = jax.random.split(key, 24)

    def nrm(k, shape, s=1.0):
        return jax.random.normal(k, shape, jnp.float32) * s

    qkv_cols = N_GROUPS * 3 * N_HEADS * HEAD_DIM
    kv_shape = lambda w: (N_ATTN, DEC_BATCH, min(w, PAST_LEN), 2, N_HEADS, HEAD_DIM)
    return {
        "x_prompt": nrm(ks[0], (BATCH, SEQ, D_MODEL)),
        "x_sample": nrm(ks[1], (DEC_BATCH, DEC_SEQ, D_MODEL)),
        "cache_kv_w128": nrm(ks[2], kv_shape(ATTN_GROUPS[0][0])),
        "cache_kv_w512": nrm(ks[3], kv_shape(ATTN_GROUPS[1][0])),
        "cache_kv_w2048": nrm(ks[4], kv_shape(ATTN_GROUPS[2][0])),
        "state_pool": nrm(ks[5], (N_POOL, DEC_BATCH, POOL_PAST, D_MODEL)),
        "state_sconv": nrm(ks[6], (N_SCONV, DEC_BATCH, CONV_W - 1, D_MODEL)),
        "state_ffn_conv": nrm(ks[7], (DEPTH, DEC_BATCH, CONV_W - 1, D_FF)),
        "attn_w_qkv": nrm(ks[8], (N_ATTN, D_MODEL, qkv_cols), D_MODEL ** -0.5),
        "attn_w_o": nrm(ks[9], (N_ATTN, N_HEADS * HEAD_DIM, D_MODEL), BETA * (N_HEADS * HEAD_DIM) ** -0.5),
        "pool_w_in": nrm(ks[10], (N_POOL, D_MODEL, D_MODEL), D_MODEL ** -0.5),
        "pool_w_grp": nrm(ks[11], (N_POOL, len(POOL_WINDOWS), POOL_GROUP, POOL_GROUP), POOL_GROUP ** -0.5),
        "pool_scale": 1.0 + nrm(ks[12], (N_POOL, D_MODEL), 0.02),
        "pool_w_out": nrm(ks[13], (N_POOL, D_MODEL, D_MODEL), BETA * D_MODEL ** -0.5),
        "sconv_w_in": nrm(ks[14], (N_SCONV, D_MODEL, 3 * D_MODEL), D_MODEL ** -0.5),
        "sconv_k": nrm(ks[15], (N_SCONV, CONV_W, D_MODEL), CONV_W ** -0.5),
        "sconv_w_out": nrm(ks[16], (N_SCONV, D_MODEL, D_MODEL), BETA * D_MODEL ** -0.5),
        "ffn_w_up": nrm(ks[17], (DEPTH, D_MODEL, 2 * D_FF), D_MODEL ** -0.5),
        "ffn_k": nrm(ks[18], (DEPTH, CONV_W, D_FF), CONV_W ** -0.5),
        "ffn_w_down": nrm(ks[19], (DEPTH, D_FF, D_MODEL), BETA * D_FF ** -0.5),
        "ln_g": 1.0 + nrm(ks[20], (DEPTH, 2, D_MODEL), 0.02),
        "ln_b": nrm(ks[21], (DEPTH, 2, D_MODEL), 0.02),
    }


def reference(x_prompt, x_sample, cache_kv_w128, cache_kv_w512, cache_kv_w2048,
              state_pool, state_sconv, state_ffn_conv,
              attn_w_qkv, attn_w_o, pool_w_in, pool_w_grp, pool_scale, pool_w_out,
              sconv_w_in, sconv_k, sconv_w_out, ffn_w_up, ffn_k, ffn_w_down, ln_g, ln_b):
    weights = (attn_w_qkv, attn_w_o, pool_w_in, pool_w_grp, pool_scale, pool_w_out,
               sconv_w_in, sconv_k, sconv_w_out, ffn_w_up, ffn_k, ffn_w_down, ln_g, ln_b)
    y_prompt, kv_p, pool_p, sconv_p, ffn_p = trunk(
        x_prompt, 0, None, None, None, None, *weights)
    y_sample, kv_s, pool_s, sconv_s, ffn_s = trunk(
        x_sample, PAST_LEN, (cache_kv_w128, cache_kv_w512, cache_kv_w2048),
        state_pool, state_sconv, state_ffn_conv, *weights)
    return (y_prompt, y_sample, kv_p[0], kv_s[0], kv_p[1], kv_s[1], kv_p[2], kv_s[2],
            pool_p, pool_s, sconv_p, sconv_s, ffn_p, ffn_s)
```

```python
import numpy as np
import concourse.bass as bass
import concourse.mybir as mybir
from concourse.bass_utils import run_bass_kernel_spmd

F32 = mybir.dt.float32
BF16 = mybir.dt.bfloat16
ALU = mybir.AluOpType
AF = mybir.ActivationFunctionType
AX = mybir.AxisListType


class Buf:
    __slots__ = ("name", "w", "r", "excl")

    def __init__(self, name="", excl=False):
        self.name = name
        self.w = None
        self.r = []
        self.excl = excl


class Ev:
    __slots__ = ("eng", "sem", "val", "resolved")

    def __init__(self, eng):
        self.eng = eng
        self.sem = None
        self.val = None
        self.resolved = False


def L(method, *args, **kw):
    return lambda e: getattr(e, method)(*args, **kw)


class Sched:
    ROLL = 30000
    NDMA = 24

    def __init__(self, nc, stack):
        self.nc = nc
        self.stack = stack
        self.engs = ["pe", "act", "dve", "pool", "sp"]
        self.ops = {e: [] for e in self.engs}
        self.cur_sem = {}
        self.cur_cnt = {}
        self.pending = {e: [] for e in self.engs}
        self.waited = {e: {} for e in self.engs}
        for e in self.engs:
            self._new_sem(e)
        self.dma_sems = [stack.enter_context(nc.semaphore(f"dq{i}")) for i in range(self.NDMA)]
        self.dma_cnt = [0] * self.NDMA
        self.dma_rr = 0
        self.nops = 0

    def _new_sem(self, e):
        self.cur_sem[e] = self.stack.enter_context(self.nc.semaphore(f"s_{e}_{len(self.ops[e])}"))
        self.cur_cnt[e] = 0

    def _collect(self, eng, reads, writes):
        evs = []
        for b in reads:
            if b.w is not None:
                evs.append(b.w)
            if b.excl:
                evs.extend(e for e in b.r if e.eng != eng)
        for b in writes:
            if b.w is not None:
                evs.append(b.w)
            evs.extend(b.r)
        waits = []
        for ev in evs:
            if ev.eng == eng and ev.eng == "pe":
                continue
            if not ev.resolved:
                if ev.eng == eng:
                    continue
                raise RuntimeError("dependency on unresolved (unsignaled) event")
            sid = id(ev.sem)
            if self.waited[eng].get(sid, -1) >= ev.val:
                continue
            self.waited[eng][sid] = ev.val
            waits.append((ev.sem, ev.val))
        return waits

    def _mark(self, ev, reads, writes):
        for b in reads:
            b.r.append(ev)
        for b in writes:
            b.w = ev
            b.r = []

    def op(self, eng, fn, reads=(), writes=(), signal=True):
        self.nops += 1
        waits = self._collect(eng, reads, writes)
        ev = Ev(eng)
        if signal:
            if self.cur_cnt[eng] >= self.ROLL:
                self._new_sem(eng)
            self.cur_cnt[eng] += 1
            ev.sem = self.cur_sem[eng]
            ev.val = self.cur_cnt[eng]
            ev.resolved = True
            for p in self.pending[eng]:
                p.sem, p.val, p.resolved = ev.sem, ev.val, True
            self.pending[eng] = []
            self.ops[eng].append((waits, fn, (ev.sem, 1)))
        else:
            self.pending[eng].append(ev)
            self.ops[eng].append((waits, fn, None))
        self._mark(ev, reads, writes)
        return ev

    def dma(self, out_ap, in_ap, reads=(), writes=(), eng="sp", **kw):
        self.nops += 1
        waits = self._collect(eng, reads, writes)
        s = self.dma_rr
        self.dma_rr = (self.dma_rr + 1) % self.NDMA
        sem = self.dma_sems[s]
        if self.dma_cnt[s] > 0:
            sid = id(sem)
            if self.waited[eng].get(sid, -1) < self.dma_cnt[s]:
                self.waited[eng][sid] = self.dma_cnt[s]
                waits.append((sem, self.dma_cnt[s]))
        self.dma_cnt[s] += 16
        ev = Ev("dma")
        ev.sem, ev.val, ev.resolved = sem, self.dma_cnt[s], True
        fn = L("dma_start", out=out_ap, in_=in_ap, **kw)
        self.ops[eng].append((waits, fn, (sem, 16)))
        self._mark(ev, reads, writes)
        return ev

    def wait_all(self, eng, evs):
        waits = []
        for ev in evs:
            waits.append((ev.sem, ev.val))
        self.ops[eng].append((waits, None, None))

    def final_wait_dmas(self, eng="sp"):
        waits = [(self.dma_sems[s], self.dma_cnt[s]) for s in range(self.NDMA) if self.dma_cnt[s] > 0]
        self.ops[eng].append((waits, None, None))

    def replay(self):
        nc = self.nc
        with nc.Block() as block:
            def run(e_name):
                def body(eng):
                    for waits, fn, sig in self.ops[e_name]:
                        for (sem, val) in waits:
                            eng.wait_ge(sem, val)
                        if fn is None:
                            continue
                        ins = fn(eng)
                        if sig is not None:
                            ins.then_inc(sig[0], sig[1])
                return body
            block.tensor(run("pe"))
            block.scalar(run("act"))
            block.vector(run("dve"))
            block.gpsimd(run("pool"))
            block.sync(run("sp"))

from contextlib import ExitStack

T = 4096
NT = 32
D = 1024
DFF = 2816
NJ = 22
DEPTH = 4
ALPHA = (2.0 * DEPTH) ** 0.25
LN_EPS = 1e-5
GROUPS = ((128, 1), (512, 4), (2048, 16))
POOL_W = (2, 4, 8, 16)
PAST = 8192
NS = 4
FCH = (12, 12, 8)
DBG = {}


def build(with_sample=True, phases=None):
    nc = bass.Bass("TRN2", target_bir_lowering=False)
    din = lambda n, shp: nc.dram_tensor(n, list(shp), F32, kind="ExternalInput").ap()
    dout = lambda n, shp: nc.dram_tensor(n, list(shp), F32, kind="ExternalOutput").ap()
    x_in = din("x", [T, D]); xs_in = din("xs", [NS, D])
    wqkv = din("wqkv", [2, 8, 3, D, 384]); wo = din("wo", [2, D, D])
    wup = din("wup", [DEPTH, NJ, D, 256]); wdn = din("wdn", [DEPTH, DFF, D]); ffk = din("ffk", [DEPTH, 128, NJ, 3])
    ffk_tm = din("ffk_tm", [DEPTH, 3, DFF])
    pwin = din("pwin", [D, D]); pwgrp = din("pwgrp", [4, 256, 256]); pscale = din("pscale", [128, 8]); pwout = din("pwout", [D, D])
    prcnt = din("prcnt", [128, 8, 16])
    swin = din("swin", [8, D, 384]); sk = din("sk", [128, 8, 3]); sk_tm = din("sk_tm", [3, D]); swout = din("swout", [D, D])
    lng = din("lng", [DEPTH, 2, D]); lnb = din("lnb", [DEPTH, 2, D])
    rope = [din(f"rope{d}", [128, NT, 2, 32]) for (_, d) in GROUPS]
    rope_s = din("rope_s", [NS, 2, 32])
    band_in = din("band", [128, 256]); ident_in = din("ident", [128, 128])
    sel_in = din("sel", [NS, NS, 128]); oneh_in = din("oneh", [128, NS, NS])
    c_in = [din(f"c{w}", [2, NS, w, 2048]) for (w, _) in GROUPS]
    st_pool = din("st_pool", [NS, 15, D]); st_sconv = din("st_sconv", [NS, 2, D]); st_ffn = din("st_ffn", [DEPTH, NS, 2, DFF])

    y_out = dout("y", [T, D]); ys_out = dout("ys", [NS, D])
    kvp = [dout(f"kv{w}p", [2, w, 2048]) for (w, _) in GROUPS]
    kvs = [dout(f"kv{w}s", [2, NS, w, 2048]) for (w, _) in GROUPS]
    poolp = dout("poolp", [15, D]); pools = dout("pools", [NS, 15, D])
    sconvp = dout("sconvp", [2, D]); sconvs = dout("sconvs", [NS, 2, D])
    ffnp = dout("ffnp", [DEPTH, 2, DFF]); ffns = dout("ffns", [DEPTH, NS, 2, DFF])

    xres = nc.dram_tensor("xres_scr", [T, D], F32).ap()
    xT_scr = nc.dram_tensor("xT_scr", [128, 8, T], BF16).ap()
    mixT_scr = nc.dram_tensor("mixT_scr", [128, 8, T], BF16).ap()
    xres_s = nc.dram_tensor("xres_s_scr", [NS, D], F32).ap()
    xTs_scr = nc.dram_tensor("xTs_scr", [128, 8, NS], BF16).ap()
    qkvs_scr = nc.dram_tensor("qkvs_scr", [NS, 3, 3, D], F32).ap()
    ab_scr = nc.dram_tensor("ab_scr", [NS, 2, DFF], F32).ap()

    top = ExitStack()
    with top:
        S = Sched(nc, top)
        P = [top.enter_context(nc.psum_tensor(f"P{i}", [128, 512], F32)) for i in range(6)]
        PB = [top.enter_context(nc.psum_tensor(f"PB{i}", [128, 1024], BF16)) for i in range(2)]
        BP = [Buf(f"P{i}", excl=True) for i in range(6)]
        BPB = [Buf(f"PB{i}", excl=True) for i in range(2)]
        uid = [0]
        def sbt(st, name, shape, dt):
            uid[0] += 1
            return st.enter_context(nc.sbuf_tensor(f"sb_{name}_{uid[0]}", list(shape), dt))
        idf = sbt(top, "idf", [128, 128], F32); idb = sbt(top, "idb", [128, 128], BF16)
        bandf = sbt(top, "bandf", [128, 256], F32); band = sbt(top, "band", [128, 256], BF16)
        onesb = sbt(top, "onesb", [128, 64], BF16)
        Bconst = Buf("const")
        S.dma(idf[:], ident_in[:, :], writes=[Bconst])
        S.dma(bandf[:], band_in[:, :], writes=[Bconst])
        S.op("dve", L("tensor_copy", out=idb[:], in_=idf[:]), reads=[Bconst], writes=[Bconst])
        S.op("dve", L("tensor_copy", out=band[:], in_=bandf[:]), reads=[Bconst], writes=[Bconst])
        S.op("dve", L("memset", onesb[:], 1.0), writes=[Bconst])
        Bxres = [Buf(f"xres{t}") for t in range(NT)]
        BxTs = [Buf(f"xTs{t}") for t in range(NT)]
        BmixT = [Buf(f"mixT{h}") for h in range(8)]
        Bxres_s = Buf("xres_s"); BxTs_s = Buf("xTs_s"); Bqkvs = Buf("qkvs"); Babs = Buf("abs")
        sel_t = sbt(top, "sel", [NS, NS, 128], F32); oneh_t = sbt(top, "oneh", [128, NS, NS], F32)
        S.dma(sel_t[:], sel_in[:, :, :], writes=[Bconst])
        S.dma(oneh_t[:], oneh_in[:, :, :], writes=[Bconst])

        def load_xTs(st):
            t_ = sbt(st, "xTs", [128, 8, NS], BF16); B_ = Buf()
            S.dma(t_[:], xTs_scr[:, :, :], reads=[BxTs_s], writes=[B_])
            return t_, B_

        def to_T(st, src, Bsrc, n, name):
            sbb = sbt(st, name + "_b", [NS, n * 128], BF16); Bb = Buf()
            S.op("act", L("copy", out=sbb[:], in_=src), reads=[Bsrc], writes=[Bb])
            dst = sbt(st, name + "_T", [128, n, NS], BF16); Bd = Buf()
            for k in range(n):
                S.op("pe", L("transpose", out=PB[0][:, k * NS:(k + 1) * NS], in_=sbb[:, k * 128:(k + 1) * 128], identity=idb[0:NS, 0:NS]),
                     reads=[Bb, Bconst], writes=[BPB[0]], signal=(k == n - 1))
            S.op("dve", L("tensor_copy", out=dst[:].rearrange("p n s -> p (n s)"), in_=PB[0][:, 0:n * NS]), reads=[BPB[0]], writes=[Bd])
            return dst, Bd

        def ln_sample(ln, li, which, actT_s, Bact_s, nk, w, Bw):
            for c, pi in enumerate((4, 5)):
                for k in range(nk):
                    S.op("pe", L("matmul", P[pi][0:NS, :], lhsT=actT_s[:, k, :], rhs=w[:, k, c * 512:(c + 1) * 512], start=(k == 0), stop=(k == nk - 1)),
                         reads=[Bw, Bact_s], writes=[BP[pi]], signal=(k == nk - 1))
            src = xs_in[:, :] if (li == 0 and which == 0) else xres_s[:, :]
            srcb = [] if (li == 0 and which == 0) else [Bxres_s]
            dst = ys_out[:, :] if (li == DEPTH - 1 and which == 1) else xres_s[:, :]
            ln.tile(P[4][0:NS, :], P[5][0:NS, :], BP[4], BP[5], src, srcb, dst, [Bxres_s], xTs_scr[:, :, :], [BxTs_s], np_=NS)
        rr = {"cast": 0}

        def cast(out_ap, in_ap, reads, writes):
            eng = ("pool", "act")[rr["cast"] % 2]
            rr["cast"] += 1
            if eng == "act":
                return S.op("act", L("copy", out=out_ap, in_=in_ap), reads=reads, writes=writes)
            return S.op("pool", L("tensor_copy", out=out_ap, in_=in_ap), reads=reads, writes=writes)

        def load_xT(st):
            xT = sbt(st, "xT", [128, 8, T], BF16)
            BxT = [Buf(f"xT{t}") for t in range(NT)]
            for q in range(8):
                S.dma(xT[:, :, q * 512:(q + 1) * 512], xT_scr[:, :, q * 512:(q + 1) * 512],
                      reads=BxTs[4 * q:4 * q + 4], writes=BxT[4 * q:4 * q + 4])
            return xT, BxT

        def fm_block_to_rows(src_ap, Bsrc, r, rowbuf, Brow, j):
            S.op("pe", L("matmul", P[5][0:r, 0:128], lhsT=src_ap, rhs=idf[:, :], start=True, stop=True), reads=[Bsrc, Bconst], writes=[BP[5]])
            S.op("dve", L("tensor_copy", out=rowbuf[0:r, j * 128:(j + 1) * 128], in_=P[5][0:r, 0:128]), reads=[BP[5]], writes=[Brow])

        class LN:
            def __init__(self, st, li, which):
                self.xr = [sbt(st, f"ln_xr{i}", [128, D], F32) for i in range(2)]
                self.xb = [sbt(st, f"ln_xb{i}", [128, D], BF16) for i in range(2)]
                self.stats = [sbt(st, f"ln_st{i}", [128, 2, 6], F32) for i in range(2)]
                self.mv = [sbt(st, f"ln_mv{i}", [128, 2], F32) for i in range(2)]
                self.rs = [sbt(st, f"ln_rs{i}", [128, 1], F32) for i in range(2)]
                self.xo = [sbt(st, f"ln_xo{i}", [128, 8, 128], BF16) for i in range(2)]
                self.gb = sbt(st, "ln_gb", [128, 2, D], F32)
                self.B = [[Buf() for _ in range(6)] for _ in range(2)]
                self.Bgb = Buf()
                S.dma(self.gb[:, 0, :], lng[li, which:which + 1, :].partition_broadcast(128), writes=[self.Bgb])
                S.dma(self.gb[:, 1, :], lnb[li, which:which + 1, :].partition_broadcast(128), writes=[self.Bgb])
                self.n = 0

            def tile(self, psA, psB, BpA, BpB, src_ap, src_bufs, dst_ap, dst_bufs, xT_dst_ap, xT_bufs, np_=128):
                i = self.n % 2
                self.n += 1
                xr, xb, stats, mv, rs, xo = self.xr[i], self.xb[i], self.stats[i], self.mv[i], self.rs[i], self.xo[i]
                Bx, Bb, Bs, Bm, Br, Bo = self.B[i]
                S.dma(xr[0:np_, :], src_ap, reads=src_bufs, writes=[Bx])
                for c, (ps, Bp) in enumerate(((psA, BpA), (psB, BpB))):
                    S.op("dve", L("scalar_tensor_tensor", out=xr[0:np_, c * 512:(c + 1) * 512], in0=xr[0:np_, c * 512:(c + 1) * 512], scalar=ALPHA, in1=ps, op0=ALU.mult, op1=ALU.add),
                         reads=[Bx, Bp], writes=[Bx])
                for c in range(2):
                    S.op("dve", L("bn_stats", out=stats[0:np_, c, :], in_=xr[0:np_, c * 512:(c + 1) * 512]), reads=[Bx], writes=[Bs])
                S.op("dve", L("bn_aggr", out=mv[0:np_, :], in_=stats[0:np_]), reads=[Bs], writes=[Bm])
                S.op("act", L("activation", out=rs[0:np_, :], in_=mv[0:np_, 1:2], func=AF.Sqrt, bias=LN_EPS, scale=1.0), reads=[Bm], writes=[Br])
                S.op("dve", L("reciprocal", out=rs[0:np_, :], in_=rs[0:np_, :]), reads=[Br], writes=[Br])
                S.op("dve", L("tensor_scalar", out=xr[0:np_, :], in0=xr[0:np_, :], scalar1=mv[0:np_, 0:1], scalar2=rs[0:np_, 0:1], op0=ALU.subtract, op1=ALU.mult),
                     reads=[Bx, Bm, Br], writes=[Bx])
                S.op("pool", L("tensor_tensor", out=xr[0:np_, :], in0=xr[0:np_, :], in1=self.gb[0:np_, 0, :], op=ALU.mult), reads=[Bx, self.Bgb], writes=[Bx])
                S.op("pool", L("tensor_tensor", out=xr[0:np_, :], in0=xr[0:np_, :], in1=self.gb[0:np_, 1, :], op=ALU.add), reads=[Bx, self.Bgb], writes=[Bx])
                S.dma(dst_ap, xr[0:np_, :], reads=[Bx], writes=dst_bufs)
                S.op("act", L("copy", out=xb[0:np_, :], in_=xr[0:np_, :]), reads=[Bx], writes=[Bb])
                pb = PB[i]
                for k in range(8):
                    S.op("pe", L("transpose", out=pb[:, k * 128:k * 128 + np_], in_=xb[0:np_, k * 128:(k + 1) * 128], identity=idb[0:np_, 0:np_]),
                         reads=[Bb, Bconst], writes=[BPB[i]], signal=(k == 7))
                S.op("dve", L("tensor_copy", out=xo[:, :, 0:np_], in_=pb[:].rearrange("p (k n) -> p k n", k=8)[:, :, 0:np_]), reads=[BPB[i]], writes=[Bo])
                S.dma(xT_dst_ap, xo[:, :, 0:np_], reads=[Bo], writes=xT_bufs)

        def ln_dst(li, which, t):
            if li == DEPTH - 1 and which == 1:
                return y_out[t * 128:(t + 1) * 128, :]
            return xres[t * 128:(t + 1) * 128, :]

        def ln_src(li, which, t):
            if li == 0 and which == 0:
                return x_in[t * 128:(t + 1) * 128, :], []
            return xres[t * 128:(t + 1) * 128, :], [Bxres[t]]

        def load_wfull(st, name, src, nk, stg, Bstg):
            w = sbt(st, name, [128, nk, D], BF16)
            Bw = Buf(name)
            for k in range(nk):
                for c in range(2):
                    S.dma(stg[:, 0:512], src[k * 128:(k + 1) * 128, c * 512:(c + 1) * 512], writes=[Bstg])
                    cast(w[:, k, c * 512:(c + 1) * 512], stg[:, 0:512], [Bstg], [Bw])
            return w, Bw

        def proj_ln_phase(li, which, actT, BactT_fn, nk, w_src, st, sample_fn=None):
            stg = sbt(st, "wf_stg", [128, 512], F32); Bstg = Buf()
            w, Bw = load_wfull(st, "wfull", w_src, nk, stg, Bstg)
            ln = LN(st, li, which)
            for t in range(NT):
                pa, pb_ = (0, 1) if t % 2 == 0 else (2, 3)
                for c, pi in enumerate((pa, pb_)):
                    for k in range(nk):
                        S.op("pe", L("matmul", P[pi][:, :], lhsT=actT[:, k, t * 128:(t + 1) * 128], rhs=w[:, k, c * 512:(c + 1) * 512], start=(k == 0), stop=(k == nk - 1)),
                             reads=[Bw] + BactT_fn(t), writes=[BP[pi]], signal=(k == nk - 1))
                src, sb_ = ln_src(li, which, t)
                ln.tile(P[pa][:, :], P[pb_][:, :], BP[pa], BP[pb_], src, sb_, ln_dst(li, which, t), [Bxres[t]],
                        xT_scr[:, :, t * 128:(t + 1) * 128], [BxTs[t]])
            if sample_fn is not None and with_sample:
                aT, BaT = sample_fn(st)
                ln_sample(ln, li, which, aT, BaT, nk, w, Bw)

        def init_phase():
            with ExitStack() as st:
                xr = [sbt(st, f"i_xr{i}", [128, D], F32) for i in range(2)]
                xb = [sbt(st, f"i_xb{i}", [128, D], BF16) for i in range(2)]
                xo = [sbt(st, f"i_xo{i}", [128, 8, 128], BF16) for i in range(2)]
                Bs = [[Buf() for _ in range(3)] for _ in range(2)]
                for t in range(NT):
                    i = t % 2
                    S.dma(xr[i][:], x_in[t * 128:(t + 1) * 128, :], writes=[Bs[i][0]])
                    S.op("act", L("copy", out=xb[i][:], in_=xr[i][:]), reads=[Bs[i][0]], writes=[Bs[i][1]])
                    for k in range(8):
                        S.op("pe", L("transpose", out=PB[i][:, k * 128:(k + 1) * 128], in_=xb[i][:, k * 128:(k + 1) * 128], identity=idb[:]),
                             reads=[Bs[i][1], Bconst], writes=[BPB[i]], signal=(k == 7))
                    S.op("dve", L("tensor_copy", out=xo[i][:], in_=PB[i][:].rearrange("p (k n) -> p k n", k=8)), reads=[BPB[i]], writes=[Bs[i][2]])
                    S.dma(xT_scr[:, :, t * 128:(t + 1) * 128], xo[i][:], reads=[Bs[i][2]], writes=[BxTs[t]])
                if with_sample:
                    xs0 = sbt(st, "xs0", [NS, D], F32); Bxs0 = Buf()
                    S.dma(xs0[:], xs_in[:, :], writes=[Bxs0])
                    xsT, BxsT0 = to_T(st, xs0[:], Bxs0, 8, "xs0")
                    S.dma(xTs_scr[:, :, :], xsT[:], reads=[BxsT0], writes=[BxTs_s])
                    for gi, (win, d) in enumerate(GROUPS):
                        for ia_ in range(2):
                            for s_ in range(NS):
                                S.dma(kvs[gi][ia_, s_, 0:win - 1, :], c_in[gi][ia_, s_, 1:win, :])
                    for s_ in range(NS):
                        S.dma(pools[s_, 0:14, :], st_pool[s_, 1:15, :])
                    S.dma(sconvs[:, 0, :], st_sconv[:, 1, :])
                    for l_ in range(DEPTH):
                        S.dma(ffns[l_, :, 0, :], st_ffn[l_, :, 1, :])

        def ffn_phase(li):
            with ExitStack() as st:
                stg = sbt(st, "wf_stg", [128, 512], F32); Bstg = Buf()
                wd, Bwd = load_wfull(st, "wdn", wdn[li], NJ, stg, Bstg)
                fk = sbt(st, "fk", [128, NJ, 3], F32); Bfk = Buf()
                S.dma(fk[:], ffk[li], writes=[Bfk])
                ah = sbt(st, "ah", [128, NJ, 2], F32); Bah = Buf()
                S.op("dve", L("memset", ah[:], 0.0), writes=[Bah])
                ln = LN(st, li, 1)
                if with_sample:
                    xs_T, BxsT = load_xTs(st)
                    abt = [sbt(st, f"abt{i}", [NS, 256], F32) for i in range(2)]; Babt = [Buf(), Buf()]
                sti = ExitStack()
                wst = sbt(sti, "wst", [128, 8, 256], F32); Bwst = Buf()
                wbf = [sbt(sti, f"wbf{i}", [128, 8, 256], BF16) for i in range(2)]; Bwbf = [Buf(), Buf()]
                asb = [sbt(sti, f"asb{i}", [128, 514], F32) for i in range(2)]; Basb = [Buf(), Buf()]
                tmp = [sbt(sti, f"ftmp{i}", [128, 512], F32) for i in range(2)]; Btmp = [Buf(), Buf()]
                uu = [sbt(sti, f"fu{i}", [128, 512], F32) for i in range(2)]; Bu = [Buf(), Buf()]
                g = sbt(sti, "g", [128, NJ, 12 * 128], BF16)
                xTc = sbt(sti, "xTc", [128, 8, 12 * 128], BF16)
                t0 = 0
                it = 0
                for ch in FCH:
                    ntok = ch * 128
                    Bg = [Buf() for _ in range(ch)]
                    BxTc = [Buf() for _ in range(ch)]
                    for q in range(ch // 4):
                        S.dma(xTc[:, :, q * 512:(q + 1) * 512], xT_scr[:, :, t0 * 128 + q * 512:t0 * 128 + (q + 1) * 512],
                              reads=BxTs[t0 + 4 * q:t0 + 4 * q + 4], writes=BxTc[4 * q:4 * q + 4])
                    FSTG = DBG.get("ffn_stage", 9)
                    for j in range(NJ if FSTG >= 1 else 0):
                        wi = j % 2
                        S.dma(wst[:], wup[li, j].rearrange("(k p) c -> p k c", p=128), writes=[Bwst])
                        cast(wbf[wi][:], wst[:], [Bwst], [Bwbf[wi]])
                        if with_sample and t0 == 0:
                            for k in range(8):
                                S.op("pe", L("matmul", P[5][0:NS, 0:256], lhsT=xs_T[:, k, :], rhs=wbf[wi][:, k, :], start=(k == 0), stop=(k == 7)),
                                     reads=[Bwbf[wi], BxsT], writes=[BP[5]], signal=(k == 7))
                            S.op("act", L("copy", out=abt[wi][:], in_=P[5][0:NS, 0:256]), reads=[BP[5]], writes=[Babt[wi]])
                            S.dma(ab_scr[:, :, j * 128:(j + 1) * 128], abt[wi][:].rearrange("p (s c) -> p s c", s=2), reads=[Babt[wi]], writes=[Babs])
                        for b in range(ch // 4):
                            i = it % 2
                            it += 1
                            pa, pb_ = (0, 1) if i == 0 else (2, 3)
                            cols = slice(b * 512, (b + 1) * 512)
                            for half, pi in ((0, pa), (1, pb_)):
                                for k in range(8):
                                    S.op("pe", L("matmul", P[pi][:, :], lhsT=wbf[wi][:, k, half * 128:(half + 1) * 128], rhs=xTc[:, k, cols], start=(k == 0), stop=(k == 7)),
                                         reads=[Bwbf[wi]] + BxTc[4 * b:4 * b + 4], writes=[BP[pi]], signal=(k == 7))
                            a = asb[i]
                            S.op("dve", L("tensor_copy", out=a[:, 0:2], in_=ah[:, j, :]), reads=[Bah], writes=[Basb[i]])
                            S.op("act", L("copy", out=a[:, 2:514], in_=P[pa][:, :]), reads=[BP[pa]], writes=[Basb[i]])
                            S.op("dve", L("tensor_copy", out=ah[:, j, :], in_=a[:, 512:514]), reads=[Basb[i]], writes=[Bah])
                            tm = tmp[i]
                            S.op("dve", L("tensor_scalar", out=tm[:], in0=a[:, 0:512], scalar1=fk[:, j, 0:1], scalar2=None, op0=ALU.mult), reads=[Basb[i], Bfk], writes=[Btmp[i]])
                            S.op("dve", L("scalar_tensor_tensor", out=tm[:], in0=a[:, 1:513], scalar=fk[:, j, 1:2], in1=tm[:], op0=ALU.mult, op1=ALU.add), reads=[Basb[i], Bfk, Btmp[i]], writes=[Btmp[i]])
                            S.op("dve", L("scalar_tensor_tensor", out=tm[:], in0=a[:, 2:514], scalar=fk[:, j, 2:3], in1=tm[:], op0=ALU.mult, op1=ALU.add), reads=[Basb[i], Bfk, Btmp[i]], writes=[Btmp[i]])
                            u = uu[i]
                            S.op("act", L("activation", out=u[:], in_=tm[:], func=AF.Gelu), reads=[Btmp[i]], writes=[Bu[i]])
                            S.op("dve", L("tensor_tensor", out=g[:, j, cols], in0=P[pb_][:, :], in1=u[:], op=ALU.mult), reads=[Bu[i], BP[pb_]], writes=Bg[4 * b:4 * b + 4])
                    for tt in range(ch if FSTG >= 2 else 0):
                        t = t0 + tt
                        pa, pb_ = (4, 5)
                        for c, pi in enumerate((pa, pb_)):
                            for j in range(NJ):
                                S.op("pe", L("matmul", P[pi][:, :], lhsT=g[:, j, tt * 128:(tt + 1) * 128], rhs=wd[:, j, c * 512:(c + 1) * 512], start=(j == 0), stop=(j == NJ - 1)),
                                     reads=[Bwd, Bg[tt]], writes=[BP[pi]], signal=(j == NJ - 1))
                        ln.tile(P[pa][:, :], P[pb_][:, :], BP[pa], BP[pb_], xres[t * 128:(t + 1) * 128, :], [Bxres[t]], ln_dst(li, 1, t), [Bxres[t]],
                                xT_scr[:, :, t * 128:(t + 1) * 128], [BxTs[t]])
                    t0 += ch
                sti.close()
                with ExitStack() as st3:
                    rowb = sbt(st3, "rowb", [2, DFF], F32); Browb = Buf()
                    for j in range(NJ if FSTG >= 3 else 0):
                        fm_block_to_rows(ah[:, j, :], Bah, 2, rowb, Browb, j)
                    if FSTG >= 3:
                        S.dma(ffnp[li, :, :], rowb[:], reads=[Browb])
                if with_sample:
                    with ExitStack() as st2:
                        a_b = sbt(st2, "ab_s", [NS, 2, DFF], F32); Bab = Buf()
                        S.dma(a_b[:], ab_scr[:, :, :], reads=[Babs], writes=[Bab])
                        stf = sbt(st2, "stf", [NS, 2, DFF], F32); Bstf = Buf()
                        S.dma(stf[:], st_ffn[li], writes=[Bstf])
                        ktm = sbt(st2, "ktm", [NS, 3, DFF], F32); Bktm = Buf()
                        for tap in range(3):
                            S.dma(ktm[:, tap, :], ffk_tm[li, tap:tap + 1, :].partition_broadcast(NS), writes=[Bktm])
                        t1 = sbt(st2, "t1", [NS, DFF], F32); t2 = sbt(st2, "t2", [NS, DFF], F32); Bt1, Bt2 = Buf(), Buf()
                        tt_ = lambda o, a, b, op, rd, wr: S.op("dve", L("tensor_tensor", out=o, in0=a, in1=b, op=op), reads=rd, writes=wr)
                        tt_(t1[:], stf[:, 0, :], ktm[:, 0, :], ALU.mult, [Bstf, Bktm], [Bt1])
                        tt_(t2[:], stf[:, 1, :], ktm[:, 1, :], ALU.mult, [Bstf, Bktm], [Bt2])
                        tt_(t1[:], t1[:], t2[:], ALU.add, [Bt1, Bt2], [Bt1])
                        tt_(t2[:], a_b[:, 0, :], ktm[:, 2, :], ALU.mult, [Bab, Bktm, Bt1], [Bt2])
                        tt_(t1[:], t1[:], t2[:], ALU.add, [Bt1, Bt2], [Bt1])
                        S.op("act", L("activation", out=t2[:], in_=t1[:], func=AF.Gelu), reads=[Bt1], writes=[Bt2])
                        tt_(t1[:], t2[:], a_b[:, 1, :], ALU.mult, [Bt2, Bab], [Bt1])
                        gsT, BgsT = to_T(st2, t1[:], Bt1, NJ, "gs")
                        ln_sample(ln, li, 1, gsT, BgsT, NJ, wd, Bwd)
                        S.dma(ffns[li, :, 1, :], a_b[:, 0, :], reads=[Bab])

        def attn_phase(li, ia):
            with ExitStack() as st:
                xT, BxT = load_xT(st)
                wst = sbt(st, "wst", [128, 8, 384], F32); Bwst = Buf()
                wbf = [sbt(st, f"wbf{i}", [128, 8, 384], BF16) for i in range(2)]; Bwbf = [Buf(), Buf()]
                rt0 = sbt(st, "rt0", [128, NT, 2, 32], F32); rt = [rt0, rt0]; Brt0 = Buf(); Brt = [Brt0, Brt0]
                QT = sbt(st, "QT", [128, NT, 128], BF16); KTt = sbt(st, "KT", [128, NT, 128], BF16)
                V = sbt(st, "V", [128, NT, 128], BF16)
                BQ = [Buf() for _ in range(NT)]; BK = [Buf() for _ in range(NT)]; BV = [Buf() for _ in range(NT)]
                accN = sbt(st, "accN", [128, T], F32); accD = sbt(st, "accD", [128, T], F32)
                Bacc = [Buf() for _ in range(8)]
                qkr = [sbt(st, f"qkr{i}", [128, 4, 256], F32) for i in range(2)]; Bqkr = [Buf(), Buf()]
                tb = [sbt(st, f"tb{i}", [128, 4, 4, 2, 32], F32) for i in range(2)]; Btb = [Buf(), Buf()]
                qkb = [sbt(st, f"qkb{i}", [128, 4, 256], BF16) for i in range(2)]; Bqkb = [Buf(), Buf()]
                vf = [sbt(st, f"vf{i}", [128, 4, 128], F32) for i in range(2)]; Bvf = [Buf(), Buf()]
                Et = [sbt(st, f"E{i}", [128, 256], BF16) for i in range(4)]; BE = [Buf() for _ in range(4)]
                Em = [sbt(st, f"Em{i}", [128, 256], BF16) for i in range(6)]; BEm = [Buf() for _ in range(6)]
                mixT = sbt(st, "mixThp", [128, T], BF16); Bmix = Buf()
                xg = [sbt(st, f"xg{i}", [128, 8, 128], BF16) for i in range(4)]; Bxg = [Buf() for _ in range(4)]
                gcount = [0]
                if with_sample:
                    xs_T, BxsT = load_xTs(st)
                    rs_t = sbt(st, "rs_t", [NS, 2, 32], F32); Brs = Buf()
                    S.dma(rs_t[:], rope_s[:, :, :], writes=[Brs])
                    qsr = [sbt(st, f"qsr{i}", [NS, 384], F32) for i in range(2)]; Bqsr = [Buf(), Buf()]
                    tbs = [sbt(st, f"tbs{i}", [NS, 4, 2, 32], F32) for i in range(2)]; Btbs = [Buf(), Buf()]
                unit = 0
                for hp in range(DBG.get("hp_n", 8)):
                    for gi, (win, d) in enumerate(GROUPS):
                        if gi not in DBG.get("groups", (0, 1, 2)):
                            continue
                        nblk = NT // d
                        ui = unit % 2
                        unit += 1
                        S.dma(wst[:], wqkv[ia, hp, gi].rearrange("(k p) c -> p k c", p=128), writes=[Bwst])
                        cast(wbf[ui][:], wst[:], [Bwst], [Bwbf[ui]])
                        S.dma(rt[ui][:], rope[gi], writes=[Brt[ui]])
                        w = wbf[ui]
                        r_t = rt[ui]
                        if with_sample:
                            for k in range(8):
                                S.op("pe", L("matmul", P[5][0:NS, 0:384], lhsT=xs_T[:, k, :], rhs=w[:, k, :], start=(k == 0), stop=(k == 7)),
                                     reads=[Bwbf[ui], BxsT], writes=[BP[5]], signal=(k == 7))
                            qs = qsr[ui]; ts_ = tbs[ui]
                            ssrc = P[5][0:NS, 0:256].rearrange("p (j h f) -> p j h f", j=4, h=2)
                            scos = rs_t[:, 0, :].unsqueeze(1).unsqueeze(1).to_broadcast([NS, 4, 2, 32])
                            ssin = rs_t[:, 1, :].unsqueeze(1).to_broadcast([NS, 4, 32])
                            sdst = qs[:, 0:256].rearrange("p (j h f) -> p j h f", j=4, h=2)
                            S.op("dve", L("tensor_tensor", out=sdst, in0=ssrc, in1=scos, op=ALU.mult), reads=[BP[5], Brs], writes=[Bqsr[ui]])
                            S.op("dve", L("tensor_tensor", out=ts_[:, :, 0, :], in0=ssrc[:, :, 1, :], in1=ssin, op=ALU.mult), reads=[BP[5], Brs], writes=[Btbs[ui]])
                            S.op("dve", L("tensor_tensor", out=ts_[:, :, 1, :], in0=ssrc[:, :, 0, :], in1=ssin, op=ALU.mult), reads=[BP[5], Brs], writes=[Btbs[ui]])
                            S.op("act", L("copy", out=qs[:, 256:384], in_=P[5][0:NS, 256:384]), reads=[BP[5]], writes=[Bqsr[ui]])
                            S.op("dve", L("tensor_tensor", out=sdst[:, :, 0, :], in0=sdst[:, :, 0, :], in1=ts_[:, :, 0, :], op=ALU.subtract), reads=[Bqsr[ui], Btbs[ui]], writes=[Bqsr[ui]])
                            S.op("dve", L("tensor_tensor", out=sdst[:, :, 1, :], in0=sdst[:, :, 1, :], in1=ts_[:, :, 1, :], op=ALU.add), reads=[Bqsr[ui], Btbs[ui]], writes=[Bqsr[ui]])
                            S.dma(qkvs_scr[:, gi, :, hp * 128:(hp + 1) * 128], qs[:].rearrange("p (s c) -> p s c", s=3), reads=[Bqsr[ui]], writes=[Bqkvs])
                        PSTG = DBG.get("proj_stage", 9)
                        for bt in range(NT // 4 if PSTG >= 1 else 0):
                            i = bt % 2
                            for bi in range(4):
                                blk = bt * 4 + bi
                                r, nb = blk // nblk, blk % nblk
                                tok0 = r + d * 128 * nb
                                toks = slice(tok0, tok0 + d * 127 + 1, d)
                                tl = sorted(set([tok0 // 128 + x for x in range(0, (d * 127) // 128 + 1)]))
                                if d == 1:
                                    lsrc = lambda k, toks=toks: xT[:, k, toks]
                                    lreads = [BxT[x] for x in tl if x < NT]
                                else:
                                    gsl = gcount[0] % 4
                                    gcount[0] += 1
                                    xg_ = xg[gsl]
                                    S.op(("pool", "act")[gsl % 2], L(("tensor_copy", "copy")[gsl % 2], out=xg_[:], in_=xT[:, :, toks]),
                                         reads=[BxT[x] for x in tl if x < NT], writes=[Bxg[gsl]])
                                    lsrc = lambda k, xg_=xg_: xg_[:, k, :]
                                    lreads = [Bxg[gsl]]
                                for k in range(8):
                                    S.op("pe", L("matmul", P[bi][:, 0:384], lhsT=lsrc(k), rhs=w[:, k, :], start=(k == 0), stop=(k == 7)),
                                         reads=[Bwbf[ui]] + lreads, writes=[BP[bi]], signal=(k == 7))
                            if PSTG < 2:
                                continue
                            q_r = qkr[i]; t_b = tb[i]; q_b = qkb[i]; v_f = vf[i]
                            for bi in range(4):
                                blk = bt * 4 + bi
                                src = P[bi][:, 0:256].rearrange("p (j h f) -> p j h f", j=4, h=2)
                                cosb = r_t[:, blk, 0, :].unsqueeze(1).unsqueeze(1).to_broadcast([128, 4, 2, 32])
                                sinb = r_t[:, blk, 1, :].unsqueeze(1).to_broadcast([128, 4, 32])
                                dst = q_r[:, bi, :].rearrange("p (j h f) -> p j h f", j=4, h=2)
                                S.op("dve", L("tensor_tensor", out=dst, in0=src, in1=cosb, op=ALU.mult), reads=[BP[bi], Brt[ui]], writes=[Bqkr[i]])
                                S.op("dve", L("tensor_tensor", out=t_b[:, bi, :, 0, :], in0=src[:, :, 1, :], in1=sinb, op=ALU.mult), reads=[BP[bi], Brt[ui]], writes=[Btb[i]])
                                S.op("dve", L("tensor_tensor", out=t_b[:, bi, :, 1, :], in0=src[:, :, 0, :], in1=sinb, op=ALU.mult), reads=[BP[bi], Brt[ui]], writes=[Btb[i]])
                                S.op("act", L("copy", out=v_f[:, bi, :], in_=P[bi][:, 256:384]), reads=[BP[bi]], writes=[Bvf[i]])
                            qv = q_r[:].rearrange("p b (j h f) -> p (b j) h f", j=4, h=2)
                            tv = t_b[:].rearrange("p b j h f -> p (b j) h f")
                            S.op("pool", L("tensor_tensor", out=qv[:, :, 0, :], in0=qv[:, :, 0, :], in1=tv[:, :, 0, :], op=ALU.subtract), reads=[Bqkr[i], Btb[i]], writes=[Bqkr[i]])
                            S.op("pool", L("tensor_tensor", out=qv[:, :, 1, :], in0=qv[:, :, 1, :], in1=tv[:, :, 1, :], op=ALU.add), reads=[Bqkr[i], Btb[i]], writes=[Bqkr[i]])
                            S.op("act", L("copy", out=q_b[:], in_=q_r[:]), reads=[Bqkr[i]], writes=[Bqkb[i]])
                            S.op("pool", L("tensor_copy", out=V[:, bt * 4:bt * 4 + 4, :], in_=v_f[:]), reads=[Bvf[i]], writes=BV[bt * 4:bt * 4 + 4])
                            for bi in range(4):
                                blk = bt * 4 + bi
                                r, nb = blk // nblk, blk % nblk
                                tok0 = r + d * 128 * nb
                                if tok0 >= T - win and DBG.get("kvout", True):
                                    row0 = tok0 - (T - win)
                                    rows = slice(row0, row0 + d * 127 + 1, d)
                                    S.dma(kvp[gi][ia, rows, hp * 128:(hp + 1) * 128], q_r[:, bi, 128:256], reads=[Bqkr[i]])
                                    S.dma(kvp[gi][ia, rows, 1024 + hp * 128:1024 + (hp + 1) * 128], v_f[:, bi, :], reads=[Bvf[i]])
                            if PSTG < 3:
                                continue
                            pbk = PB[i]
                            for bi in range(4):
                                for s_ in range(2):
                                    S.op("pe", L("transpose", out=pbk[:, (bi * 2 + s_) * 128:(bi * 2 + s_ + 1) * 128], in_=q_b[:, bi, s_ * 128:(s_ + 1) * 128], identity=idb[:]),
                                         reads=[Bqkb[i], Bconst], writes=[BPB[i]], signal=(bi == 3 and s_ == 1))
                            pv = pbk[:].rearrange("p (b s n) -> p b s n", b=4, s=2)
                            S.op("dve", L("tensor_copy", out=QT[:, bt * 4:bt * 4 + 4, :], in_=pv[:, :, 0, :]), reads=[BPB[i]], writes=BQ[bt * 4:bt * 4 + 4])
                            S.op("act", L("copy", out=KTt[:, bt * 4:bt * 4 + 4, :], in_=pv[:, :, 1, :]), reads=[BPB[i]], writes=BK[bt * 4:bt * 4 + 4])
                        ecount = 0
                        emidx = {}
                        for blk in range(NT if DBG.get("core", True) else 0):
                            r, nb = blk // nblk, blk % nblk
                            ncol = 256 if nb + 1 < nblk else 128
                            nqb = ncol // 128
                            for h in range(2):
                                pi = 4 + (ecount % 2)
                                ei = ecount % 4
                                mi = ecount % 6
                                ecount += 1
                                hs = slice(64 * h, 64 * h + 64)
                                S.op("pe", L("matmul", P[pi][:, 0:ncol], lhsT=KTt[hs, blk, :], rhs=QT[hs, blk:blk + nqb, :], start=True, stop=True),
                                     reads=[BK[blk]] + BQ[blk:blk + nqb], writes=[BP[pi]])
                                S.op("act", L("activation", out=Et[ei][:, 0:ncol], in_=P[pi][:, 0:ncol], func=AF.Exp, scale=0.125), reads=[BP[pi]], writes=[BE[ei]])
                                S.op("pool", L("tensor_tensor", out=Em[mi][:, 0:ncol], in0=Et[ei][:, 0:ncol], in1=band[:, 0:ncol], op=ALU.mult), reads=[BE[ei], Bconst], writes=[BEm[mi]])
                                emidx[(blk, h)] = mi
                            qi = blk % 4
                            bnk = (blk // 4) % 2
                            pn, pd = (0, 1) if bnk == 0 else (2, 3)
                            for h in range(2):
                                hs = slice(64 * h, 64 * h + 64)
                                srcs = []
                                if nb >= 1:
                                    srcs.append((blk - 1, emidx[(blk - 1, h)], slice(128, 256)))
                                srcs.append((blk, emidx[(blk, h)], slice(0, 128)))
                                for (pp, lhs_fn) in ((pn, lambda kb, hs=hs: V[:, kb, hs]), (pd, lambda kb: onesb[:, :])):
                                    for si, (kb, mi, cs) in enumerate(srcs):
                                        S.op("pe", L("matmul", P[pp][hs, qi * 128:(qi + 1) * 128], lhsT=lhs_fn(kb), rhs=Em[mi][:, cs], start=(si == 0), stop=(si == len(srcs) - 1)),
                                             reads=[BV[kb], BEm[mi], Bconst], writes=[BP[pp]], signal=(si == len(srcs) - 1))
                            if qi == 3:
                                b0 = blk - 3
                                r0, nb0 = b0 // nblk, b0 % nblk
                                for (pp, acc) in ((pn, accN), (pd, accD)):
                                    accv = acc[:].rearrange("p (n r) -> p r n", r=d)
                                    if nblk >= 4:
                                        dst = accv[:, r0, 128 * nb0:128 * nb0 + 512]
                                        src = P[pp][:, :]
                                    else:
                                        dst = accv[:, r0:r0 + 2, 0:256]
                                        src = P[pp][:, :].rearrange("p (a n) -> p a n", a=2)
                                    if gi == 0:
                                        S.op("dve", L("tensor_copy", out=dst, in_=src), reads=[BP[pp]], writes=Bacc)
                                    else:
                                        S.op("dve", L("tensor_tensor", out=dst, in0=src, in1=dst, op=ALU.add), reads=[BP[pp]] + Bacc, writes=Bacc)
                    if not DBG.get("norm", True):
                        continue
                    for q in range(8):
                        cs = slice(q * 512, (q + 1) * 512)
                        S.op("dve", L("reciprocal", out=accD[:, cs], in_=accD[:, cs]), reads=Bacc, writes=Bacc)
                        S.op("dve", L("tensor_tensor", out=mixT[:, cs], in0=accN[:, cs], in1=accD[:, cs], op=ALU.mult), reads=Bacc, writes=[Bmix])
                    S.dma(mixT_scr[:, hp, :], mixT[:], reads=[Bmix], writes=[BmixT[hp]])
            if not DBG.get("wo", True):
                return
            with ExitStack() as st:
                mT = sbt(st, "mT", [128, 8, T], BF16); BmT = Buf()
                for hp in range(8):
                    S.dma(mT[:, hp, :], mixT_scr[:, hp, :], reads=[BmixT[hp]], writes=[BmT])
                def attn_sample(st2):
                    qk = sbt(st2, "qkvs", [NS, 3, 3, D], F32); Bqk = Buf()
                    S.dma(qk[:], qkvs_scr[:, :, :, :], reads=[Bqkvs], writes=[Bqk])
                    Kc = sbt(st2, "Kc", [128, D], F32); Vc = sbt(st2, "Vc", [128, D], F32); BKc, BVc = Buf(), Buf()
                    prod = sbt(st2, "prod", [128, D], F32); Bprod = Buf()
                    tmpv = sbt(st2, "tmpv", [128, D], F32); Btmpv = Buf()
                    Ssc = sbt(st2, "Ssc", [128, 16], F32); Es = sbt(st2, "Es", [128, 16], F32); BSs, BEs = Buf(), Buf()
                    pself = sbt(st2, "pself", [NS, D], F32); Bps = Buf()
                    sself = sbt(st2, "sself", [NS, 16], F32); eself = sbt(st2, "eself", [NS, 16], F32); Bss, Bes = Buf(), Buf()
                    first = True
                    for gi, (win, d) in enumerate(GROUPS):
                        for s_ in range(NS):
                            S.dma(Kc[:], c_in[gi][ia, s_, 0:win:d, 0:1024], writes=[BKc])
                            S.dma(Vc[:], c_in[gi][ia, s_, 0:win:d, 1024:2048], writes=[BVc])
                            for c in range(2):
                                S.op("pe", L("matmul", P[c][:, :], lhsT=sel_t[0:NS, s_, :], rhs=qk[:, gi, 0, c * 512:(c + 1) * 512], start=True, stop=True),
                                     reads=[Bqk, Bconst], writes=[BP[c]])
                                S.op("dve", L("tensor_tensor", out=prod[:, c * 512:(c + 1) * 512], in0=Kc[:, c * 512:(c + 1) * 512], in1=P[c][:, :], op=ALU.mult), reads=[BKc, BP[c]], writes=[Bprod])
                            S.op("dve", L("tensor_reduce", out=Ssc[:], in_=prod[:].rearrange("p (h f) -> p h f", h=16), axis=AX.X, op=ALU.add), reads=[Bprod], writes=[BSs])
                            S.op("act", L("activation", out=Es[:], in_=Ssc[:], func=AF.Exp, scale=0.125), reads=[BSs], writes=[BEs])
                            S.op("dve", L("tensor_tensor", out=tmpv[:].rearrange("p (h f) -> p h f", h=16), in0=Vc[:].rearrange("p (h f) -> p h f", h=16), in1=Es[:].unsqueeze(2).to_broadcast([128, 16, 64]), op=ALU.mult), reads=[BVc, BEs], writes=[Btmpv])
                            for c in range(2):
                                S.op("pe", L("matmul", P[2 + c][0:NS, :], lhsT=oneh_t[:, s_, :], rhs=tmpv[:, c * 512:(c + 1) * 512], start=first, stop=False),
                                     reads=[Btmpv, Bconst], writes=[BP[2 + c]])
                            S.op("pe", L("matmul", P[4][0:NS, 0:16], lhsT=oneh_t[:, s_, :], rhs=Es[:], start=first, stop=False), reads=[BEs, Bconst], writes=[BP[4]])
                            first = False
                        S.op("dve", L("tensor_tensor", out=pself[:], in0=qk[:, gi, 0, :], in1=qk[:, gi, 1, :], op=ALU.mult), reads=[Bqk], writes=[Bps])
                        S.op("dve", L("tensor_reduce", out=sself[:], in_=pself[:].rearrange("p (h f) -> p h f", h=16), axis=AX.X, op=ALU.add), reads=[Bps], writes=[Bss])
                        S.op("act", L("activation", out=eself[:], in_=sself[:], func=AF.Exp, scale=0.125), reads=[Bss], writes=[Bes])
                        S.op("dve", L("tensor_tensor", out=pself[:].rearrange("p (h f) -> p h f", h=16), in0=qk[:, gi, 2, :].rearrange("p (h f) -> p h f", h=16), in1=eself[:].unsqueeze(2).to_broadcast([NS, 16, 64]), op=ALU.mult), reads=[Bqk, Bes, Bss], writes=[Bps])
                        last = (gi == len(GROUPS) - 1)
                        for c in range(2):
                            S.op("pe", L("matmul", P[2 + c][0:NS, :], lhsT=idf[0:NS, 0:NS], rhs=pself[:, c * 512:(c + 1) * 512], start=False, stop=last), reads=[Bps, Bconst], writes=[BP[2 + c]])
                        S.op("pe", L("matmul", P[4][0:NS, 0:16], lhsT=idf[0:NS, 0:NS], rhs=eself[:], start=False, stop=last), reads=[Bes, Bconst], writes=[BP[4]])
                        S.dma(kvs[gi][ia, :, win - 1, 0:1024], qk[:, gi, 1, :], reads=[Bqk])
                        S.dma(kvs[gi][ia, :, win - 1, 1024:2048], qk[:, gi, 2, :], reads=[Bqk])
                    rden = sbt(st2, "rden", [NS, 16], F32); Brd = Buf()
                    mixs = sbt(st2, "mixs", [NS, D], F32); Bmx = Buf()
                    S.op("dve", L("reciprocal", out=rden[:], in_=P[4][0:NS, 0:16]), reads=[BP[4]], writes=[Brd])
                    for c in range(2):
                        S.op("dve", L("tensor_tensor", out=mixs[:, c * 512:(c + 1) * 512].rearrange("p (h f) -> p h f", h=8), in0=P[2 + c][0:NS, :].rearrange("p (h f) -> p h f", h=8), in1=rden[:, c * 8:(c + 1) * 8].unsqueeze(2).to_broadcast([NS, 8, 64]), op=ALU.mult), reads=[BP[2 + c], Brd], writes=[Bmx])
                    return to_T(st2, mixs[:], Bmx, 8, "mixs")
                proj_ln_phase(li, 0, mT, lambda t: [BmT], 8, wo[ia], st, sample_fn=attn_sample)

        def pool_phase(li):
            with ExitStack() as st0:
                pT = sbt(st0, "pooledT", [128, 8, T], BF16); BpT = [Buf() for _ in range(8)]
                us = sbt(st0, "us", [NS, D], F32); Bus = Buf()
                with ExitStack() as st:
                    xT, BxT = load_xT(st)
                    stg = sbt(st, "wf_stg", [128, 512], F32); Bstg = Buf()
                    w, Bw = load_wfull(st, "pwin", pwin, 8, stg, Bstg)
                    rc = sbt(st, "rc", [128, 8, 16], F32); Brc = Buf()
                    S.dma(rc[:], prcnt[:, :, :], writes=[Brc])
                    prow = sbt(st, "prow", [15, D], F32); Bprow = Buf()
                    ub = sbt(st, "ub", [128, 16 + T], F32); sA = sbt(st, "sA", [128, 16 + T], F32); sB = sbt(st, "sB", [128, 16 + T], F32)
                    Bub, BsA, BsB = Buf(), Buf(), Buf()
                    for (bu, Bb) in ((ub, Bub), (sA, BsA), (sB, BsB)):
                        S.op("pool", L("memset", bu[:, 0:16], 0.0), writes=[Bb])
                    if with_sample:
                        xs_T, BxsT = load_xTs(st)
                        for c in range(2):
                            for k in range(8):
                                S.op("pe", L("matmul", P[4 + c][0:NS, :], lhsT=xs_T[:, k, :], rhs=w[:, k, c * 512:(c + 1) * 512], start=(k == 0), stop=(k == 7)),
                                     reads=[Bw, BxsT], writes=[BP[4 + c]], signal=(k == 7))
                            S.op("act", L("copy", out=us[:, c * 512:(c + 1) * 512], in_=P[4 + c][0:NS, :]), reads=[BP[4 + c]], writes=[Bus])
                    for j in range(8):
                        for blk in range(8):
                            pi = blk % 4
                            for k in range(8):
                                S.op("pe", L("matmul", P[pi][:, :], lhsT=w[:, k, j * 128:(j + 1) * 128], rhs=xT[:, k, blk * 512:(blk + 1) * 512], start=(k == 0), stop=(k == 7)),
                                     reads=[Bw] + BxT[4 * blk:4 * blk + 4], writes=[BP[pi]], signal=(k == 7))
                            S.op("act", L("copy", out=ub[:, 16 + blk * 512:16 + (blk + 1) * 512], in_=P[pi][:, :]), reads=[BP[pi]], writes=[Bub])
                        wj = POOL_W[j // 2]
                        cur, Bcur = ub, Bub
                        step = 1
                        bufs = [(sA, BsA), (sB, BsB)]
                        bi = 0
                        while step < wj:
                            nxt, Bn = bufs[bi % 2]
                            bi += 1
                            for hh in range(2):
                                cs = slice(16 + hh * 2048, 16 + (hh + 1) * 2048)
                                cs2 = slice(16 - step + hh * 2048, 16 - step + (hh + 1) * 2048)
                                S.op(("dve", "pool")[hh], L("tensor_tensor", out=nxt[:, cs], in0=cur[:, cs], in1=cur[:, cs2], op=ALU.add), reads=[Bcur], writes=[Bn])
                            cur, Bcur = nxt, Bn
                            step *= 2
                        S.op("dve", L("tensor_tensor", out=cur[:, 16:32], in0=cur[:, 16:32], in1=rc[:, j, :], op=ALU.mult), reads=[Bcur, Brc], writes=[Bcur])
                        S.op("dve", L("tensor_scalar", out=cur[:, 16:32], in0=cur[:, 16:32], scalar1=float(wj), scalar2=None, op0=ALU.mult), reads=[Bcur], writes=[Bcur])
                        for hh in range(2):
                            cs = slice(16 + hh * 2048, 16 + (hh + 1) * 2048)
                            co = slice(hh * 2048, (hh + 1) * 2048)
                            S.op("dve", L("scalar_tensor_tensor", out=pT[:, j, co], in0=cur[:, cs], scalar=1.0 / wj, in1=ub[:, cs], op0=ALU.mult, op1=ALU.subtract), reads=[Bcur, Bub], writes=[BpT[j]])
                        fm_block_to_rows(ub[:, 16 + T - 15:16 + T], Bub, 15, prow, Bprow, j)
                    S.dma(poolp[:, :], prow[:], reads=[Bprow])
                with ExitStack() as st:
                    zT = pT
                    wgs = sbt(st, "wgs", [128, 4, 2, 256], F32); wgb = sbt(st, "wgb", [128, 4, 2, 256], BF16); Bwg = Buf()
                    psc = sbt(st, "psc", [128, 8], F32)
                    S.dma(wgs[:], pwgrp.rearrange("g (k p) c -> p g k c", p=128), writes=[Bwg])
                    S.dma(psc[:], pscale[:, :], writes=[Bwg])
                    S.op("dve", L("tensor_copy", out=wgb[:], in_=wgs[:]), reads=[Bwg], writes=[Bwg])
                    n = 0
                    for blk in range(8):
                        for gi in range(4):
                            pis = (0, 1) if n % 2 == 0 else (2, 3)
                            n += 1
                            for mt in range(2):
                                pi = pis[mt]
                                for kt in range(2):
                                    S.op("pe", L("matmul", P[pi][:, :], lhsT=wgb[:, gi, kt, mt * 128:(mt + 1) * 128], rhs=pT[:, 2 * gi + kt, blk * 512:(blk + 1) * 512], start=(kt == 0), stop=(kt == 1)),
                                         reads=[Bwg, BpT[2 * gi], BpT[2 * gi + 1]], writes=[BP[pi]], signal=(kt == 1))
                            for mt in range(2):
                                pi = pis[mt]
                                jt = 2 * gi + mt
                                S.op("act", L("activation", out=zT[:, jt, blk * 512:(blk + 1) * 512], in_=P[pi][:, :], func=AF.Copy, scale=psc[:, jt:jt + 1]), reads=[BP[pi], Bwg], writes=[BpT[jt]])
                    def pool_sample(st2):
                        sp = sbt(st2, "sp", [NS, 15, 256], F32); Bsp = Buf()
                        ws = sbt(st2, "ws", [NS, D], F32); Bws = Buf()
                        for gi, w_ in enumerate(POOL_W):
                            cs = slice(gi * 256, (gi + 1) * 256)
                            S.dma(sp[:, 0:w_ - 1, :], st_pool[:, 15 - (w_ - 1):15, cs], writes=[Bsp])
                            S.op("dve", L("tensor_reduce", out=ws[:, cs], in_=sp[:, 0:w_ - 1, :].rearrange("p r c -> p c r"), axis=AX.X, op=ALU.add), reads=[Bsp], writes=[Bws])
                            S.op("dve", L("tensor_tensor", out=ws[:, cs], in0=ws[:, cs], in1=us[:, cs], op=ALU.add), reads=[Bws, Bus], writes=[Bws])
                            S.op("dve", L("scalar_tensor_tensor", out=ws[:, cs], in0=ws[:, cs], scalar=1.0 / w_, in1=us[:, cs], op0=ALU.mult, op1=ALU.subtract), reads=[Bws, Bus], writes=[Bws])
                        pTs, BpTs = to_T(st2, ws[:], Bws, 8, "pls")
                        zTs = sbt(st2, "zTs", [128, 8, NS], BF16); BzTs = Buf()
                        for gi in range(4):
                            for mt in range(2):
                                for kt in range(2):
                                    S.op("pe", L("matmul", P[4][:, 0:NS], lhsT=wgb[:, gi, kt, mt * 128:(mt + 1) * 128], rhs=pTs[:, 2 * gi + kt, :], start=(kt == 0), stop=(kt == 1)),
                                         reads=[Bwg, BpTs], writes=[BP[4]], signal=(kt == 1))
                                jt = 2 * gi + mt
                                S.op("act", L("activation", out=zTs[:, jt, :], in_=P[4][:, 0:NS], func=AF.Copy, scale=psc[:, jt:jt + 1]), reads=[BP[4], Bwg], writes=[BzTs])
                        S.dma(pools[:, 14, :], us[:], reads=[Bus])
                        return zTs, BzTs
                    proj_ln_phase(li, 0, zT, lambda t: BpT, 8, pwout, st, sample_fn=pool_sample)

        def sconv_phase(li):
            with ExitStack() as st0:
                yT = sbt(st0, "yT", [128, 8, T], BF16); ByT = Buf()
                gs3 = sbt(st0, "gs3", [NS, 8, 384], F32); Bgs3 = Buf()
                with ExitStack() as st:
                    xT, BxT = load_xT(st)
                    wst = sbt(st, "wst", [128, 8, 384], F32); Bwst = Buf()
                    wbf = [sbt(st, f"wbf{i}", [128, 8, 384], BF16) for i in range(2)]; Bwbf = [Buf(), Buf()]
                    skt = sbt(st, "skt", [128, 8, 3], F32); Bsk = Buf()
                    S.dma(skt[:], sk[:, :, :], writes=[Bsk])
                    pb2 = [sbt(st, f"pb2_{i}", [128, 514], F32) for i in range(2)]; Bpb2 = [Buf(), Buf()]
                    srow = sbt(st, "srow", [2, D], F32); Bsrow = Buf()
                    hb = [sbt(st, f"hb{i}", [128, 512], F32) for i in range(2)]; Bhb = [Buf(), Buf()]
                    tmp = [sbt(st, f"stmp{i}", [128, 512], F32) for i in range(2)]; Btmp = [Buf(), Buf()]
                    if with_sample:
                        xs_T, BxsT = load_xTs(st)
                    for j in range(8):
                        wi = j % 2
                        S.dma(wst[:], swin[j].rearrange("(k p) c -> p k c", p=128), writes=[Bwst])
                        cast(wbf[wi][:], wst[:], [Bwst], [Bwbf[wi]])
                        if with_sample:
                            for k in range(8):
                                S.op("pe", L("matmul", P[5][0:NS, 0:384], lhsT=xs_T[:, k, :], rhs=wbf[wi][:, k, :], start=(k == 0), stop=(k == 7)),
                                     reads=[Bwbf[wi], BxsT], writes=[BP[5]], signal=(k == 7))
                            S.op("act", L("copy", out=gs3[:, j, :], in_=P[5][0:NS, 0:384]), reads=[BP[5]], writes=[Bgs3])
                        for blk in range(8):
                            i = blk % 2
                            pis = (0, 1, 2) if i == 0 else (3, 4, 5)
                            for s_, pi in enumerate(pis):
                                for k in range(8):
                                    S.op("pe", L("matmul", P[pi][:, :], lhsT=wbf[wi][:, k, s_ * 128:(s_ + 1) * 128], rhs=xT[:, k, blk * 512:(blk + 1) * 512], start=(k == 0), stop=(k == 7)),
                                         reads=[Bwbf[wi]] + BxT[4 * blk:4 * blk + 4], writes=[BP[pi]], signal=(k == 7))
                            pb_ = pb2[i]
                            if blk == 0:
                                S.op("pool", L("memset", pb_[:, 0:2], 0.0), writes=[Bpb2[i]])
                            else:
                                S.op("pool", L("tensor_copy", out=pb_[:, 0:2], in_=pb2[1 - i][:, 512:514]), reads=[Bpb2[1 - i]], writes=[Bpb2[i]])
                            S.op("act", L("copy", out=hb[i][:], in_=P[pis[2]][:, :]), reads=[BP[pis[2]]], writes=[Bhb[i]])
                            S.op("dve", L("tensor_tensor", out=pb_[:, 2:514], in0=P[pis[1]][:, :], in1=hb[i][:], op=ALU.mult), reads=[BP[pis[1]], Bhb[i]], writes=[Bpb2[i]])
                            tm = tmp[i]
                            S.op("dve", L("tensor_scalar", out=tm[:], in0=pb_[:, 0:512], scalar1=skt[:, j, 0:1], scalar2=None, op0=ALU.mult), reads=[Bpb2[i], Bsk], writes=[Btmp[i]])
                            S.op("dve", L("scalar_tensor_tensor", out=tm[:], in0=pb_[:, 1:513], scalar=skt[:, j, 1:2], in1=tm[:], op0=ALU.mult, op1=ALU.add), reads=[Bpb2[i], Bsk, Btmp[i]], writes=[Btmp[i]])
                            S.op("dve", L("scalar_tensor_tensor", out=tm[:], in0=pb_[:, 2:514], scalar=skt[:, j, 2:3], in1=tm[:], op0=ALU.mult, op1=ALU.add), reads=[Bpb2[i], Bsk, Btmp[i]], writes=[Btmp[i]])
                            S.op("dve", L("tensor_tensor", out=yT[:, j, blk * 512:(blk + 1) * 512], in0=P[pis[0]][:, :], in1=tm[:], op=ALU.mult), reads=[BP[pis[0]], Btmp[i]], writes=[ByT])
                            if blk == 7:
                                fm_block_to_rows(pb_[:, 512:514], Bpb2[i], 2, srow, Bsrow, j)
                    S.dma(sconvp[:, :], srow[:], reads=[Bsrow])
                with ExitStack() as st:
                    def sconv_sample(st2):
                        sst = sbt(st2, "sst", [NS, 2, D], F32); Bsst = Buf()
                        S.dma(sst[:], st_sconv[:, :, :], writes=[Bsst])
                        ktm = sbt(st2, "sktm", [NS, 3, D], F32); Bktm = Buf()
                        for tap in range(3):
                            S.dma(ktm[:, tap, :], sk_tm[tap:tap + 1, :].partition_broadcast(NS), writes=[Bktm])
                        p_s = sbt(st2, "p_s", [NS, D], F32); t1 = sbt(st2, "t1", [NS, D], F32); t2 = sbt(st2, "t2", [NS, D], F32)
                        Bp_s, Bt1, Bt2 = Buf(), Buf(), Buf()
                        tt_ = lambda o, a, b, op, rd, wr: S.op("dve", L("tensor_tensor", out=o, in0=a, in1=b, op=op), reads=rd, writes=wr)
                        v3 = lambda t_: t_[:].rearrange("p (j c) -> p j c", j=8)
                        tt_(v3(p_s), gs3[:, :, 128:256], gs3[:, :, 256:384], ALU.mult, [Bgs3], [Bp_s])
                        tt_(t1[:], sst[:, 0, :], ktm[:, 0, :], ALU.mult, [Bsst, Bktm], [Bt1])
                        tt_(t2[:], sst[:, 1, :], ktm[:, 1, :], ALU.mult, [Bsst, Bktm], [Bt2])
                        tt_(t1[:], t1[:], t2[:], ALU.add, [Bt1, Bt2], [Bt1])
                        tt_(t2[:], p_s[:], ktm[:, 2, :], ALU.mult, [Bp_s, Bktm, Bt1], [Bt2])
                        tt_(t1[:], t1[:], t2[:], ALU.add, [Bt1, Bt2], [Bt1])
                        tt_(v3(t1), v3(t1), gs3[:, :, 0:128], ALU.mult, [Bt1, Bgs3], [Bt1])
                        S.dma(sconvs[:, 1, :], p_s[:], reads=[Bp_s])
                        return to_T(st2, t1[:], Bt1, 8, "scs")
                    proj_ln_phase(li, 0, yT, lambda t: [ByT], 8, swout, st, sample_fn=sconv_sample)

        on = lambda name: phases is None or name in phases
        if on("init"):
            init_phase()
        ia = 0
        for li in range(DEPTH):
            kind = li % 3
            if kind == 0:
                if on(f"mix{li}"):
                    attn_phase(li, ia)
                ia += 1
            elif kind == 1:
                if on(f"mix{li}"):
                    pool_phase(li)
            else:
                if on(f"mix{li}"):
                    sconv_phase(li)
            if on(f"ffn{li}"):
                ffn_phase(li)
        S.final_wait_dmas()
        print("ops recorded:", S.nops, {e: len(v) for e, v in S.ops.items()})
        S.replay()
    return nc


def _rope_tables():
    half = 32
    inv = (np.float32(10000.0) ** (-np.arange(half, dtype=np.float32) / np.float32(half))).astype(np.float32)
    out = {}
    for (_, d) in GROUPS:
        nblk = NT // d
        tab = np.zeros((128, NT, 2, 32), np.float32)
        for blk in range(NT):
            r, nb = blk // nblk, blk % nblk
            pos = (r + d * (128 * nb + np.arange(128))).astype(np.float32)
            ang = pos[:, None] * inv[None, :]
            tab[:, blk, 0, :] = np.cos(ang)
            tab[:, blk, 1, :] = np.sin(ang)
        out[d] = tab
    angs = (np.float32(PAST) * inv).astype(np.float32)
    rs = np.zeros((NS, 2, 32), np.float32)
    rs[:, 0, :] = np.cos(angs)[None]
    rs[:, 1, :] = np.sin(angs)[None]
    return out, rs


def _consts():
    j = np.arange(128)[:, None]
    c = np.arange(256)[None, :]
    band = np.where(c < 128, (c >= j), ((c - 128) <= j)).astype(np.float32)
    ident = np.eye(128, dtype=np.float32)
    prcnt = np.zeros((128, 8, 16), np.float32)
    for jt in range(8):
        w = POOL_W[jt // 2]
        cnt = np.minimum(w, np.arange(16) + 1).astype(np.float32)
        prcnt[:, jt, :] = (1.0 / cnt)[None, :] / np.float32(w) * np.float32(w) / 1.0
    return band, ident, prcnt


_NC_CACHE = {}


def kernel(x_prompt, x_sample, cache_kv_w128, cache_kv_w512, cache_kv_w2048,
           state_pool, state_sconv, state_ffn_conv,
           attn_w_qkv, attn_w_o, pool_w_in, pool_w_grp, pool_scale, pool_w_out,
           sconv_w_in, sconv_k, sconv_w_out, ffn_w_up, ffn_k, ffn_w_down, ln_g, ln_b):
    f = lambda a: np.ascontiguousarray(np.asarray(a, dtype=np.float32))
    if "nc" not in _NC_CACHE:
        _NC_CACHE["nc"] = build()
    nc = _NC_CACHE["nc"]
    ropes, rope_s = _rope_tables()
    band, ident, prcnt = _consts()
    wq = f(attn_w_qkv).reshape(2, D, 3, 3, 8, 2, 64)
    wqkv = np.ascontiguousarray(wq.transpose(0, 4, 2, 1, 3, 5, 6)).reshape(2, 8, 3, D, 384)
    wu = f(ffn_w_up).reshape(DEPTH, D, 2, NJ, 128)
    wup = np.ascontiguousarray(wu.transpose(0, 3, 1, 2, 4)).reshape(DEPTH, NJ, D, 256)
    ffk = np.ascontiguousarray(f(ffn_k).reshape(DEPTH, 3, NJ, 128).transpose(0, 3, 2, 1))
    sw = f(sconv_w_in)[0].reshape(D, 3, 8, 128)
    swin = np.ascontiguousarray(sw.transpose(2, 0, 1, 3)).reshape(8, D, 384)
    skf = np.ascontiguousarray(f(sconv_k)[0].reshape(3, 8, 128).transpose(2, 1, 0))
    psc = np.ascontiguousarray(f(pool_scale)[0].reshape(8, 128).T)
    common = {
        "wqkv": wqkv, "wo": f(attn_w_o), "wup": wup, "wdn": f(ffn_w_down), "ffk": ffk, "ffk_tm": f(ffn_k),
        "pwin": f(pool_w_in)[0], "pwgrp": f(pool_w_grp)[0], "pscale": psc, "pwout": f(pool_w_out)[0], "prcnt": prcnt,
        "swin": swin, "sk": skf, "sk_tm": f(sconv_k)[0], "swout": f(sconv_w_out)[0],
        "lng": f(ln_g), "lnb": f(ln_b), "rope_s": rope_s, "band": band, "ident": ident,
        "sel": np.ascontiguousarray(np.broadcast_to(np.eye(NS, dtype=np.float32)[:, :, None], (NS, NS, 128))),
        "oneh": np.ascontiguousarray(np.broadcast_to(np.eye(NS, dtype=np.float32)[None, :, :], (128, NS, NS))),
    }
    for (_, d) in GROUPS:
        common[f"rope{d}"] = ropes[d]
    caches = {128: f(cache_kv_w128), 512: f(cache_kv_w512), 2048: f(cache_kv_w2048)}
    in_maps = []
    for c in range(8):
        seq = c // 2
        ss = slice(NS * c, NS * (c + 1))
        m = dict(common)
        m["x"] = f(x_prompt[seq])
        m["xs"] = f(x_sample[ss, 0, :])
        for w in (128, 512, 2048):
            m[f"c{w}"] = np.ascontiguousarray(caches[w][:, ss].reshape(2, NS, w, 2048))
        m["st_pool"] = f(state_pool[0, ss])
        m["st_sconv"] = f(state_sconv[0, ss])
        m["st_ffn"] = f(state_ffn_conv[:, ss])
        in_maps.append(m)
    res = run_bass_kernel_spmd(nc, in_maps, core_ids=list(range(8)))
    R = res.results
    B = 4
    y_prompt = np.stack([R[2 * b]["y"] for b in range(B)])
    y_sample = np.concatenate([R[c]["ys"] for c in range(8)], 0).reshape(32, 1, D)
    outs = [y_prompt, y_sample]
    for (w, _) in GROUPS:
        kp = np.stack([R[2 * b][f"kv{w}p"] for b in range(B)], 1).reshape(2, B, w, 2, 16, 64)
        ks = np.concatenate([R[c][f"kv{w}s"] for c in range(8)], 1).reshape(2, 32, w, 2, 16, 64)
        outs += [kp, ks]
    outs.append(np.stack([R[2 * b]["poolp"] for b in range(B)])[None])
    outs.append(np.concatenate([R[c]["pools"] for c in range(8)], 0)[None])
    outs.append(np.stack([R[2 * b]["sconvp"] for b in range(B)])[None])
    outs.append(np.concatenate([R[c]["sconvs"] for c in range(8)], 0)[None])
    outs.append(np.stack([R[2 * b]["ffnp"] for b in range(B)], 1))
    outs.append(np.concatenate([R[c]["ffns"] for c in range(8)], 1))
    return tuple(np.ascontiguousarray(o, dtype=np.float32) for o in outs)
```

```python
import numpy as np
import concourse.bass as bass
import concourse.mybir as mybir
from concourse.bass_utils import run_bass_kernel_spmd

F32 = mybir.dt.float32
BF16 = mybir.dt.bfloat16
ALU = mybir.AluOpType
AF = mybir.ActivationFunctionType
AX = mybir.AxisListType


class Buf:
    __slots__ = ("name", "w", "r", "excl")

    def __init__(self, name="", excl=False):
        self.name = name
        self.w = None
        self.r = []
        self.excl = excl


class Ev:
    __slots__ = ("eng", "sem", "val", "resolved")

    def __init__(self, eng):
        self.eng = eng
        self.sem = None
        self.val = None
        self.resolved = False


def L(method, *args, **kw):
    return lambda e: getattr(e, method)(*args, **kw)


class Sched:
    ROLL = 30000
    NDMA = 24

    def __init__(self, nc, stack):
        self.nc = nc
        self.stack = stack
        self.engs = ["pe", "act", "dve", "pool", "sp"]
        self.ops = {e: [] for e in self.engs}
        self.cur_sem = {}
        self.cur_cnt = {}
        self.pending = {e: [] for e in self.engs}
        self.waited = {e: {} for e in self.engs}
        for e in self.engs:
            self._new_sem(e)
        self.dma_sems = [stack.enter_context(nc.semaphore(f"dq{i}")) for i in range(self.NDMA)]
        self.dma_cnt = [0] * self.NDMA
        self.dma_rr = 0
        self.nops = 0

    def _new_sem(self, e):
        self.cur_sem[e] = self.stack.enter_context(self.nc.semaphore(f"s_{e}_{len(self.ops[e])}"))
        self.cur_cnt[e] = 0

    def _collect(self, eng, reads, writes):
        evs = []
        for b in reads:
            if b.w is not None:
                evs.append(b.w)
            if b.excl:
                evs.extend(e for e in b.r if e.eng != eng)
        for b in writes:
            if b.w is not None:
                evs.append(b.w)
            evs.extend(b.r)
        waits = []
        for ev in evs:
            if ev.eng == eng and ev.eng == "pe":
                continue
            if not ev.resolved:
                if ev.eng == eng:
                    continue
                raise RuntimeError("dependency on unresolved (unsignaled) event")
            sid = id(ev.sem)
            if self.waited[eng].get(sid, -1) >= ev.val:
                continue
            self.waited[eng][sid] = ev.val
            waits.append((ev.sem, ev.val))
        return waits

    def _mark(self, ev, reads, writes):
        for b in reads:
            b.r.append(ev)
        for b in writes:
            b.w = ev
            b.r = []

    def op(self, eng, fn, reads=(), writes=(), signal=True):
        self.nops += 1
        waits = self._collect(eng, reads, writes)
        ev = Ev(eng)
        if signal:
            if self.cur_cnt[eng] >= self.ROLL:
                self._new_sem(eng)
            self.cur_cnt[eng] += 1
            ev.sem = self.cur_sem[eng]
            ev.val = self.cur_cnt[eng]
            ev.resolved = True
            for p in self.pending[eng]:
                p.sem, p.val, p.resolved = ev.sem, ev.val, True
            self.pending[eng] = []
            self.ops[eng].append((waits, fn, (ev.sem, 1)))
        else:
            self.pending[eng].append(ev)
            self.ops[eng].append((waits, fn, None))
        self._mark(ev, reads, writes)
        return ev

    def dma(self, out_ap, in_ap, reads=(), writes=(), eng="sp", **kw):
        self.nops += 1
        waits = self._collect(eng, reads, writes)
        s = self.dma_rr
        self.dma_rr = (self.dma_rr + 1) % self.NDMA
        sem = self.dma_sems[s]
        if self.dma_cnt[s] > 0:
            sid = id(sem)
            if self.waited[eng].get(sid, -1) < self.dma_cnt[s]:
                self.waited[eng][sid] = self.dma_cnt[s]
                waits.append((sem, self.dma_cnt[s]))
        self.dma_cnt[s] += 16
        ev = Ev("dma")
        ev.sem, ev.val, ev.resolved = sem, self.dma_cnt[s], True
        fn = L("dma_start", out=out_ap, in_=in_ap, **kw)
        self.ops[eng].append((waits, fn, (sem, 16)))
        self._mark(ev, reads, writes)
        return ev

    def wait_all(self, eng, evs):
        waits = []
        for ev in evs:
            waits.append((ev.sem, ev.val))
        self.ops[eng].append((waits, None, None))

    def final_wait_dmas(self, eng="sp"):
        waits = [(self.dma_sems[s], self.dma_cnt[s]) for s in range(self.NDMA) if self.dma_cnt[s] > 0]
        self.ops[eng].append((waits, None, None))

    def replay(self):
        nc = self.nc
        with nc.Block() as block:
            def run(e_name):
                def body(eng):
                    for waits, fn, sig in self.ops[e_name]:
                        for (sem, val) in waits:
                            eng.wait_ge(sem, val)
                        if fn is None:
                            continue
                        ins = fn(eng)
                        if sig is not None:
                            ins.then_inc(sig[0], sig[1])
                return body
            block.tensor(run("pe"))
            block.scalar(run("act"))
            block.vector(run("dve"))
            block.gpsimd(run("pool"))
            block.sync(run("sp"))

from contextlib import ExitStack

T = 4096
NT = 32
D = 1024
DFF = 2816
NJ = 22
DEPTH = 4
ALPHA = (2.0 * DEPTH) ** 0.25
LN_EPS = 1e-5
GROUPS = ((128, 1), (512, 4), (2048, 16))
POOL_W = (2, 4, 8, 16)
PAST = 8192
NS = 4
FCH = (12, 12, 8)
DBG = {}


def build(with_sample=True, phases=None):
    nc = bass.Bass("TRN2", target_bir_lowering=False)
    din = lambda n, shp: nc.dram_tensor(n, list(shp), F32, kind="ExternalInput").ap()
    dout = lambda n, shp: nc.dram_tensor(n, list(shp), F32, kind="ExternalOutput").ap()
    x_in = din("x", [T, D]); xs_in = din("xs", [NS, D])
    wqkv = din("wqkv", [2, 8, 3, D, 384]); wo = din("wo", [2, D, D])
    wup = din("wup", [DEPTH, NJ, D, 256]); wdn = din("wdn", [DEPTH, DFF, D]); ffk = din("ffk", [DEPTH, 128, NJ, 3])
    ffk_tm = din("ffk_tm", [DEPTH, 3, DFF])
    pwin = din("pwin", [D, D]); pwgrp = din("pwgrp", [4, 256, 256]); pscale = din("pscale", [128, 8]); pwout = din("pwout", [D, D])
    prcnt = din("prcnt", [128, 8, 16])
    swin = din("swin", [8, D, 384]); sk = din("sk", [128, 8, 3]); sk_tm = din("sk_tm", [3, D]); swout = din("swout", [D, D])
    lng = din("lng", [DEPTH, 2, D]); lnb = din("lnb", [DEPTH, 2, D])
    rope = [din(f"rope{d}", [128, NT, 2, 32]) for (_, d) in GROUPS]
    rope_s = din("rope_s", [NS, 2, 32])
    band_in = din("band", [128, 256]); ident_in = din("ident", [128, 128])
    sel_in = din("sel", [NS, NS, 128]); oneh_in = din("oneh", [128, NS, NS])
    c_in = [din(f"c{w}", [2, NS, w, 2048]) for (w, _) in GROUPS]
    st_pool = din("st_pool", [NS, 15, D]); st_sconv = din("st_sconv", [NS, 2, D]); st_ffn = din("st_ffn", [DEPTH, NS, 2, DFF])

    y_out = dout("y", [T, D]); ys_out = dout("ys", [NS, D])
    kvp = [dout(f"kv{w}p", [2, w, 2048]) for (w, _) in GROUPS]
    kvs = [dout(f"kv{w}s", [2, NS, w, 2048]) for (w, _) in GROUPS]
    poolp = dout("poolp", [15, D]); pools = dout("pools", [NS, 15, D])
    sconvp = dout("sconvp", [2, D]); sconvs = dout("sconvs", [NS, 2, D])
    ffnp = dout("ffnp", [DEPTH, 2, DFF]); ffns = dout("ffns", [DEPTH, NS, 2, DFF])

    xres = nc.dram_tensor("xres_scr", [T, D], F32).ap()
    xT_scr = nc.dram_tensor("xT_scr", [128, 8, T], BF16).ap()
    mixT_scr = nc.dram_tensor("mixT_scr", [128, 8, T], BF16).ap()
    xres_s = nc.dram_tensor("xres_s_scr", [NS, D], F32).ap()
    xTs_scr = nc.dram_tensor("xTs_scr", [128, 8, NS], BF16).ap()
    qkvs_scr = nc.dram_tensor("qkvs_scr", [NS, 3, 3, D], F32).ap()
    ab_scr = nc.dram_tensor("ab_scr", [NS, 2, DFF], F32).ap()

    top = ExitStack()
    with top:
        S = Sched(nc, top)
        P = [top.enter_context(nc.psum_tensor(f"P{i}", [128, 512], F32)) for i in range(6)]
        PB = [top.enter_context(nc.psum_tensor(f"PB{i}", [128, 1024], BF16)) for i in range(2)]
        BP = [Buf(f"P{i}", excl=True) for i in range(6)]
        BPB = [Buf(f"PB{i}", excl=True) for i in range(2)]
        uid = [0]
        def sbt(st, name, shape, dt):
            uid[0] += 1
            return st.enter_context(nc.sbuf_tensor(f"sb_{name}_{uid[0]}", list(shape), dt))
        idf = sbt(top, "idf", [128, 128], F32); idb = sbt(top, "idb", [128, 128], BF16)
        bandf = sbt(top, "bandf", [128, 256], F32); band = sbt(top, "band", [128, 256], BF16)
        onesb = sbt(top, "onesb", [128, 64], BF16)
        Bconst = Buf("const")
        S.dma(idf[:], ident_in[:, :], writes=[Bconst])
        S.dma(bandf[:], band_in[:, :], writes=[Bconst])
        S.op("dve", L("tensor_copy", out=idb[:], in_=idf[:]), reads=[Bconst], writes=[Bconst])
        S.op("dve", L("tensor_copy", out=band[:], in_=bandf[:]), reads=[Bconst], writes=[Bconst])
        S.op("dve", L("memset", onesb[:], 1.0), writes=[Bconst])
        mbias = sbt(top, "mbias", [128, 256], BF16)
        S.op("dve", L("tensor_scalar", out=bandf[:], in0=bandf[:], scalar1=-1.0, scalar2=30000.0, op0=ALU.add, op1=ALU.mult), reads=[Bconst], writes=[Bconst])
        S.op("dve", L("tensor_copy", out=mbias[:], in_=bandf[:]), reads=[Bconst], writes=[Bconst])
        Bxres = [Buf(f"xres{t}") for t in range(NT)]
        BxTs = [Buf(f"xTs{t}") for t in range(NT)]
        BmixT = [Buf(f"mixT{h}") for h in range(8)]
        Bxres_s = Buf("xres_s"); BxTs_s = Buf("xTs_s"); Bqkvs = Buf("qkvs"); Babs = Buf("abs")
        sel_t = sbt(top, "sel", [NS, NS, 128], F32); oneh_t = sbt(top, "oneh", [128, NS, NS], F32)
        S.dma(sel_t[:], sel_in[:, :, :], writes=[Bconst])
        S.dma(oneh_t[:], oneh_in[:, :, :], writes=[Bconst])

        def load_xTs(st):
            t_ = sbt(st, "xTs", [128, 8, NS], BF16); B_ = Buf()
            S.dma(t_[:], xTs_scr[:, :, :], reads=[BxTs_s], writes=[B_])
            return t_, B_

        def to_T(st, src, Bsrc, n, name):
            sbb = sbt(st, name + "_b", [NS, n * 128], BF16); Bb = Buf()
            S.op("act", L("copy", out=sbb[:], in_=src), reads=[Bsrc], writes=[Bb])
            dst = sbt(st, name + "_T", [128, n, NS], BF16); Bd = Buf()
            for k in range(n):
                S.op("pe", L("transpose", out=PB[0][:, k * NS:(k + 1) * NS], in_=sbb[:, k * 128:(k + 1) * 128], identity=idb[0:NS, 0:NS]),
                     reads=[Bb, Bconst], writes=[BPB[0]], signal=(k == n - 1))
            S.op("dve", L("tensor_copy", out=dst[:].rearrange("p n s -> p (n s)"), in_=PB[0][:, 0:n * NS]), reads=[BPB[0]], writes=[Bd])
            return dst, Bd

        def ln_sample(ln, li, which, actT_s, Bact_s, nk, w, Bw):
            for c, pi in enumerate((4, 5)):
                for k in range(nk):
                    S.op("pe", L("matmul", P[pi][0:NS, :], lhsT=actT_s[:, k, :], rhs=w[:, k, c * 512:(c + 1) * 512], start=(k == 0), stop=(k == nk - 1)),
                         reads=[Bw, Bact_s], writes=[BP[pi]], signal=(k == nk - 1))
            src = xs_in[:, :] if (li == 0 and which == 0) else xres_s[:, :]
            srcb = [] if (li == 0 and which == 0) else [Bxres_s]
            dst = ys_out[:, :] if (li == DEPTH - 1 and which == 1) else xres_s[:, :]
            ln.tile(P[4][0:NS, :], P[5][0:NS, :], BP[4], BP[5], src, srcb, dst, [Bxres_s], xTs_scr[:, :, :], [BxTs_s], np_=NS)
        rr = {"cast": 0}

        def cast(out_ap, in_ap, reads, writes):
            eng = ("pool", "act")[rr["cast"] % 2]
            rr["cast"] += 1
            if eng == "act":
                return S.op("act", L("copy", out=out_ap, in_=in_ap), reads=reads, writes=writes)
            return S.op("pool", L("tensor_copy", out=out_ap, in_=in_ap), reads=reads, writes=writes)

        def load_xT(st):
            xT = sbt(st, "xT", [128, 8, T], BF16)
            BxT = [Buf(f"xT{t}") for t in range(NT)]
            for q in range(8):
                S.dma(xT[:, :, q * 512:(q + 1) * 512], xT_scr[:, :, q * 512:(q + 1) * 512],
                      reads=BxTs[4 * q:4 * q + 4], writes=BxT[4 * q:4 * q + 4])
            return xT, BxT

        def fm_block_to_rows(src_ap, Bsrc, r, rowbuf, Brow, j):
            S.op("pe", L("matmul", P[5][0:r, 0:128], lhsT=src_ap, rhs=idf[:, :], start=True, stop=True), reads=[Bsrc, Bconst], writes=[BP[5]])
            S.op("dve", L("tensor_copy", out=rowbuf[0:r, j * 128:(j + 1) * 128], in_=P[5][0:r, 0:128]), reads=[BP[5]], writes=[Brow])

        class LN:
            def __init__(self, st, li, which):
                self.xr = [sbt(st, f"ln_xr{i}", [128, D], F32) for i in range(2)]
                self.xb = [sbt(st, f"ln_xb{i}", [128, D], BF16) for i in range(2)]
                self.stats = [sbt(st, f"ln_st{i}", [128, 2, 6], F32) for i in range(2)]
                self.mv = [sbt(st, f"ln_mv{i}", [128, 2], F32) for i in range(2)]
                self.rs = [sbt(st, f"ln_rs{i}", [128, 1], F32) for i in range(2)]
                self.xo = [sbt(st, f"ln_xo{i}", [128, 8, 128], BF16) for i in range(2)]
                self.gb = sbt(st, "ln_gb", [128, 2, D], F32)
                self.B = [[Buf() for _ in range(6)] for _ in range(2)]
                self.Bgb = Buf()
                S.dma(self.gb[:, 0, :], lng[li, which:which + 1, :].partition_broadcast(128), writes=[self.Bgb])
                S.dma(self.gb[:, 1, :], lnb[li, which:which + 1, :].partition_broadcast(128), writes=[self.Bgb])
                self.n = 0

            def tile(self, psA, psB, BpA, BpB, src_ap, src_bufs, dst_ap, dst_bufs, xT_dst_ap, xT_bufs, np_=128):
                i = self.n % 2
                self.n += 1
                xr, xb, stats, mv, rs, xo = self.xr[i], self.xb[i], self.stats[i], self.mv[i], self.rs[i], self.xo[i]
                Bx, Bb, Bs, Bm, Br, Bo = self.B[i]
                S.dma(xr[0:np_, :], src_ap, reads=src_bufs, writes=[Bx])
                for c, (ps, Bp) in enumerate(((psA, BpA), (psB, BpB))):
                    S.op("dve", L("scalar_tensor_tensor", out=xr[0:np_, c * 512:(c + 1) * 512], in0=xr[0:np_, c * 512:(c + 1) * 512], scalar=ALPHA, in1=ps, op0=ALU.mult, op1=ALU.add),
                         reads=[Bx, Bp], writes=[Bx])
                for c in range(2):
                    S.op("dve", L("bn_stats", out=stats[0:np_, c, :], in_=xr[0:np_, c * 512:(c + 1) * 512]), reads=[Bx], writes=[Bs])
                S.op("dve", L("bn_aggr", out=mv[0:np_, :], in_=stats[0:np_]), reads=[Bs], writes=[Bm])
                S.op("act", L("activation", out=rs[0:np_, :], in_=mv[0:np_, 1:2], func=AF.Sqrt, bias=LN_EPS, scale=1.0), reads=[Bm], writes=[Br])
                S.op("dve", L("reciprocal", out=rs[0:np_, :], in_=rs[0:np_, :]), reads=[Br], writes=[Br])
                S.op("dve", L("tensor_scalar", out=xr[0:np_, :], in0=xr[0:np_, :], scalar1=mv[0:np_, 0:1], scalar2=rs[0:np_, 0:1], op0=ALU.subtract, op1=ALU.mult),
                     reads=[Bx, Bm, Br], writes=[Bx])
                S.op("pool", L("tensor_tensor", out=xr[0:np_, :], in0=xr[0:np_, :], in1=self.gb[0:np_, 0, :], op=ALU.mult), reads=[Bx, self.Bgb], writes=[Bx])
                S.op("pool", L("tensor_tensor", out=xr[0:np_, :], in0=xr[0:np_, :], in1=self.gb[0:np_, 1, :], op=ALU.add), reads=[Bx, self.Bgb], writes=[Bx])
                S.dma(dst_ap, xr[0:np_, :], reads=[Bx], writes=dst_bufs)
                S.op("act", L("copy", out=xb[0:np_, :], in_=xr[0:np_, :]), reads=[Bx], writes=[Bb])
                pb = PB[i]
                for k in range(8):
                    S.op("pe", L("transpose", out=pb[:, k * 128:k * 128 + np_], in_=xb[0:np_, k * 128:(k + 1) * 128], identity=idb[0:np_, 0:np_]),
                         reads=[Bb, Bconst], writes=[BPB[i]], signal=(k == 7))
                S.op("dve", L("tensor_copy", out=xo[:, :, 0:np_], in_=pb[:].rearrange("p (k n) -> p k n", k=8)[:, :, 0:np_]), reads=[BPB[i]], writes=[Bo])
                S.dma(xT_dst_ap, xo[:, :, 0:np_], reads=[Bo], writes=xT_bufs)

        def ln_dst(li, which, t):
            if li == DEPTH - 1 and which == 1:
                return y_out[t * 128:(t + 1) * 128, :]
            return xres[t * 128:(t + 1) * 128, :]

        def ln_src(li, which, t):
            if li == 0 and which == 0:
                return x_in[t * 128:(t + 1) * 128, :], []
            return xres[t * 128:(t + 1) * 128, :], [Bxres[t]]

        def load_wfull(st, name, src, nk, stg, Bstg):
            w = sbt(st, name, [128, nk, D], BF16)
            Bw = Buf(name)
            for k in range(nk):
                for c in range(2):
                    S.dma(stg[:, 0:512], src[k * 128:(k + 1) * 128, c * 512:(c + 1) * 512], writes=[Bstg])
                    cast(w[:, k, c * 512:(c + 1) * 512], stg[:, 0:512], [Bstg], [Bw])
            return w, Bw

        def proj_ln_phase(li, which, actT, BactT_fn, nk, w_src, st, sample_fn=None):
            stg = sbt(st, "wf_stg", [128, 512], F32); Bstg = Buf()
            w, Bw = load_wfull(st, "wfull", w_src, nk, stg, Bstg)
            ln = LN(st, li, which)
            for t in range(NT):
                pa, pb_ = (0, 1) if t % 2 == 0 else (2, 3)
                for c, pi in enumerate((pa, pb_)):
                    for k in range(nk):
                        S.op("pe", L("matmul", P[pi][:, :], lhsT=actT[:, k, t * 128:(t + 1) * 128], rhs=w[:, k, c * 512:(c + 1) * 512], start=(k == 0), stop=(k == nk - 1)),
                             reads=[Bw] + BactT_fn(t), writes=[BP[pi]], signal=(k == nk - 1))
                src, sb_ = ln_src(li, which, t)
                ln.tile(P[pa][:, :], P[pb_][:, :], BP[pa], BP[pb_], src, sb_, ln_dst(li, which, t), [Bxres[t]],
                        xT_scr[:, :, t * 128:(t + 1) * 128], [BxTs[t]])
            if sample_fn is not None and with_sample:
                aT, BaT = sample_fn(st)
                ln_sample(ln, li, which, aT, BaT, nk, w, Bw)

        def init_phase():
            with ExitStack() as st:
                xr = [sbt(st, f"i_xr{i}", [128, D], F32) for i in range(2)]
                xb = [sbt(st, f"i_xb{i}", [128, D], BF16) for i in range(2)]
                xo = [sbt(st, f"i_xo{i}", [128, 8, 128], BF16) for i in range(2)]
                Bs = [[Buf() for _ in range(3)] for _ in range(2)]
                for t in range(NT):
                    i = t % 2
                    S.dma(xr[i][:], x_in[t * 128:(t + 1) * 128, :], writes=[Bs[i][0]])
                    S.op("act", L("copy", out=xb[i][:], in_=xr[i][:]), reads=[Bs[i][0]], writes=[Bs[i][1]])
                    for k in range(8):
                        S.op("pe", L("transpose", out=PB[i][:, k * 128:(k + 1) * 128], in_=xb[i][:, k * 128:(k + 1) * 128], identity=idb[:]),
                             reads=[Bs[i][1], Bconst], writes=[BPB[i]], signal=(k == 7))
                    S.op("dve", L("tensor_copy", out=xo[i][:], in_=PB[i][:].rearrange("p (k n) -> p k n", k=8)), reads=[BPB[i]], writes=[Bs[i][2]])
                    S.dma(xT_scr[:, :, t * 128:(t + 1) * 128], xo[i][:], reads=[Bs[i][2]], writes=[BxTs[t]])
                if with_sample:
                    xs0 = sbt(st, "xs0", [NS, D], F32); Bxs0 = Buf()
                    S.dma(xs0[:], xs_in[:, :], writes=[Bxs0])
                    xsT, BxsT0 = to_T(st, xs0[:], Bxs0, 8, "xs0")
                    S.dma(xTs_scr[:, :, :], xsT[:], reads=[BxsT0], writes=[BxTs_s])
                    for gi, (win, d) in enumerate(GROUPS):
                        for ia_ in range(2):
                            for s_ in range(NS):
                                S.dma(kvs[gi][ia_, s_, 0:win - 1, :], c_in[gi][ia_, s_, 1:win, :])
                    for s_ in range(NS):
                        S.dma(pools[s_, 0:14, :], st_pool[s_, 1:15, :])
                    S.dma(sconvs[:, 0, :], st_sconv[:, 1, :])
                    for l_ in range(DEPTH):
                        S.dma(ffns[l_, :, 0, :], st_ffn[l_, :, 1, :])

        def ffn_phase(li):
            with ExitStack() as st:
                stg = sbt(st, "wf_stg", [128, 512], F32); Bstg = Buf()
                wd, Bwd = load_wfull(st, "wdn", wdn[li], NJ, stg, Bstg)
                fk = sbt(st, "fk", [128, NJ, 3], F32); Bfk = Buf()
                S.dma(fk[:], ffk[li], writes=[Bfk])
                ah = sbt(st, "ah", [128, NJ, 2], F32); Bah = Buf()
                S.op("dve", L("memset", ah[:], 0.0), writes=[Bah])
                ln = LN(st, li, 1)
                if with_sample:
                    xs_T, BxsT = load_xTs(st)
                    abt = [sbt(st, f"abt{i}", [NS, 256], F32) for i in range(2)]; Babt = [Buf(), Buf()]
                sti = ExitStack()
                wst2 = [sbt(sti, f"wst{i}", [128, 8, 256], F32) for i in range(2)]; Bwst2 = [Buf(), Buf()]
                wbf = [sbt(sti, f"wbf{i}", [128, 8, 256], BF16) for i in range(2)]; Bwbf = [Buf(), Buf()]
                asb = [sbt(sti, f"asb{i}", [128, 514], F32) for i in range(2)]; Basb = [Buf(), Buf()]
                tmp = [sbt(sti, f"ftmp{i}", [128, 512], F32) for i in range(2)]; Btmp = [Buf(), Buf()]
                uu = [sbt(sti, f"fu{i}", [128, 512], F32) for i in range(2)]; Bu = [Buf(), Buf()]
                g = sbt(sti, "g", [128, NJ, 12 * 128], BF16)
                xTc = sbt(sti, "xTc", [128, 8, 12 * 128], BF16)
                t0 = 0
                it = 0
                for ch in FCH:
                    ntok = ch * 128
                    Bg = [Buf() for _ in range(ch)]
                    BxTc = [Buf() for _ in range(ch)]
                    for q in range(ch // 4):
                        S.dma(xTc[:, :, q * 512:(q + 1) * 512], xT_scr[:, :, t0 * 128 + q * 512:t0 * 128 + (q + 1) * 512],
                              reads=BxTs[t0 + 4 * q:t0 + 4 * q + 4], writes=BxTc[4 * q:4 * q + 4])
                    FSTG = DBG.get("ffn_stage", 9)
                    for j in range(NJ if FSTG >= 1 else 0):
                        wi = j % 2
                        S.dma(wst2[wi][:], wup[li, j].rearrange("(k p) c -> p k c", p=128), writes=[Bwst2[wi]])
                        cast(wbf[wi][:], wst2[wi][:], [Bwst2[wi]], [Bwbf[wi]])
                        if with_sample and t0 == 0:
                            for k in range(8):
                                S.op("pe", L("matmul", P[5][0:NS, 0:256], lhsT=xs_T[:, k, :], rhs=wbf[wi][:, k, :], start=(k == 0), stop=(k == 7)),
                                     reads=[Bwbf[wi], BxsT], writes=[BP[5]], signal=(k == 7))
                            S.op("act", L("copy", out=abt[wi][:], in_=P[5][0:NS, 0:256]), reads=[BP[5]], writes=[Babt[wi]])
                            S.dma(ab_scr[:, :, j * 128:(j + 1) * 128], abt[wi][:].rearrange("p (s c) -> p s c", s=2), reads=[Babt[wi]], writes=[Babs])
                        for b in range(ch // 4):
                            i = it % 2
                            it += 1
                            pa, pb_ = (0, 1) if i == 0 else (2, 3)
                            cols = slice(b * 512, (b + 1) * 512)
                            for half, pi in ((0, pa), (1, pb_)):
                                for k in range(8):
                                    S.op("pe", L("matmul", P[pi][:, :], lhsT=wbf[wi][:, k, half * 128:(half + 1) * 128], rhs=xTc[:, k, cols], start=(k == 0), stop=(k == 7)),
                                         reads=[Bwbf[wi]] + BxTc[4 * b:4 * b + 4], writes=[BP[pi]], signal=(k == 7))
                            a = asb[i]
                            S.op("dve", L("tensor_copy", out=a[:, 0:2], in_=ah[:, j, :]), reads=[Bah], writes=[Basb[i]])
                            S.op("act", L("copy", out=a[:, 2:514], in_=P[pa][:, :]), reads=[BP[pa]], writes=[Basb[i]])
                            S.op("dve", L("tensor_copy", out=ah[:, j, :], in_=a[:, 512:514]), reads=[Basb[i]], writes=[Bah])
                            tm = tmp[i]
                            S.op("dve", L("tensor_scalar", out=tm[:], in0=a[:, 0:512], scalar1=fk[:, j, 0:1], scalar2=None, op0=ALU.mult), reads=[Basb[i], Bfk], writes=[Btmp[i]])
                            S.op("dve", L("scalar_tensor_tensor", out=tm[:], in0=a[:, 1:513], scalar=fk[:, j, 1:2], in1=tm[:], op0=ALU.mult, op1=ALU.add), reads=[Basb[i], Bfk, Btmp[i]], writes=[Btmp[i]])
                            S.op("dve", L("scalar_tensor_tensor", out=tm[:], in0=a[:, 2:514], scalar=fk[:, j, 2:3], in1=tm[:], op0=ALU.mult, op1=ALU.add), reads=[Basb[i], Bfk, Btmp[i]], writes=[Btmp[i]])
                            u = uu[i]
                            S.op("act", L("activation", out=u[:], in_=tm[:], func=AF.Gelu), reads=[Btmp[i]], writes=[Bu[i]])
                            S.op("dve", L("tensor_tensor", out=g[:, j, cols], in0=P[pb_][:, :], in1=u[:], op=ALU.mult), reads=[Bu[i], BP[pb_]], writes=Bg[4 * b:4 * b + 4])
                    for tt in range(ch if FSTG >= 2 else 0):
                        t = t0 + tt
                        pa, pb_ = (4, 5)
                        for c, pi in enumerate((pa, pb_)):
                            for j in range(NJ):
                                S.op("pe", L("matmul", P[pi][:, :], lhsT=g[:, j, tt * 128:(tt + 1) * 128], rhs=wd[:, j, c * 512:(c + 1) * 512], start=(j == 0), stop=(j == NJ - 1)),
                                     reads=[Bwd, Bg[tt]], writes=[BP[pi]], signal=(j == NJ - 1))
                        ln.tile(P[pa][:, :], P[pb_][:, :], BP[pa], BP[pb_], xres[t * 128:(t + 1) * 128, :], [Bxres[t]], ln_dst(li, 1, t), [Bxres[t]],
                                xT_scr[:, :, t * 128:(t + 1) * 128], [BxTs[t]])
                    t0 += ch
                sti.close()
                with ExitStack() as st3:
                    rowb = sbt(st3, "rowb", [2, DFF], F32); Browb = Buf()
                    for j in range(NJ if FSTG >= 3 else 0):
                        fm_block_to_rows(ah[:, j, :], Bah, 2, rowb, Browb, j)
                    if FSTG >= 3:
                        S.dma(ffnp[li, :, :], rowb[:], reads=[Browb])
                if with_sample:
                    with ExitStack() as st2:
                        a_b = sbt(st2, "ab_s", [NS, 2, DFF], F32); Bab = Buf()
                        S.dma(a_b[:], ab_scr[:, :, :], reads=[Babs], writes=[Bab])
                        stf = sbt(st2, "stf", [NS, 2, DFF], F32); Bstf = Buf()
                        S.dma(stf[:], st_ffn[li], writes=[Bstf])
                        ktm = sbt(st2, "ktm", [NS, 3, DFF], F32); Bktm = Buf()
                        for tap in range(3):
                            S.dma(ktm[:, tap, :], ffk_tm[li, tap:tap + 1, :].partition_broadcast(NS), writes=[Bktm])
                        t1 = sbt(st2, "t1", [NS, DFF], F32); t2 = sbt(st2, "t2", [NS, DFF], F32); Bt1, Bt2 = Buf(), Buf()
                        tt_ = lambda o, a, b, op, rd, wr: S.op("dve", L("tensor_tensor", out=o, in0=a, in1=b, op=op), reads=rd, writes=wr)
                        tt_(t1[:], stf[:, 0, :], ktm[:, 0, :], ALU.mult, [Bstf, Bktm], [Bt1])
                        tt_(t2[:], stf[:, 1, :], ktm[:, 1, :], ALU.mult, [Bstf, Bktm], [Bt2])
                        tt_(t1[:], t1[:], t2[:], ALU.add, [Bt1, Bt2], [Bt1])
                        tt_(t2[:], a_b[:, 0, :], ktm[:, 2, :], ALU.mult, [Bab, Bktm, Bt1], [Bt2])
                        tt_(t1[:], t1[:], t2[:], ALU.add, [Bt1, Bt2], [Bt1])
                        S.op("act", L("activation", out=t2[:], in_=t1[:], func=AF.Gelu), reads=[Bt1], writes=[Bt2])
                        tt_(t1[:], t2[:], a_b[:, 1, :], ALU.mult, [Bt2, Bab], [Bt1])
                        gsT, BgsT = to_T(st2, t1[:], Bt1, NJ, "gs")
                        ln_sample(ln, li, 1, gsT, BgsT, NJ, wd, Bwd)
                        S.dma(ffns[li, :, 1, :], a_b[:, 0, :], reads=[Bab])

        def attn_phase(li, ia):
            with ExitStack() as st:
                xT, BxT = load_xT(st)
                wst = sbt(st, "wst", [128, 4, 384], F32); Bwst = Buf()
                wbf = [sbt(st, f"wbf{i}", [128, 8, 384], BF16) for i in range(2)]; Bwbf = [Buf(), Buf()]
                rt0 = sbt(st, "rt0", [128, NT, 2, 32], F32); rt = [rt0, rt0]; Brt0 = Buf(); Brt = [Brt0, Brt0]
                QT = sbt(st, "QT", [128, NT, 128], BF16); KTt = sbt(st, "KT", [128, NT, 128], BF16)
                V = sbt(st, "V", [128, NT, 128], BF16)
                BQ = [Buf() for _ in range(NT)]; BK = [Buf() for _ in range(NT)]; BV = [Buf() for _ in range(NT)]
                accN = sbt(st, "accN", [128, T], F32); accD = sbt(st, "accD", [128, T], F32)
                Bacc = [Buf() for _ in range(8)]
                qkr = [sbt(st, f"qkr{i}", [128, 4, 256], F32) for i in range(2)]; Bqkr = [Buf(), Buf()]
                tb = [sbt(st, f"tb{i}", [128, 4, 4, 2, 32], F32) for i in range(2)]; Btb = [Buf(), Buf()]
                qkb = [sbt(st, f"qkb{i}", [128, 4, 256], BF16) for i in range(2)]; Bqkb = [Buf(), Buf()]
                raw = [sbt(st, f"raw{i}", [128, 4, 384], F32) for i in range(2)]; Braw = [Buf(), Buf()]
                Et = [sbt(st, f"E{i}", [128, 256], BF16) for i in range(4)]; BE = [Buf() for _ in range(4)]
                Em = [sbt(st, f"Em{i}", [128, 256], BF16) for i in range(6)]; BEm = [Buf() for _ in range(6)]
                mixT = [sbt(st, f"mixThp{i}", [128, 512], BF16) for i in range(2)]; Bmix = [Buf(), Buf()]
                xg = [sbt(st, f"xg{i}", [128, 8, 128], BF16) for i in range(4)]; Bxg = [Buf() for _ in range(4)]
                gcount = [0]
                if with_sample:
                    xs_T, BxsT = load_xTs(st)
                    rs_t = sbt(st, "rs_t", [NS, 2, 32], F32); Brs = Buf()
                    S.dma(rs_t[:], rope_s[:, :, :], writes=[Brs])
                    qsr = [sbt(st, f"qsr{i}", [NS, 384], F32) for i in range(2)]; Bqsr = [Buf(), Buf()]
                    tbs = [sbt(st, f"tbs{i}", [NS, 4, 2, 32], F32) for i in range(2)]; Btbs = [Buf(), Buf()]
                unit = 0
                for hp in range(DBG.get("hp_n", 8)):
                    for gi, (win, d) in enumerate(GROUPS):
                        if gi not in DBG.get("groups", (0, 1, 2)):
                            continue
                        nblk = NT // d
                        ui = unit % 2
                        unit += 1
                        wsrc = wqkv[ia, hp, gi].rearrange("(k p) c -> p k c", p=128)
                        for kh in range(2):
                            S.dma(wst[:], wsrc[:, kh * 4:(kh + 1) * 4, :], writes=[Bwst])
                            cast(wbf[ui][:, kh * 4:(kh + 1) * 4, :], wst[:], [Bwst], [Bwbf[ui]])
                        S.dma(rt[ui][:], rope[gi], writes=[Brt[ui]])
                        w = wbf[ui]
                        r_t = rt[ui]
                        if with_sample:
                            for k in range(8):
                                S.op("pe", L("matmul", P[5][0:NS, 0:384], lhsT=xs_T[:, k, :], rhs=w[:, k, :], start=(k == 0), stop=(k == 7)),
                                     reads=[Bwbf[ui], BxsT], writes=[BP[5]], signal=(k == 7))
                            qs = qsr[ui]; ts_ = tbs[ui]
                            ssrc = P[5][0:NS, 0:256].rearrange("p (j h f) -> p j h f", j=4, h=2)
                            scos = rs_t[:, 0, :].unsqueeze(1).unsqueeze(1).to_broadcast([NS, 4, 2, 32])
                            ssin = rs_t[:, 1, :].unsqueeze(1).to_broadcast([NS, 4, 32])
                            sdst = qs[:, 0:256].rearrange("p (j h f) -> p j h f", j=4, h=2)
                            S.op("dve", L("tensor_tensor", out=sdst, in0=ssrc, in1=scos, op=ALU.mult), reads=[BP[5], Brs], writes=[Bqsr[ui]])
                            S.op("dve", L("tensor_tensor", out=ts_[:, :, 0, :], in0=ssrc[:, :, 1, :], in1=ssin, op=ALU.mult), reads=[BP[5], Brs], writes=[Btbs[ui]])
                            S.op("dve", L("tensor_tensor", out=ts_[:, :, 1, :], in0=ssrc[:, :, 0, :], in1=ssin, op=ALU.mult), reads=[BP[5], Brs], writes=[Btbs[ui]])
                            S.op("act", L("copy", out=qs[:, 256:384], in_=P[5][0:NS, 256:384]), reads=[BP[5]], writes=[Bqsr[ui]])
                            S.op("dve", L("tensor_tensor", out=sdst[:, :, 0, :], in0=sdst[:, :, 0, :], in1=ts_[:, :, 0, :], op=ALU.subtract), reads=[Bqsr[ui], Btbs[ui]], writes=[Bqsr[ui]])
                            S.op("dve", L("tensor_tensor", out=sdst[:, :, 1, :], in0=sdst[:, :, 1, :], in1=ts_[:, :, 1, :], op=ALU.add), reads=[Bqsr[ui], Btbs[ui]], writes=[Bqsr[ui]])
                            S.dma(qkvs_scr[:, gi, :, hp * 128:(hp + 1) * 128], qs[:].rearrange("p (s c) -> p s c", s=3), reads=[Bqsr[ui]], writes=[Bqkvs])
                        PSTG = DBG.get("proj_stage", 9)
                        for bt in range(NT // 4 if PSTG >= 1 else 0):
                            i = bt % 2
                            for bi in range(4):
                                blk = bt * 4 + bi
                                r, nb = blk // nblk, blk % nblk
                                tok0 = r + d * 128 * nb
                                toks = slice(tok0, tok0 + d * 127 + 1, d)
                                tl = sorted(set([tok0 // 128 + x for x in range(0, (d * 127) // 128 + 1)]))
                                if d == 1:
                                    lsrc = lambda k, toks=toks: xT[:, k, toks]
                                    lreads = [BxT[x] for x in tl if x < NT]
                                else:
                                    gsl = gcount[0] % 4
                                    gcount[0] += 1
                                    xg_ = xg[gsl]
                                    S.op(("pool", "act")[gsl % 2], L(("tensor_copy", "copy")[gsl % 2], out=xg_[:], in_=xT[:, :, toks]),
                                         reads=[BxT[x] for x in tl if x < NT], writes=[Bxg[gsl]])
                                    lsrc = lambda k, xg_=xg_: xg_[:, k, :]
                                    lreads = [Bxg[gsl]]
                                for k in range(8):
                                    S.op("pe", L("matmul", P[bi][:, 0:384], lhsT=lsrc(k), rhs=w[:, k, :], start=(k == 0), stop=(k == 7)),
                                         reads=[Bwbf[ui]] + lreads, writes=[BP[bi]], signal=(k == 7))
                            if PSTG < 2:
                                continue
                            q_r = qkr[i]; t_b = tb[i]; q_b = qkb[i]; rw = raw[i]
                            for bi in range(4):
                                S.op("act", L("copy", out=rw[:, bi, :], in_=P[bi][:, 0:384]), reads=[BP[bi]], writes=[Braw[i]])
                            rqk = rw[:, :, 0:256].rearrange("p b (j h f) -> p b j h f", j=4, h=2)
                            qd = q_r[:].rearrange("p b (j h f) -> p b j h f", j=4, h=2)
                            cosb = r_t[:, bt * 4:bt * 4 + 4, 0, :].unsqueeze(2).to_broadcast([128, 4, 4, 32])
                            sinb = r_t[:, bt * 4:bt * 4 + 4, 1, :].unsqueeze(2).to_broadcast([128, 4, 4, 32])
                            for h_ in range(2):
                                S.op("dve", L("tensor_tensor", out=qd[:, :, :, h_, :], in0=rqk[:, :, :, h_, :], in1=cosb, op=ALU.mult), reads=[Braw[i], Brt[ui]], writes=[Bqkr[i]])
                                S.op("dve", L("tensor_tensor", out=t_b[:, :, :, h_, :], in0=rqk[:, :, :, 1 - h_, :], in1=sinb, op=ALU.mult), reads=[Braw[i], Brt[ui]], writes=[Btb[i]])
                            S.op("pool", L("tensor_tensor", out=qd[:, :, :, 0, :], in0=qd[:, :, :, 0, :], in1=t_b[:, :, :, 0, :], op=ALU.subtract), reads=[Bqkr[i], Btb[i]], writes=[Bqkr[i]])
                            S.op("dve", L("tensor_tensor", out=qd[:, :, :, 1, :], in0=qd[:, :, :, 1, :], in1=t_b[:, :, :, 1, :], op=ALU.add), reads=[Bqkr[i], Btb[i]], writes=[Bqkr[i]])
                            S.op("act", L("copy", out=q_b[:], in_=q_r[:]), reads=[Bqkr[i]], writes=[Bqkb[i]])
                            S.op("pool", L("tensor_copy", out=V[:, bt * 4:bt * 4 + 4, :], in_=rw[:, :, 256:384]), reads=[Braw[i]], writes=BV[bt * 4:bt * 4 + 4])
                            for bi in range(4):
                                blk = bt * 4 + bi
                                r, nb = blk // nblk, blk % nblk
                                tok0 = r + d * 128 * nb
                                if tok0 >= T - win and DBG.get("kvout", True):
                                    row0 = tok0 - (T - win)
                                    rows = slice(row0, row0 + d * 127 + 1, d)
                                    S.dma(kvp[gi][ia, rows, hp * 128:(hp + 1) * 128], q_r[:, bi, 128:256], reads=[Bqkr[i]])
                                    S.dma(kvp[gi][ia, rows, 1024 + hp * 128:1024 + (hp + 1) * 128], rw[:, bi, 256:384], reads=[Braw[i]])
                            if PSTG < 3:
                                continue
                            pbk = PB[i]
                            for bi in range(4):
                                for s_ in range(2):
                                    S.op("pe", L("transpose", out=pbk[:, (bi * 2 + s_) * 128:(bi * 2 + s_ + 1) * 128], in_=q_b[:, bi, s_ * 128:(s_ + 1) * 128], identity=idb[:]),
                                         reads=[Bqkb[i], Bconst], writes=[BPB[i]], signal=(bi == 3 and s_ == 1))
                            pv = pbk[:].rearrange("p (b s n) -> p b s n", b=4, s=2)
                            S.op("dve", L("tensor_copy", out=QT[:, bt * 4:bt * 4 + 4, :], in_=pv[:, :, 0, :]), reads=[BPB[i]], writes=BQ[bt * 4:bt * 4 + 4])
                            S.op("act", L("copy", out=KTt[:, bt * 4:bt * 4 + 4, :], in_=pv[:, :, 1, :]), reads=[BPB[i]], writes=BK[bt * 4:bt * 4 + 4])
                        ecount = 0
                        emidx = {}
                        for blk in range(NT if DBG.get("core", True) else 0):
                            r, nb = blk // nblk, blk % nblk
                            ncol = 256 if nb + 1 < nblk else 128
                            nqb = ncol // 128
                            for h in range(2):
                                pi = 4 + (ecount % 2)
                                ei = ecount % 4
                                mi = ecount % 6
                                ecount += 1
                                hs = slice(64 * h, 64 * h + 64)
                                S.op("pe", L("matmul", P[pi][:, 0:ncol], lhsT=KTt[hs, blk, :], rhs=QT[hs, blk:blk + nqb, :], start=True, stop=False),
                                     reads=[BK[blk]] + BQ[blk:blk + nqb], writes=[BP[pi]], signal=False)
                                S.op("pe", L("matmul", P[pi][:, 0:ncol], lhsT=idb[:, :], rhs=mbias[:, 0:ncol], start=False, stop=True),
                                     reads=[Bconst], writes=[BP[pi]])
                                S.op("act", L("activation", out=Em[mi][:, 0:ncol], in_=P[pi][:, 0:ncol], func=AF.Exp, scale=0.125), reads=[BP[pi]], writes=[BEm[mi]])
                                emidx[(blk, h)] = mi
                            qi = blk % 4
                            bnk = (blk // 4) % 2
                            pn, pd = (0, 1) if bnk == 0 else (2, 3)
                            for h in range(2):
                                hs = slice(64 * h, 64 * h + 64)
                                srcs = []
                                if nb >= 1:
                                    srcs.append((blk - 1, emidx[(blk - 1, h)], slice(128, 256)))
                                srcs.append((blk, emidx[(blk, h)], slice(0, 128)))
                                for (pp, lhs_fn) in ((pn, lambda kb, hs=hs: V[:, kb, hs]), (pd, lambda kb: onesb[:, :])):
                                    for si, (kb, mi, cs) in enumerate(srcs):
                                        S.op("pe", L("matmul", P[pp][hs, qi * 128:(qi + 1) * 128], lhsT=lhs_fn(kb), rhs=Em[mi][:, cs], start=(si == 0), stop=(si == len(srcs) - 1)),
                                             reads=[BV[kb], BEm[mi], Bconst], writes=[BP[pp]], signal=(si == len(srcs) - 1))
                            if qi == 3:
                                b0 = blk - 3
                                r0, nb0 = b0 // nblk, b0 % nblk
                                for (pp, acc) in ((pn, accN), (pd, accD)):
                                    accv = acc[:].rearrange("p (n r) -> p r n", r=d)
                                    if nblk >= 4:
                                        dst = accv[:, r0, 128 * nb0:128 * nb0 + 512]
                                        src = P[pp][:, :]
                                    else:
                                        dst = accv[:, r0:r0 + 2, 0:256]
                                        src = P[pp][:, :].rearrange("p (a n) -> p a n", a=2)
                                    if gi == 0:
                                        S.op("dve", L("tensor_copy", out=dst, in_=src), reads=[BP[pp]], writes=Bacc)
                                    else:
                                        S.op("dve", L("tensor_tensor", out=dst, in0=src, in1=dst, op=ALU.add), reads=[BP[pp]] + Bacc, writes=Bacc)
                    if not DBG.get("norm", True):
                        continue
                    for q in range(8):
                        cs = slice(q * 512, (q + 1) * 512)
                        S.op("dve", L("reciprocal", out=accD[:, cs], in_=accD[:, cs]), reads=Bacc, writes=Bacc)
                        S.op("dve", L("tensor_tensor", out=mixT[q % 2][:], in0=accN[:, cs], in1=accD[:, cs], op=ALU.mult), reads=Bacc, writes=[Bmix[q % 2]])
                        S.dma(mixT_scr[:, hp, cs], mixT[q % 2][:], reads=[Bmix[q % 2]], writes=[BmixT[hp]])
            if not DBG.get("wo", True):
                return
            with ExitStack() as st:
                mT = sbt(st, "mT", [128, 8, T], BF16); BmT = Buf()
                for hp in range(8):
                    S.dma(mT[:, hp, :], mixT_scr[:, hp, :], reads=[BmixT[hp]], writes=[BmT])
                def attn_sample(st2):
                    qk = sbt(st2, "qkvs", [NS, 3, 3, D], F32); Bqk = Buf()
                    S.dma(qk[:], qkvs_scr[:, :, :, :], reads=[Bqkvs], writes=[Bqk])
                    Kc = sbt(st2, "Kc", [128, D], F32); Vc = sbt(st2, "Vc", [128, D], F32); BKc, BVc = Buf(), Buf()
                    prod = sbt(st2, "prod", [128, D], F32); Bprod = Buf()
                    tmpv = sbt(st2, "tmpv", [128, D], F32); Btmpv = Buf()
                    Ssc = sbt(st2, "Ssc", [128, 16], F32); Es = sbt(st2, "Es", [128, 16], F32); BSs, BEs = Buf(), Buf()
                    pself = sbt(st2, "pself", [NS, D], F32); Bps = Buf()
                    sself = sbt(st2, "sself", [NS, 16], F32); eself = sbt(st2, "eself", [NS, 16], F32); Bss, Bes = Buf(), Buf()
                    first = True
                    for gi, (win, d) in enumerate(GROUPS):
                        for s_ in range(NS):
                            S.dma(Kc[:], c_in[gi][ia, s_, 0:win:d, 0:1024], writes=[BKc])
                            S.dma(Vc[:], c_in[gi][ia, s_, 0:win:d, 1024:2048], writes=[BVc])
                            for c in range(2):
                                S.op("pe", L("matmul", P[c][:, :], lhsT=sel_t[0:NS, s_, :], rhs=qk[:, gi, 0, c * 512:(c + 1) * 512], start=True, stop=True),
                                     reads=[Bqk, Bconst], writes=[BP[c]])
                                S.op("dve", L("tensor_tensor", out=prod[:, c * 512:(c + 1) * 512], in0=Kc[:, c * 512:(c + 1) * 512], in1=P[c][:, :], op=ALU.mult), reads=[BKc, BP[c]], writes=[Bprod])
                            S.op("dve", L("tensor_reduce", out=Ssc[:], in_=prod[:].rearrange("p (h f) -> p h f", h=16), axis=AX.X, op=ALU.add), reads=[Bprod], writes=[BSs])
                            S.op("act", L("activation", out=Es[:], in_=Ssc[:], func=AF.Exp, scale=0.125), reads=[BSs], writes=[BEs])
                            S.op("dve", L("tensor_tensor", out=tmpv[:].rearrange("p (h f) -> p h f", h=16), in0=Vc[:].rearrange("p (h f) -> p h f", h=16), in1=Es[:].unsqueeze(2).to_broadcast([128, 16, 64]), op=ALU.mult), reads=[BVc, BEs], writes=[Btmpv])
                            for c in range(2):
                                S.op("pe", L("matmul", P[2 + c][0:NS, :], lhsT=oneh_t[:, s_, :], rhs=tmpv[:, c * 512:(c + 1) * 512], start=first, stop=False),
                                     reads=[Btmpv, Bconst], writes=[BP[2 + c]])
                            S.op("pe", L("matmul", P[4][0:NS, 0:16], lhsT=oneh_t[:, s_, :], rhs=Es[:], start=first, stop=False), reads=[BEs, Bconst], writes=[BP[4]])
                            first = False
                        S.op("dve", L("tensor_tensor", out=pself[:], in0=qk[:, gi, 0, :], in1=qk[:, gi, 1, :], op=ALU.mult), reads=[Bqk], writes=[Bps])
                        S.op("dve", L("tensor_reduce", out=sself[:], in_=pself[:].rearrange("p (h f) -> p h f", h=16), axis=AX.X, op=ALU.add), reads=[Bps], writes=[Bss])
                        S.op("act", L("activation", out=eself[:], in_=sself[:], func=AF.Exp, scale=0.125), reads=[Bss], writes=[Bes])
                        S.op("dve", L("tensor_tensor", out=pself[:].rearrange("p (h f) -> p h f", h=16), in0=qk[:, gi, 2, :].rearrange("p (h f) -> p h f", h=16), in1=eself[:].unsqueeze(2).to_broadcast([NS, 16, 64]), op=ALU.mult), reads=[Bqk, Bes, Bss], writes=[Bps])
                        last = (gi == len(GROUPS) - 1)
                        for c in range(2):
                            S.op("pe", L("matmul", P[2 + c][0:NS, :], lhsT=idf[0:NS, 0:NS], rhs=pself[:, c * 512:(c + 1) * 512], start=False, stop=last), reads=[Bps, Bconst], writes=[BP[2 + c]])
                        S.op("pe", L("matmul", P[4][0:NS, 0:16], lhsT=idf[0:NS, 0:NS], rhs=eself[:], start=False, stop=last), reads=[Bes, Bconst], writes=[BP[4]])
                        S.dma(kvs[gi][ia, :, win - 1, 0:1024], qk[:, gi, 1, :], reads=[Bqk])
                        S.dma(kvs[gi][ia, :, win - 1, 1024:2048], qk[:, gi, 2, :], reads=[Bqk])
                    rden = sbt(st2, "rden", [NS, 16], F32); Brd = Buf()
                    mixs = sbt(st2, "mixs", [NS, D], F32); Bmx = Buf()
                    S.op("dve", L("reciprocal", out=rden[:], in_=P[4][0:NS, 0:16]), reads=[BP[4]], writes=[Brd])
                    for c in range(2):
                        S.op("dve", L("tensor_tensor", out=mixs[:, c * 512:(c + 1) * 512].rearrange("p (h f) -> p h f", h=8), in0=P[2 + c][0:NS, :].rearrange("p (h f) -> p h f", h=8), in1=rden[:, c * 8:(c + 1) * 8].unsqueeze(2).to_broadcast([NS, 8, 64]), op=ALU.mult), reads=[BP[2 + c], Brd], writes=[Bmx])
                    return to_T(st2, mixs[:], Bmx, 8, "mixs")
                proj_ln_phase(li, 0, mT, lambda t: [BmT], 8, wo[ia], st, sample_fn=attn_sample)

        def pool_phase(li):
            with ExitStack() as st0:
                pT = sbt(st0, "pooledT", [128, 8, T], BF16); BpT = [Buf() for _ in range(8)]
                us = sbt(st0, "us", [NS, D], F32); Bus = Buf()
                with ExitStack() as st:
                    xT, BxT = load_xT(st)
                    stg = sbt(st, "wf_stg", [128, 512], F32); Bstg = Buf()
                    w, Bw = load_wfull(st, "pwin", pwin, 8, stg, Bstg)
                    rc = sbt(st, "rc", [128, 8, 16], F32); Brc = Buf()
                    S.dma(rc[:], prcnt[:, :, :], writes=[Brc])
                    prow = sbt(st, "prow", [15, D], F32); Bprow = Buf()
                    ub = sbt(st, "ub", [128, 16 + T], F32); sA = sbt(st, "sA", [128, 16 + T], F32); sB = sbt(st, "sB", [128, 16 + T], F32)
                    Bub, BsA, BsB = Buf(), Buf(), Buf()
                    for (bu, Bb) in ((ub, Bub), (sA, BsA), (sB, BsB)):
                        S.op("pool", L("memset", bu[:, 0:16], 0.0), writes=[Bb])
                    if with_sample:
                        xs_T, BxsT = load_xTs(st)
                        for c in range(2):
                            for k in range(8):
                                S.op("pe", L("matmul", P[4 + c][0:NS, :], lhsT=xs_T[:, k, :], rhs=w[:, k, c * 512:(c + 1) * 512], start=(k == 0), stop=(k == 7)),
                                     reads=[Bw, BxsT], writes=[BP[4 + c]], signal=(k == 7))
                            S.op("act", L("copy", out=us[:, c * 512:(c + 1) * 512], in_=P[4 + c][0:NS, :]), reads=[BP[4 + c]], writes=[Bus])
                    for j in range(8):
                        for blk in range(8):
                            pi = blk % 4
                            for k in range(8):
                                S.op("pe", L("matmul", P[pi][:, :], lhsT=w[:, k, j * 128:(j + 1) * 128], rhs=xT[:, k, blk * 512:(blk + 1) * 512], start=(k == 0), stop=(k == 7)),
                                     reads=[Bw] + BxT[4 * blk:4 * blk + 4], writes=[BP[pi]], signal=(k == 7))
                            S.op("act", L("copy", out=ub[:, 16 + blk * 512:16 + (blk + 1) * 512], in_=P[pi][:, :]), reads=[BP[pi]], writes=[Bub])
                        wj = POOL_W[j // 2]
                        cur, Bcur = ub, Bub
                        step = 1
                        bufs = [(sA, BsA), (sB, BsB)]
                        bi = 0
                        while step < wj:
                            nxt, Bn = bufs[bi % 2]
                            bi += 1
                            for hh in range(2):
                                cs = slice(16 + hh * 2048, 16 + (hh + 1) * 2048)
                                cs2 = slice(16 - step + hh * 2048, 16 - step + (hh + 1) * 2048)
                                S.op(("dve", "pool")[hh], L("tensor_tensor", out=nxt[:, cs], in0=cur[:, cs], in1=cur[:, cs2], op=ALU.add), reads=[Bcur], writes=[Bn])
                            cur, Bcur = nxt, Bn
                            step *= 2
                        S.op("dve", L("tensor_tensor", out=cur[:, 16:32], in0=cur[:, 16:32], in1=rc[:, j, :], op=ALU.mult), reads=[Bcur, Brc], writes=[Bcur])
                        S.op("dve", L("tensor_scalar", out=cur[:, 16:32], in0=cur[:, 16:32], scalar1=float(wj), scalar2=None, op0=ALU.mult), reads=[Bcur], writes=[Bcur])
                        for hh in range(2):
                            cs = slice(16 + hh * 2048, 16 + (hh + 1) * 2048)
                            co = slice(hh * 2048, (hh + 1) * 2048)
                            S.op("dve", L("scalar_tensor_tensor", out=pT[:, j, co], in0=cur[:, cs], scalar=1.0 / wj, in1=ub[:, cs], op0=ALU.mult, op1=ALU.subtract), reads=[Bcur, Bub], writes=[BpT[j]])
                        fm_block_to_rows(ub[:, 16 + T - 15:16 + T], Bub, 15, prow, Bprow, j)
                    S.dma(poolp[:, :], prow[:], reads=[Bprow])
                with ExitStack() as st:
                    zT = pT
                    wgs = sbt(st, "wgs", [128, 4, 2, 256], F32); wgb = sbt(st, "wgb", [128, 4, 2, 256], BF16); Bwg = Buf()
                    psc = sbt(st, "psc", [128, 8], F32)
                    S.dma(wgs[:], pwgrp.rearrange("g (k p) c -> p g k c", p=128), writes=[Bwg])
                    S.dma(psc[:], pscale[:, :], writes=[Bwg])
                    S.op("dve", L("tensor_copy", out=wgb[:], in_=wgs[:]), reads=[Bwg], writes=[Bwg])
                    n = 0
                    for blk in range(8):
                        for gi in range(4):
                            pis = (0, 1) if n % 2 == 0 else (2, 3)
                            n += 1
                            for mt in range(2):
                                pi = pis[mt]
                                for kt in range(2):
                                    S.op("pe", L("matmul", P[pi][:, :], lhsT=wgb[:, gi, kt, mt * 128:(mt + 1) * 128], rhs=pT[:, 2 * gi + kt, blk * 512:(blk + 1) * 512], start=(kt == 0), stop=(kt == 1)),
                                         reads=[Bwg, BpT[2 * gi], BpT[2 * gi + 1]], writes=[BP[pi]], signal=(kt == 1))
                            for mt in range(2):
                                pi = pis[mt]
                                jt = 2 * gi + mt
                                S.op("act", L("activation", out=zT[:, jt, blk * 512:(blk + 1) * 512], in_=P[pi][:, :], func=AF.Copy, scale=psc[:, jt:jt + 1]), reads=[BP[pi], Bwg], writes=[BpT[jt]])
                    def pool_sample(st2):
                        sp = sbt(st2, "sp", [NS, 15, 256], F32); Bsp = Buf()
                        ws = sbt(st2, "ws", [NS, D], F32); Bws = Buf()
                        for gi, w_ in enumerate(POOL_W):
                            cs = slice(gi * 256, (gi + 1) * 256)
                            S.dma(sp[:, 0:w_ - 1, :], st_pool[:, 15 - (w_ - 1):15, cs], writes=[Bsp])
                            S.op("dve", L("tensor_reduce", out=ws[:, cs], in_=sp[:, 0:w_ - 1, :].rearrange("p r c -> p c r"), axis=AX.X, op=ALU.add), reads=[Bsp], writes=[Bws])
                            S.op("dve", L("tensor_tensor", out=ws[:, cs], in0=ws[:, cs], in1=us[:, cs], op=ALU.add), reads=[Bws, Bus], writes=[Bws])
                            S.op("dve", L("scalar_tensor_tensor", out=ws[:, cs], in0=ws[:, cs], scalar=1.0 / w_, in1=us[:, cs], op0=ALU.mult, op1=ALU.subtract), reads=[Bws, Bus], writes=[Bws])
                        pTs, BpTs = to_T(st2, ws[:], Bws, 8, "pls")
                        zTs = sbt(st2, "zTs", [128, 8, NS], BF16); BzTs = Buf()
                        for gi in range(4):
                            for mt in range(2):
                                for kt in range(2):
                                    S.op("pe", L("matmul", P[4][:, 0:NS], lhsT=wgb[:, gi, kt, mt * 128:(mt + 1) * 128], rhs=pTs[:, 2 * gi + kt, :], start=(kt == 0), stop=(kt == 1)),
                                         reads=[Bwg, BpTs], writes=[BP[4]], signal=(kt == 1))
                                jt = 2 * gi + mt
                                S.op("act", L("activation", out=zTs[:, jt, :], in_=P[4][:, 0:NS], func=AF.Copy, scale=psc[:, jt:jt + 1]), reads=[BP[4], Bwg], writes=[BzTs])
                        S.dma(pools[:, 14, :], us[:], reads=[Bus])
                        return zTs, BzTs
                    proj_ln_phase(li, 0, zT, lambda t: BpT, 8, pwout, st, sample_fn=pool_sample)

        def sconv_phase(li):
            with ExitStack() as st0:
                yT = sbt(st0, "yT", [128, 8, T], BF16); ByT = Buf()
                gs3 = sbt(st0, "gs3", [NS, 8, 384], F32); Bgs3 = Buf()
                with ExitStack() as st:
                    xT, BxT = load_xT(st)
                    wst = sbt(st, "wst", [128, 8, 384], F32); Bwst = Buf()
                    wbf = [sbt(st, f"wbf{i}", [128, 8, 384], BF16) for i in range(2)]; Bwbf = [Buf(), Buf()]
                    skt = sbt(st, "skt", [128, 8, 3], F32); Bsk = Buf()
                    S.dma(skt[:], sk[:, :, :], writes=[Bsk])
                    pb2 = [sbt(st, f"pb2_{i}", [128, 514], F32) for i in range(2)]; Bpb2 = [Buf(), Buf()]
                    srow = sbt(st, "srow", [2, D], F32); Bsrow = Buf()
                    hb = [sbt(st, f"hb{i}", [128, 512], F32) for i in range(2)]; Bhb = [Buf(), Buf()]
                    tmp = [sbt(st, f"stmp{i}", [128, 512], F32) for i in range(2)]; Btmp = [Buf(), Buf()]
                    if with_sample:
                        xs_T, BxsT = load_xTs(st)
                    for j in range(8):
                        wi = j % 2
                        S.dma(wst[:], swin[j].rearrange("(k p) c -> p k c", p=128), writes=[Bwst])
                        cast(wbf[wi][:], wst[:], [Bwst], [Bwbf[wi]])
                        if with_sample:
                            for k in range(8):
                                S.op("pe", L("matmul", P[5][0:NS, 0:384], lhsT=xs_T[:, k, :], rhs=wbf[wi][:, k, :], start=(k == 0), stop=(k == 7)),
                                     reads=[Bwbf[wi], BxsT], writes=[BP[5]], signal=(k == 7))
                            S.op("act", L("copy", out=gs3[:, j, :], in_=P[5][0:NS, 0:384]), reads=[BP[5]], writes=[Bgs3])
                        for blk in range(8):
                            i = blk % 2
                            pis = (0, 1, 2) if i == 0 else (3, 4, 5)
                            for s_, pi in enumerate(pis):
                                for k in range(8):
                                    S.op("pe", L("matmul", P[pi][:, :], lhsT=wbf[wi][:, k, s_ * 128:(s_ + 1) * 128], rhs=xT[:, k, blk * 512:(blk + 1) * 512], start=(k == 0), stop=(k == 7)),
                                         reads=[Bwbf[wi]] + BxT[4 * blk:4 * blk + 4], writes=[BP[pi]], signal=(k == 7))
                            pb_ = pb2[i]
                            if blk == 0:
                                S.op("pool", L("memset", pb_[:, 0:2], 0.0), writes=[Bpb2[i]])
                            else:
                                S.op("pool", L("tensor_copy", out=pb_[:, 0:2], in_=pb2[1 - i][:, 512:514]), reads=[Bpb2[1 - i]], writes=[Bpb2[i]])
                            S.op("act", L("copy", out=hb[i][:], in_=P[pis[2]][:, :]), reads=[BP[pis[2]]], writes=[Bhb[i]])
                            S.op("dve", L("tensor_tensor", out=pb_[:, 2:514], in0=P[pis[1]][:, :], in1=hb[i][:], op=ALU.mult), reads=[BP[pis[1]], Bhb[i]], writes=[Bpb2[i]])
                            tm = tmp[i]
                            S.op("dve", L("tensor_scalar", out=tm[:], in0=pb_[:, 0:512], scalar1=skt[:, j, 0:1], scalar2=None, op0=ALU.mult), reads=[Bpb2[i], Bsk], writes=[Btmp[i]])
                            S.op("dve", L("scalar_tensor_tensor", out=tm[:], in0=pb_[:, 1:513], scalar=skt[:, j, 1:2], in1=tm[:], op0=ALU.mult, op1=ALU.add), reads=[Bpb2[i], Bsk, Btmp[i]], writes=[Btmp[i]])
                            S.op("dve", L("scalar_tensor_tensor", out=tm[:], in0=pb_[:, 2:514], scalar=skt[:, j, 2:3], in1=tm[:], op0=ALU.mult, op1=ALU.add), reads=[Bpb2[i], Bsk, Btmp[i]], writes=[Btmp[i]])
                            S.op("dve", L("tensor_tensor", out=yT[:, j, blk * 512:(blk + 1) * 512], in0=P[pis[0]][:, :], in1=tm[:], op=ALU.mult), reads=[BP[pis[0]], Btmp[i]], writes=[ByT])
                            if blk == 7:
                                fm_block_to_rows(pb_[:, 512:514], Bpb2[i], 2, srow, Bsrow, j)
                    S.dma(sconvp[:, :], srow[:], reads=[Bsrow])
                with ExitStack() as st:
                    def sconv_sample(st2):
                        sst = sbt(st2, "sst", [NS, 2, D], F32); Bsst = Buf()
                        S.dma(sst[:], st_sconv[:, :, :], writes=[Bsst])
                        ktm = sbt(st2, "sktm", [NS, 3, D], F32); Bktm = Buf()
                        for tap in range(3):
                            S.dma(ktm[:, tap, :], sk_tm[tap:tap + 1, :].partition_broadcast(NS), writes=[Bktm])
                        p_s = sbt(st2, "p_s", [NS, D], F32); t1 = sbt(st2, "t1", [NS, D], F32); t2 = sbt(st2, "t2", [NS, D], F32)
                        Bp_s, Bt1, Bt2 = Buf(), Buf(), Buf()
                        tt_ = lambda o, a, b, op, rd, wr: S.op("dve", L("tensor_tensor", out=o, in0=a, in1=b, op=op), reads=rd, writes=wr)
                        v3 = lambda t_: t_[:].rearrange("p (j c) -> p j c", j=8)
                        tt_(v3(p_s), gs3[:, :, 128:256], gs3[:, :, 256:384], ALU.mult, [Bgs3], [Bp_s])
                        tt_(t1[:], sst[:, 0, :], ktm[:, 0, :], ALU.mult, [Bsst, Bktm], [Bt1])
                        tt_(t2[:], sst[:, 1, :], ktm[:, 1, :], ALU.mult, [Bsst, Bktm], [Bt2])
                        tt_(t1[:], t1[:], t2[:], ALU.add, [Bt1, Bt2], [Bt1])
                        tt_(t2[:], p_s[:], ktm[:, 2, :], ALU.mult, [Bp_s, Bktm, Bt1], [Bt2])
                        tt_(t1[:], t1[:], t2[:], ALU.add, [Bt1, Bt2], [Bt1])
                        tt_(v3(t1), v3(t1), gs3[:, :, 0:128], ALU.mult, [Bt1, Bgs3], [Bt1])
                        S.dma(sconvs[:, 1, :], p_s[:], reads=[Bp_s])
                        return to_T(st2, t1[:], Bt1, 8, "scs")
                    proj_ln_phase(li, 0, yT, lambda t: [ByT], 8, swout, st, sample_fn=sconv_sample)

        on = lambda name: phases is None or name in phases
        if on("init"):
            init_phase()
        ia = 0
        for li in range(DEPTH):
            kind = li % 3
            if kind == 0:
                if on(f"mix{li}"):
                    attn_phase(li, ia)
                ia += 1
            elif kind == 1:
                if on(f"mix{li}"):
                    pool_phase(li)
            else:
                if on(f"mix{li}"):
                    sconv_phase(li)
            if on(f"ffn{li}"):
                ffn_phase(li)
        S.final_wait_dmas()
        print("ops recorded:", S.nops, {e: len(v) for e, v in S.ops.items()})
        S.replay()
    return nc


def _rope_tables():
    half = 32
    inv = (np.float32(10000.0) ** (-np.arange(half, dtype=np.float32) / np.float32(half))).astype(np.float32)
    out = {}
    for (_, d) in GROUPS:
        nblk = NT // d
        tab = np.zeros((128, NT, 2, 32), np.float32)
        for blk in range(NT):
            r, nb = blk // nblk, blk % nblk
            pos = (r + d * (128 * nb + np.arange(128))).astype(np.float32)
            ang = pos[:, None] * inv[None, :]
            tab[:, blk, 0, :] = np.cos(ang)
            tab[:, blk, 1, :] = np.sin(ang)
        out[d] = tab
    angs = (np.float32(PAST) * inv).astype(np.float32)
    rs = np.zeros((NS, 2, 32), np.float32)
    rs[:, 0, :] = np.cos(angs)[None]
    rs[:, 1, :] = np.sin(angs)[None]
    return out, rs


def _consts():
    j = np.arange(128)[:, None]
    c = np.arange(256)[None, :]
    band = np.where(c < 128, (c >= j), ((c - 128) <= j)).astype(np.float32)
    ident = np.eye(128, dtype=np.float32)
    prcnt = np.zeros((128, 8, 16), np.float32)
    for jt in range(8):
        w = POOL_W[jt // 2]
        cnt = np.minimum(w, np.arange(16) + 1).astype(np.float32)
        prcnt[:, jt, :] = (1.0 / cnt)[None, :] / np.float32(w) * np.float32(w) / 1.0
    return band, ident, prcnt


_NC_CACHE = {}


def kernel(x_prompt, x_sample, cache_kv_w128, cache_kv_w512, cache_kv_w2048,
           state_pool, state_sconv, state_ffn_conv,
           attn_w_qkv, attn_w_o, pool_w_in, pool_w_grp, pool_scale, pool_w_out,
           sconv_w_in, sconv_k, sconv_w_out, ffn_w_up, ffn_k, ffn_w_down, ln_g, ln_b):
    f = lambda a: np.ascontiguousarray(np.asarray(a, dtype=np.float32))
    if "nc" not in _NC_CACHE:
        _NC_CACHE["nc"] = build()
    nc = _NC_CACHE["nc"]
    ropes, rope_s = _rope_tables()
    band, ident, prcnt = _consts()
    wq = f(attn_w_qkv).reshape(2, D, 3, 3, 8, 2, 64)
    wqkv = np.ascontiguousarray(wq.transpose(0, 4, 2, 1, 3, 5, 6)).reshape(2, 8, 3, D, 384)
    wu = f(ffn_w_up).reshape(DEPTH, D, 2, NJ, 128)
    wup = np.ascontiguousarray(wu.transpose(0, 3, 1, 2, 4)).reshape(DEPTH, NJ, D, 256)
    ffk = np.ascontiguousarray(f(ffn_k).reshape(DEPTH, 3, NJ, 128).transpose(0, 3, 2, 1))
    sw = f(sconv_w_in)[0].reshape(D, 3, 8, 128)
    swin = np.ascontiguousarray(sw.transpose(2, 0, 1, 3)).reshape(8, D, 384)
    skf = np.ascontiguousarray(f(sconv_k)[0].reshape(3, 8, 128).transpose(2, 1, 0))
    psc = np.ascontiguousarray(f(pool_scale)[0].reshape(8, 128).T)
    common = {
        "wqkv": wqkv, "wo": f(attn_w_o), "wup": wup, "wdn": f(ffn_w_down), "ffk": ffk, "ffk_tm": f(ffn_k),
        "pwin": f(pool_w_in)[0], "pwgrp": f(pool_w_grp)[0], "pscale": psc, "pwout": f(pool_w_out)[0], "prcnt": prcnt,
        "swin": swin, "sk": skf, "sk_tm": f(sconv_k)[0], "swout": f(sconv_w_out)[0],
        "lng": f(ln_g), "lnb": f(ln_b), "rope_s": rope_s, "band": band, "ident": ident,
        "sel": np.ascontiguousarray(np.broadcast_to(np.eye(NS, dtype=np.float32)[:, :, None], (NS, NS, 128))),
        "oneh": np.ascontiguousarray(np.broadcast_to(np.eye(NS, dtype=np.float32)[None, :, :], (128, NS, NS))),
    }
    for (_, d) in GROUPS:
        common[f"rope{d}"] = ropes[d]
    caches = {128: f(cache_kv_w128), 512: f(cache_kv_w512), 2048: f(cache_kv_w2048)}
    in_maps = []
    for c in range(8):
        seq = c // 2
        ss = slice(NS * c, NS * (c + 1))
        m = dict(common)
        m["x"] = f(x_prompt[seq])
        m["xs"] = f(x_sample[ss, 0, :])
        for w in (128, 512, 2048):
            m[f"c{w}"] = np.ascontiguousarray(caches[w][:, ss].reshape(2, NS, w, 2048))
        m["st_pool"] = f(state_pool[0, ss])
        m["st_sconv"] = f(state_sconv[0, ss])
        m["st_ffn"] = f(state_ffn_conv[:, ss])
        in_maps.append(m)
    res = run_bass_kernel_spmd(nc, in_maps, core_ids=list(range(8)))
    R = res.results
    B = 4
    y_prompt = np.stack([R[2 * b]["y"] for b in range(B)])
    y_sample = np.concatenate([R[c]["ys"] for c in range(8)], 0).reshape(32, 1, D)
    outs = [y_prompt, y_sample]
    for (w, _) in GROUPS:
        kp = np.stack([R[2 * b][f"kv{w}p"] for b in range(B)], 1).reshape(2, B, w, 2, 16, 64)
        ks = np.concatenate([R[c][f"kv{w}s"] for c in range(8)], 1).reshape(2, 32, w, 2, 16, 64)
        outs += [kp, ks]
    outs.append(np.stack([R[2 * b]["poolp"] for b in range(B)])[None])
    outs.append(np.concatenate([R[c]["pools"] for c in range(8)], 0)[None])
    outs.append(np.stack([R[2 * b]["sconvp"] for b in range(B)])[None])
    outs.append(np.concatenate([R[c]["sconvs"] for c in range(8)], 0)[None])
    outs.append(np.stack([R[2 * b]["ffnp"] for b in range(B)], 1))
    outs.append(np.concatenate([R[c]["ffns"] for c in range(8)], 1))
    return tuple(np.ascontiguousarray(o, dtype=np.float32) for o in outs)
```

```python
import numpy as np
import concourse.bass as bass
import concourse.mybir as mybir
from concourse.bass_utils import run_bass_kernel_spmd

F32 = mybir.dt.float32
BF16 = mybir.dt.bfloat16
ALU = mybir.AluOpType
AF = mybir.ActivationFunctionType
AX = mybir.AxisListType


class Buf:
    __slots__ = ("name", "w", "r", "excl")

    def __init__(self, name="", excl=False):
        self.name = name
        self.w = None
        self.r = []
        self.excl = excl


class Ev:
    __slots__ = ("eng", "sem", "val", "resolved")

    def __init__(self, eng):
        self.eng = eng
        self.sem = None
        self.val = None
        self.resolved = False


def L(method, *args, **kw):
    return lambda e: getattr(e, method)(*args, **kw)


class Sched:
    ROLL = 30000
    NDMA = 24

    def __init__(self, nc, stack):
        self.nc = nc
        self.stack = stack
        self.engs = ["pe", "act", "dve", "pool", "sp"]
        self.ops = {e: [] for e in self.engs}
        self.cur_sem = {}
        self.cur_cnt = {}
        self.pending = {e: [] for e in self.engs}
        self.waited = {e: {} for e in self.engs}
        for e in self.engs:
            self._new_sem(e)
        self.dma_sems = [stack.enter_context(nc.semaphore(f"dq{i}")) for i in range(self.NDMA)]
        self.dma_cnt = [0] * self.NDMA
        self.dma_rr = 0
        self.nops = 0

    def _new_sem(self, e):
        self.cur_sem[e] = self.stack.enter_context(self.nc.semaphore(f"s_{e}_{len(self.ops[e])}"))
        self.cur_cnt[e] = 0

    def _collect(self, eng, reads, writes):
        evs = []
        for b in reads:
            if b.w is not None:
                evs.append(b.w)
            if b.excl:
                evs.extend(e for e in b.r if e.eng != eng)
        for b in writes:
            if b.w is not None:
                evs.append(b.w)
            evs.extend(b.r)
        waits = []
        for ev in evs:
            if ev.eng == eng and ev.eng == "pe":
                continue
            if not ev.resolved:
                if ev.eng == eng:
                    continue
                raise RuntimeError("dependency on unresolved (unsignaled) event")
            sid = id(ev.sem)
            if self.waited[eng].get(sid, -1) >= ev.val:
                continue
            self.waited[eng][sid] = ev.val
            waits.append((ev.sem, ev.val))
        return waits

    def _mark(self, ev, reads, writes):
        for b in reads:
            b.r.append(ev)
        for b in writes:
            b.w = ev
            b.r = []

    def op(self, eng, fn, reads=(), writes=(), signal=True):
        self.nops += 1
        waits = self._collect(eng, reads, writes)
        ev = Ev(eng)
        if signal:
            if self.cur_cnt[eng] >= self.ROLL:
                self._new_sem(eng)
            self.cur_cnt[eng] += 1
            ev.sem = self.cur_sem[eng]
            ev.val = self.cur_cnt[eng]
            ev.resolved = True
            for p in self.pending[eng]:
                p.sem, p.val, p.resolved = ev.sem, ev.val, True
            self.pending[eng] = []
            self.ops[eng].append((waits, fn, (ev.sem, 1)))
        else:
            self.pending[eng].append(ev)
            self.ops[eng].append((waits, fn, None))
        self._mark(ev, reads, writes)
        return ev

    def dma(self, out_ap, in_ap, reads=(), writes=(), eng="sp", **kw):
        self.nops += 1
        waits = self._collect(eng, reads, writes)
        s = self.dma_rr
        self.dma_rr = (self.dma_rr + 1) % self.NDMA
        sem = self.dma_sems[s]
        if self.dma_cnt[s] > 0:
            sid = id(sem)
            if self.waited[eng].get(sid, -1) < self.dma_cnt[s]:
                self.waited[eng][sid] = self.dma_cnt[s]
                waits.append((sem, self.dma_cnt[s]))
        self.dma_cnt[s] += 16
        ev = Ev("dma")
        ev.sem, ev.val, ev.resolved = sem, self.dma_cnt[s], True
        fn = L("dma_start", out=out_ap, in_=in_ap, **kw)
        self.ops[eng].append((waits, fn, (sem, 16)))
        self._mark(ev, reads, writes)
        return ev

    def wait_all(self, eng, evs):
        waits = []
        for ev in evs:
            waits.append((ev.sem, ev.val))
        self.ops[eng].append((waits, None, None))

    def final_wait_dmas(self, eng="sp"):
        waits = [(self.dma_sems[s], self.dma_cnt[s]) for s in range(self.NDMA) if self.dma_cnt[s] > 0]
        self.ops[eng].append((waits, None, None))

    def replay(self):
        nc = self.nc
        with nc.Block() as block:
            def run(e_name):
                def body(eng):
                    for waits, fn, sig in self.ops[e_name]:
                        for (sem, val) in waits:
                            eng.wait_ge(sem, val)
                        if fn is None:
                            continue
                        ins = fn(eng)
                        if sig is not None:
                            ins.then_inc(sig[0], sig[1])
                return body
            block.tensor(run("pe"))
            block.scalar(run("act"))
            block.vector(run("dve"))
            block.gpsimd(run("pool"))
            block.sync(run("sp"))

from contextlib import ExitStack

T = 4096
NT = 32
D = 1024
DFF = 2816
NJ = 22
DEPTH = 4
ALPHA = (2.0 * DEPTH) ** 0.25
LN_EPS = 1e-5
GROUPS = ((128, 1), (512, 4), (2048, 16))
POOL_W = (2, 4, 8, 16)
PAST = 8192
NS = 4
FCH = (12, 12, 8)
DBG = {}


def build(with_sample=True, phases=None):
    nc = bass.Bass("TRN2", target_bir_lowering=False)
    din = lambda n, shp: nc.dram_tensor(n, list(shp), F32, kind="ExternalInput").ap()
    dout = lambda n, shp: nc.dram_tensor(n, list(shp), F32, kind="ExternalOutput").ap()
    x_in = din("x", [T, D]); xs_in = din("xs", [NS, D])
    wqkv = din("wqkv", [2, 8, 3, D, 384]); wo = din("wo", [2, D, D])
    wup = din("wup", [DEPTH, NJ, D, 256]); wdn = din("wdn", [DEPTH, DFF, D]); ffk = din("ffk", [DEPTH, 128, NJ, 3])
    ffk_tm = din("ffk_tm", [DEPTH, 3, DFF])
    pwin = din("pwin", [D, D]); pwgrp = din("pwgrp", [4, 256, 256]); pscale = din("pscale", [128, 8]); pwout = din("pwout", [D, D])
    prcnt = din("prcnt", [128, 8, 16])
    swin = din("swin", [8, D, 384]); sk = din("sk", [128, 8, 3]); sk_tm = din("sk_tm", [3, D]); swout = din("swout", [D, D])
    lng = din("lng", [DEPTH, 2, D]); lnb = din("lnb", [DEPTH, 2, D])
    rope = [din(f"rope{d}", [128, NT, 2, 32]) for (_, d) in GROUPS]
    rope_s = din("rope_s", [NS, 2, 32])
    band_in = din("band", [128, 256]); ident_in = din("ident", [128, 128])
    sel_in = din("sel", [NS, NS, 128]); oneh_in = din("oneh", [128, NS, NS])
    c_in = [din(f"c{w}", [2, NS, w, 2048]) for (w, _) in GROUPS]
    st_pool = din("st_pool", [NS, 15, D]); st_sconv = din("st_sconv", [NS, 2, D]); st_ffn = din("st_ffn", [DEPTH, NS, 2, DFF])

    y_out = dout("y", [T, D]); ys_out = dout("ys", [NS, D])
    kvp = [dout(f"kv{w}p", [2, w, 2048]) for (w, _) in GROUPS]
    kvs = [dout(f"kv{w}s", [2, NS, w, 2048]) for (w, _) in GROUPS]
    poolp = dout("poolp", [15, D]); pools = dout("pools", [NS, 15, D])
    sconvp = dout("sconvp", [2, D]); sconvs = dout("sconvs", [NS, 2, D])
    ffnp = dout("ffnp", [DEPTH, 2, DFF]); ffns = dout("ffns", [DEPTH, NS, 2, DFF])

    xres = nc.dram_tensor("xres_scr", [T, D], F32).ap()
    xT_scr = nc.dram_tensor("xT_scr", [128, 8, T], BF16).ap()
    mixT_scr = nc.dram_tensor("mixT_scr", [128, 8, T], BF16).ap()
    xres_s = nc.dram_tensor("xres_s_scr", [NS, D], F32).ap()
    xTs_scr = nc.dram_tensor("xTs_scr", [128, 8, NS], BF16).ap()
    qkvs_scr = nc.dram_tensor("qkvs_scr", [NS, 3, 3, D], F32).ap()
    ab_scr = nc.dram_tensor("ab_scr", [NS, 2, DFF], F32).ap()

    top = ExitStack()
    with top:
        S = Sched(nc, top)
        P = [top.enter_context(nc.psum_tensor(f"P{i}", [128, 512], F32)) for i in range(6)]
        PB = [top.enter_context(nc.psum_tensor(f"PB{i}", [128, 1024], BF16)) for i in range(2)]
        BP = [Buf(f"P{i}", excl=True) for i in range(6)]
        BPB = [Buf(f"PB{i}", excl=True) for i in range(2)]
        uid = [0]
        def sbt(st, name, shape, dt):
            uid[0] += 1
            return st.enter_context(nc.sbuf_tensor(f"sb_{name}_{uid[0]}", list(shape), dt))
        idf = sbt(top, "idf", [128, 128], F32); idb = sbt(top, "idb", [128, 128], BF16)
        bandf = sbt(top, "bandf", [128, 256], F32); band = sbt(top, "band", [128, 256], BF16)
        onesb = sbt(top, "onesb", [128, 64], BF16)
        Bconst = Buf("const")
        S.dma(idf[:], ident_in[:, :], writes=[Bconst])
        S.dma(bandf[:], band_in[:, :], writes=[Bconst])
        S.op("dve", L("tensor_copy", out=idb[:], in_=idf[:]), reads=[Bconst], writes=[Bconst])
        S.op("dve", L("tensor_copy", out=band[:], in_=bandf[:]), reads=[Bconst], writes=[Bconst])
        S.op("dve", L("memset", onesb[:], 1.0), writes=[Bconst])
        mbias = sbt(top, "mbias", [128, 256], BF16)
        S.op("dve", L("tensor_scalar", out=bandf[:], in0=bandf[:], scalar1=-1.0, scalar2=30000.0, op0=ALU.add, op1=ALU.mult), reads=[Bconst], writes=[Bconst])
        S.op("dve", L("tensor_copy", out=mbias[:], in_=bandf[:]), reads=[Bconst], writes=[Bconst])
        Bxres = [Buf(f"xres{t}") for t in range(NT)]
        BxTs = [Buf(f"xTs{t}") for t in range(NT)]
        BmixT = [Buf(f"mixT{h}") for h in range(8)]
        Bxres_s = Buf("xres_s"); BxTs_s = Buf("xTs_s"); Bqkvs = Buf("qkvs"); Babs = Buf("abs")
        sel_t = sbt(top, "sel", [NS, NS, 128], F32); oneh_t = sbt(top, "oneh", [128, NS, NS], F32)
        S.dma(sel_t[:], sel_in[:, :, :], writes=[Bconst])
        S.dma(oneh_t[:], oneh_in[:, :, :], writes=[Bconst])

        def load_xTs(st):
            t_ = sbt(st, "xTs", [128, 8, NS], BF16); B_ = Buf()
            S.dma(t_[:], xTs_scr[:, :, :], reads=[BxTs_s], writes=[B_])
            return t_, B_

        def to_T(st, src, Bsrc, n, name):
            sbb = sbt(st, name + "_b", [NS, n * 128], BF16); Bb = Buf()
            S.op("act", L("copy", out=sbb[:], in_=src), reads=[Bsrc], writes=[Bb])
            dst = sbt(st, name + "_T", [128, n, NS], BF16); Bd = Buf()
            for k in range(n):
                S.op("pe", L("transpose", out=PB[0][:, k * NS:(k + 1) * NS], in_=sbb[:, k * 128:(k + 1) * 128], identity=idb[0:NS, 0:NS]),
                     reads=[Bb, Bconst], writes=[BPB[0]], signal=(k == n - 1))
            S.op("dve", L("tensor_copy", out=dst[:].rearrange("p n s -> p (n s)"), in_=PB[0][:, 0:n * NS]), reads=[BPB[0]], writes=[Bd])
            return dst, Bd

        def ln_sample(ln, li, which, actT_s, Bact_s, nk, w, Bw):
            for c, pi in enumerate((4, 5)):
                for k in range(nk):
                    S.op("pe", L("matmul", P[pi][0:NS, :], lhsT=actT_s[:, k, :], rhs=w[:, k, c * 512:(c + 1) * 512], start=(k == 0), stop=(k == nk - 1)),
                         reads=[Bw, Bact_s], writes=[BP[pi]], signal=(k == nk - 1))
            src = xs_in[:, :] if (li == 0 and which == 0) else xres_s[:, :]
            srcb = [] if (li == 0 and which == 0) else [Bxres_s]
            dst = ys_out[:, :] if (li == DEPTH - 1 and which == 1) else xres_s[:, :]
            ln.tile(P[4][0:NS, :], P[5][0:NS, :], BP[4], BP[5], src, srcb, dst, [Bxres_s], xTs_scr[:, :, :], [BxTs_s], np_=NS)
        rr = {"cast": 0}
        bg = []

        def cast(out_ap, in_ap, reads, writes):
            eng = ("pool", "act")[rr["cast"] % 2]
            rr["cast"] += 1
            if eng == "act":
                return S.op("act", L("copy", out=out_ap, in_=in_ap), reads=reads, writes=writes)
            return S.op("pool", L("tensor_copy", out=out_ap, in_=in_ap), reads=reads, writes=writes)

        def load_xT(st):
            xT = sbt(st, "xT", [128, 8, T], BF16)
            BxT = [Buf(f"xT{t}") for t in range(NT)]
            for q in range(8):
                S.dma(xT[:, :, q * 512:(q + 1) * 512], xT_scr[:, :, q * 512:(q + 1) * 512],
                      reads=BxTs[4 * q:4 * q + 4], writes=BxT[4 * q:4 * q + 4])
            return xT, BxT

        def fm_block_to_rows(src_ap, Bsrc, r, rowbuf, Brow, j):
            S.op("pe", L("matmul", P[5][0:r, 0:128], lhsT=src_ap, rhs=idf[:, :], start=True, stop=True), reads=[Bsrc, Bconst], writes=[BP[5]])
            S.op("dve", L("tensor_copy", out=rowbuf[0:r, j * 128:(j + 1) * 128], in_=P[5][0:r, 0:128]), reads=[BP[5]], writes=[Brow])

        class LN:
            def __init__(self, st, li, which):
                self.xr = [sbt(st, f"ln_xr{i}", [128, D], F32) for i in range(2)]
                self.xb = [sbt(st, f"ln_xb{i}", [128, D], BF16) for i in range(2)]
                self.stats = [sbt(st, f"ln_st{i}", [128, 2, 6], F32) for i in range(2)]
                self.mv = [sbt(st, f"ln_mv{i}", [128, 2], F32) for i in range(2)]
                self.rs = [sbt(st, f"ln_rs{i}", [128, 1], F32) for i in range(2)]
                self.xo = [sbt(st, f"ln_xo{i}", [128, 8, 128], BF16) for i in range(2)]
                self.gb = sbt(st, "ln_gb", [128, 2, D], F32)
                self.B = [[Buf() for _ in range(6)] for _ in range(2)]
                self.Bgb = Buf()
                S.dma(self.gb[:, 0, :], lng[li, which:which + 1, :].partition_broadcast(128), writes=[self.Bgb])
                S.dma(self.gb[:, 1, :], lnb[li, which:which + 1, :].partition_broadcast(128), writes=[self.Bgb])
                self.n = 0

            def tile(self, psA, psB, BpA, BpB, src_ap, src_bufs, dst_ap, dst_bufs, xT_dst_ap, xT_bufs, np_=128):
                i = self.n % 2
                self.n += 1
                xr, xb, stats, mv, rs, xo = self.xr[i], self.xb[i], self.stats[i], self.mv[i], self.rs[i], self.xo[i]
                Bx, Bb, Bs, Bm, Br, Bo = self.B[i]
                S.dma(xr[0:np_, :], src_ap, reads=src_bufs, writes=[Bx])
                for c, (ps, Bp) in enumerate(((psA, BpA), (psB, BpB))):
                    S.op("dve", L("scalar_tensor_tensor", out=xr[0:np_, c * 512:(c + 1) * 512], in0=xr[0:np_, c * 512:(c + 1) * 512], scalar=ALPHA, in1=ps, op0=ALU.mult, op1=ALU.add),
                         reads=[Bx, Bp], writes=[Bx])
                for c in range(2):
                    S.op("dve", L("bn_stats", out=stats[0:np_, c, :], in_=xr[0:np_, c * 512:(c + 1) * 512]), reads=[Bx], writes=[Bs])
                S.op("dve", L("bn_aggr", out=mv[0:np_, :], in_=stats[0:np_]), reads=[Bs], writes=[Bm])
                S.op("act", L("activation", out=rs[0:np_, :], in_=mv[0:np_, 1:2], func=AF.Sqrt, bias=LN_EPS, scale=1.0), reads=[Bm], writes=[Br])
                S.op("dve", L("reciprocal", out=rs[0:np_, :], in_=rs[0:np_, :]), reads=[Br], writes=[Br])
                S.op("dve", L("tensor_scalar", out=mv[0:np_, 1:2], in0=mv[0:np_, 0:1], scalar1=rs[0:np_, 0:1], scalar2=-1.0, op0=ALU.mult, op1=ALU.mult), reads=[Bm, Br], writes=[Bm])
                S.op("act", L("activation", out=xr[0:np_, :], in_=xr[0:np_, :], func=AF.Identity, bias=mv[0:np_, 1:2], scale=rs[0:np_, 0:1]), reads=[Bx, Bm, Br], writes=[Bx])
                S.op("dve", L("tensor_tensor", out=xr[0:np_, :], in0=xr[0:np_, :], in1=self.gb[0:np_, 0, :], op=ALU.mult), reads=[Bx, self.Bgb], writes=[Bx])
                S.op("pool", L("tensor_tensor", out=xr[0:np_, :], in0=xr[0:np_, :], in1=self.gb[0:np_, 1, :], op=ALU.add), reads=[Bx, self.Bgb], writes=[Bx])
                S.dma(dst_ap, xr[0:np_, :], reads=[Bx], writes=dst_bufs)
                S.op("act", L("copy", out=xb[0:np_, :], in_=xr[0:np_, :]), reads=[Bx], writes=[Bb])
                pb = PB[i]
                for k in range(8):
                    S.op("pe", L("transpose", out=pb[:, k * 128:k * 128 + np_], in_=xb[0:np_, k * 128:(k + 1) * 128], identity=idb[0:np_, 0:np_]),
                         reads=[Bb, Bconst], writes=[BPB[i]], signal=(k == 7))
                S.op("dve", L("tensor_copy", out=xo[:, :, 0:np_], in_=pb[:].rearrange("p (k n) -> p k n", k=8)[:, :, 0:np_]), reads=[BPB[i]], writes=[Bo])
                S.dma(xT_dst_ap, xo[:, :, 0:np_], reads=[Bo], writes=xT_bufs)

        def ln_dst(li, which, t):
            if li == DEPTH - 1 and which == 1:
                return y_out[t * 128:(t + 1) * 128, :]
            return xres[t * 128:(t + 1) * 128, :]

        def ln_src(li, which, t):
            if li == 0 and which == 0:
                return x_in[t * 128:(t + 1) * 128, :], []
            return xres[t * 128:(t + 1) * 128, :], [Bxres[t]]

        def load_wrow(w, Bw, src, k, stgs):
            for c in range(2):
                stg_, Bstg_ = stgs[c]
                S.dma(stg_[:, 0:512], src[k * 128:(k + 1) * 128, c * 512:(c + 1) * 512], writes=[Bstg_])
                cast(w[:, k, c * 512:(c + 1) * 512], stg_[:, 0:512], [Bstg_], [Bw])

        def load_wfull(st, name, src, nk, stg, Bstg, dbl=True):
            w = sbt(st, name, [128, nk, D], BF16)
            Bw = Buf(name)
            if dbl:
                stg2 = sbt(st, "wf_stg2", [128, 512], F32); Bstg2 = Buf()
                stgs = ((stg, Bstg), (stg2, Bstg2))
            else:
                stgs = ((stg, Bstg), (stg, Bstg))
            for k in range(nk):
                load_wrow(w, Bw, src, k, stgs)
            return w, Bw

        def proj_ln_phase(li, which, actT, BactT_fn, nk, w_src, st, sample_fn=None):
            stg = sbt(st, "wf_stg", [128, 512], F32); Bstg = Buf()
            w, Bw = load_wfull(st, "wfull", w_src, nk, stg, Bstg)
            ln = LN(st, li, which)
            def emit_mm(t):
                pa, pb_ = (0, 1) if t % 2 == 0 else (2, 3)
                for c, pi in enumerate((pa, pb_)):
                    for k in range(nk):
                        S.op("pe", L("matmul", P[pi][:, :], lhsT=actT[:, k, t * 128:(t + 1) * 128], rhs=w[:, k, c * 512:(c + 1) * 512], start=(k == 0), stop=(k == nk - 1)),
                             reads=[Bw] + BactT_fn(t), writes=[BP[pi]], signal=(k == nk - 1))
            emit_mm(0)
            for t in range(NT):
                pa, pb_ = (0, 1) if t % 2 == 0 else (2, 3)
                if t + 1 < NT:
                    emit_mm(t + 1)
                src, sb_ = ln_src(li, which, t)
                ln.tile(P[pa][:, :], P[pb_][:, :], BP[pa], BP[pb_], src, sb_, ln_dst(li, which, t), [Bxres[t]],
                        xT_scr[:, :, t * 128:(t + 1) * 128], [BxTs[t]])
            if sample_fn is not None and with_sample:
                aT, BaT = sample_fn(st)
                ln_sample(ln, li, which, aT, BaT, nk, w, Bw)

        def init_phase():
            with ExitStack() as st:
                xr = [sbt(st, f"i_xr{i}", [128, D], F32) for i in range(2)]
                xb = [sbt(st, f"i_xb{i}", [128, D], BF16) for i in range(2)]
                xo = [sbt(st, f"i_xo{i}", [128, 8, 128], BF16) for i in range(2)]
                Bs = [[Buf() for _ in range(3)] for _ in range(2)]
                for t in range(NT):
                    i = t % 2
                    S.dma(xr[i][:], x_in[t * 128:(t + 1) * 128, :], writes=[Bs[i][0]])
                    S.op("act", L("copy", out=xb[i][:], in_=xr[i][:]), reads=[Bs[i][0]], writes=[Bs[i][1]])
                    for k in range(8):
                        S.op("pe", L("transpose", out=PB[i][:, k * 128:(k + 1) * 128], in_=xb[i][:, k * 128:(k + 1) * 128], identity=idb[:]),
                             reads=[Bs[i][1], Bconst], writes=[BPB[i]], signal=(k == 7))
                    S.op("dve", L("tensor_copy", out=xo[i][:], in_=PB[i][:].rearrange("p (k n) -> p k n", k=8)), reads=[BPB[i]], writes=[Bs[i][2]])
                    S.dma(xT_scr[:, :, t * 128:(t + 1) * 128], xo[i][:], reads=[Bs[i][2]], writes=[BxTs[t]])
                if with_sample:
                    xs0 = sbt(st, "xs0", [NS, D], F32); Bxs0 = Buf()
                    S.dma(xs0[:], xs_in[:, :], writes=[Bxs0])
                    xsT, BxsT0 = to_T(st, xs0[:], Bxs0, 8, "xs0")
                    S.dma(xTs_scr[:, :, :], xsT[:], reads=[BxsT0], writes=[BxTs_s])
                    for gi, (win, d) in enumerate(GROUPS):
                        for ia_ in range(2):
                            for s_ in range(NS):
                                bg.append((kvs[gi][ia_, s_, 0:win - 1, :], c_in[gi][ia_, s_, 1:win, :]))
                    for s_ in range(NS):
                        S.dma(pools[s_, 0:14, :], st_pool[s_, 1:15, :])
                    S.dma(sconvs[:, 0, :], st_sconv[:, 1, :])
                    for l_ in range(DEPTH):
                        S.dma(ffns[l_, :, 0, :], st_ffn[l_, :, 1, :])

        def ffn_phase(li):
            with ExitStack() as st:
                wd = sbt(st, "wdn", [128, NJ, D], BF16); Bwd = Buf("wdn")
                wd_stgs = tuple((sbt(st, f"wf_stg{i}", [128, 512], F32), Buf()) for i in range(2))
                fk = sbt(st, "fk", [128, NJ, 3], F32); Bfk = Buf()
                S.dma(fk[:], ffk[li], writes=[Bfk])
                ah = sbt(st, "ah", [128, NJ, 2], F32); Bah = Buf()
                S.op("dve", L("memset", ah[:], 0.0), writes=[Bah])
                ln = LN(st, li, 1)
                if with_sample:
                    xs_T, BxsT = load_xTs(st)
                    abt = [sbt(st, f"abt{i}", [NS, 256], F32) for i in range(2)]; Babt = [Buf(), Buf()]
                sti = ExitStack()
                wst2 = [sbt(sti, f"wst{i}", [128, 8, 256], F32) for i in range(2)]; Bwst2 = [Buf(), Buf()]
                wbf = [sbt(sti, f"wbf{i}", [128, 8, 256], BF16) for i in range(2)]; Bwbf = [Buf(), Buf()]
                asb = [sbt(sti, f"asb{i}", [128, 514], F32) for i in range(2)]; Basb = [Buf(), Buf()]
                tmp = [sbt(sti, f"ftmp{i}", [128, 512], F32) for i in range(2)]; Btmp = [Buf(), Buf()]
                uu = [sbt(sti, f"fu{i}", [128, 512], F32) for i in range(2)]; Bu = [Buf(), Buf()]
                g = sbt(sti, "g", [128, NJ, 12 * 128], BF16)
                xTc = sbt(sti, "xTc", [128, 8, 12 * 128], BF16)
                t0 = 0
                it = 0
                BxTc_all = [Buf() for _ in range(12)]
                for ch in FCH:
                    ntok = ch * 128
                    Bg = [Buf() for _ in range(ch)]
                    BxTc = BxTc_all[:ch]
                    def load_xTc(t0_, ch_):
                        for q in range(ch_ // 4):
                            S.dma(xTc[:, :, q * 512:(q + 1) * 512], xT_scr[:, :, t0_ * 128 + q * 512:t0_ * 128 + (q + 1) * 512],
                                  reads=BxTs[t0_ + 4 * q:t0_ + 4 * q + 4], writes=BxTc_all[4 * q:4 * q + 4])
                    if t0 == 0:
                        load_xTc(0, ch)
                    FSTG = DBG.get("ffn_stage", 9)
                    def load_wup(j):
                        wi = j % 2
                        S.dma(wst2[wi][:], wup[li, j].rearrange("(k p) c -> p k c", p=128), writes=[Bwst2[wi]])
                        cast(wbf[wi][:], wst2[wi][:], [Bwst2[wi]], [Bwbf[wi]])
                    for j in range(NJ if FSTG >= 1 else 0):
                        wi = j % 2
                        if not (t0 > 0 and j < 2):
                            load_wup(j)
                        if t0 == 0:
                            load_wrow(wd, Bwd, wdn[li], j, wd_stgs)
                        if with_sample and t0 == 0:
                            for k in range(8):
                                S.op("pe", L("matmul", P[5][0:NS, 0:256], lhsT=xs_T[:, k, :], rhs=wbf[wi][:, k, :], start=(k == 0), stop=(k == 7)),
                                     reads=[Bwbf[wi], BxsT], writes=[BP[5]], signal=(k == 7))
                            S.op("act", L("copy", out=abt[wi][:], in_=P[5][0:NS, 0:256]), reads=[BP[5]], writes=[Babt[wi]])
                            S.dma(ab_scr[:, :, j * 128:(j + 1) * 128], abt[wi][:].rearrange("p (s c) -> p s c", s=2), reads=[Babt[wi]], writes=[Babs])
                        for b in range(ch // 4):
                            i = it % 2
                            it += 1
                            pa, pb_ = (0, 1) if i == 0 else (2, 3)
                            cols = slice(b * 512, (b + 1) * 512)
                            for half, pi in ((0, pa), (1, pb_)):
                                for k in range(8):
                                    S.op("pe", L("matmul", P[pi][:, :], lhsT=wbf[wi][:, k, half * 128:(half + 1) * 128], rhs=xTc[:, k, cols], start=(k == 0), stop=(k == 7)),
                                         reads=[Bwbf[wi]] + BxTc[4 * b:4 * b + 4], writes=[BP[pi]], signal=(k == 7))
                            a = asb[i]
                            S.op("dve", L("tensor_copy", out=a[:, 0:2], in_=ah[:, j, :]), reads=[Bah], writes=[Basb[i]])
                            S.op("act", L("copy", out=a[:, 2:514], in_=P[pa][:, :]), reads=[BP[pa]], writes=[Basb[i]])
                            S.op("dve", L("tensor_copy", out=ah[:, j, :], in_=a[:, 512:514]), reads=[Basb[i]], writes=[Bah])
                            tm = tmp[i]
                            S.op("dve", L("tensor_scalar", out=tm[:], in0=a[:, 0:512], scalar1=fk[:, j, 0:1], scalar2=None, op0=ALU.mult), reads=[Basb[i], Bfk], writes=[Btmp[i]])
                            S.op("dve", L("scalar_tensor_tensor", out=tm[:], in0=a[:, 1:513], scalar=fk[:, j, 1:2], in1=tm[:], op0=ALU.mult, op1=ALU.add), reads=[Basb[i], Bfk, Btmp[i]], writes=[Btmp[i]])
                            S.op("dve", L("scalar_tensor_tensor", out=tm[:], in0=a[:, 2:514], scalar=fk[:, j, 2:3], in1=tm[:], op0=ALU.mult, op1=ALU.add), reads=[Basb[i], Bfk, Btmp[i]], writes=[Btmp[i]])
                            u = uu[i]
                            S.op("act", L("activation", out=u[:], in_=tm[:], func=AF.Gelu), reads=[Btmp[i]], writes=[Bu[i]])
                            S.op("dve", L("tensor_tensor", out=g[:, j, cols], in0=P[pb_][:, :], in1=u[:], op=ALU.mult), reads=[Bu[i], BP[pb_]], writes=Bg[4 * b:4 * b + 4])
                    def emit_dn(tt):
                        pa, pb_ = (0, 1) if tt % 2 == 0 else (2, 3)
                        for c, pi in enumerate((pa, pb_)):
                            for j in range(NJ):
                                S.op("pe", L("matmul", P[pi][:, :], lhsT=g[:, j, tt * 128:(tt + 1) * 128], rhs=wd[:, j, c * 512:(c + 1) * 512], start=(j == 0), stop=(j == NJ - 1)),
                                     reads=[Bwd, Bg[tt]], writes=[BP[pi]], signal=(j == NJ - 1))
                    if t0 + ch < NT and FSTG >= 1:
                        load_xTc(t0 + ch, FCH[1] if t0 == 0 else FCH[2])
                        load_wup(0)
                        load_wup(1)
                    if FSTG >= 2:
                        emit_dn(0)
                    for tt in range(ch if FSTG >= 2 else 0):
                        t = t0 + tt
                        pa, pb_ = (0, 1) if tt % 2 == 0 else (2, 3)
                        if tt + 1 < ch:
                            emit_dn(tt + 1)
                        ln.tile(P[pa][:, :], P[pb_][:, :], BP[pa], BP[pb_], xres[t * 128:(t + 1) * 128, :], [Bxres[t]], ln_dst(li, 1, t), [Bxres[t]],
                                xT_scr[:, :, t * 128:(t + 1) * 128], [BxTs[t]])
                    t0 += ch
                sti.close()
                with ExitStack() as st3:
                    rowb = sbt(st3, "rowb", [2, DFF], F32); Browb = Buf()
                    for j in range(NJ if FSTG >= 3 else 0):
                        fm_block_to_rows(ah[:, j, :], Bah, 2, rowb, Browb, j)
                    if FSTG >= 3:
                        S.dma(ffnp[li, :, :], rowb[:], reads=[Browb])
                if with_sample:
                    with ExitStack() as st2:
                        a_b = sbt(st2, "ab_s", [NS, 2, DFF], F32); Bab = Buf()
                        S.dma(a_b[:], ab_scr[:, :, :], reads=[Babs], writes=[Bab])
                        stf = sbt(st2, "stf", [NS, 2, DFF], F32); Bstf = Buf()
                        S.dma(stf[:], st_ffn[li], writes=[Bstf])
                        ktm = sbt(st2, "ktm", [NS, 3, DFF], F32); Bktm = Buf()
                        for tap in range(3):
                            S.dma(ktm[:, tap, :], ffk_tm[li, tap:tap + 1, :].partition_broadcast(NS), writes=[Bktm])
                        t1 = sbt(st2, "t1", [NS, DFF], F32); t2 = sbt(st2, "t2", [NS, DFF], F32); Bt1, Bt2 = Buf(), Buf()
                        tt_ = lambda o, a, b, op, rd, wr: S.op("dve", L("tensor_tensor", out=o, in0=a, in1=b, op=op), reads=rd, writes=wr)
                        tt_(t1[:], stf[:, 0, :], ktm[:, 0, :], ALU.mult, [Bstf, Bktm], [Bt1])
                        tt_(t2[:], stf[:, 1, :], ktm[:, 1, :], ALU.mult, [Bstf, Bktm], [Bt2])
                        tt_(t1[:], t1[:], t2[:], ALU.add, [Bt1, Bt2], [Bt1])
                        tt_(t2[:], a_b[:, 0, :], ktm[:, 2, :], ALU.mult, [Bab, Bktm, Bt1], [Bt2])
                        tt_(t1[:], t1[:], t2[:], ALU.add, [Bt1, Bt2], [Bt1])
                        S.op("act", L("activation", out=t2[:], in_=t1[:], func=AF.Gelu), reads=[Bt1], writes=[Bt2])
                        tt_(t1[:], t2[:], a_b[:, 1, :], ALU.mult, [Bt2, Bab], [Bt1])
                        gsT, BgsT = to_T(st2, t1[:], Bt1, NJ, "gs")
                        ln_sample(ln, li, 1, gsT, BgsT, NJ, wd, Bwd)
                        S.dma(ffns[li, :, 1, :], a_b[:, 0, :], reads=[Bab])

        def attn_phase(li, ia):
            with ExitStack() as st:
                xT, BxT = load_xT(st)
                wst = sbt(st, "wst", [128, 4, 384], F32); Bwst = Buf()
                wbf = [sbt(st, f"wbf{i}", [128, 8, 384], BF16) for i in range(2)]; Bwbf = [Buf(), Buf()]
                rt0 = sbt(st, "rt0", [128, NT, 2, 32], F32); rt = [rt0, rt0]; Brt0 = Buf(); Brt = [Brt0, Brt0]
                QT = sbt(st, "QT", [128, NT, 128], BF16); KTt = sbt(st, "KT", [128, NT, 128], BF16)
                V = sbt(st, "V", [128, NT, 128], BF16)
                BQ = [Buf() for _ in range(NT)]; BK = [Buf() for _ in range(NT)]; BV = [Buf() for _ in range(NT)]
                accN = sbt(st, "accN", [128, T], F32); accD = sbt(st, "accD", [128, T], F32)
                Bacc = [Buf() for _ in range(8)]
                qkr = [sbt(st, f"qkr{i}", [128, 4, 256], F32) for i in range(2)]; Bqkr = [Buf(), Buf()]
                tb = [sbt(st, f"tb{i}", [128, 4, 4, 2, 32], F32) for i in range(2)]; Btb = [Buf(), Buf()]
                qkb = [sbt(st, f"qkb{i}", [128, 4, 256], BF16) for i in range(2)]; Bqkb = [Buf(), Buf()]
                raw = [sbt(st, f"raw{i}", [128, 4, 384], F32) for i in range(2)]; Braw = [Buf(), Buf()]
                Et = [sbt(st, f"E{i}", [128, 256], BF16) for i in range(4)]; BE = [Buf() for _ in range(4)]
                Em = [sbt(st, f"Em{i}", [128, 256], BF16) for i in range(6)]; BEm = [Buf() for _ in range(6)]
                mixT = [sbt(st, f"mixThp{i}", [128, 512], BF16) for i in range(2)]; Bmix = [Buf(), Buf()]
                xg = [sbt(st, f"xg{i}", [128, 8, 128], BF16) for i in range(4)]; Bxg = [Buf() for _ in range(4)]
                gcount = [0]
                if with_sample:
                    xs_T, BxsT = load_xTs(st)
                    rs_t = sbt(st, "rs_t", [NS, 2, 32], F32); Brs = Buf()
                    S.dma(rs_t[:], rope_s[:, :, :], writes=[Brs])
                    qsr = [sbt(st, f"qsr{i}", [NS, 384], F32) for i in range(2)]; Bqsr = [Buf(), Buf()]
                    tbs = [sbt(st, f"tbs{i}", [NS, 4, 2, 32], F32) for i in range(2)]; Btbs = [Buf(), Buf()]
                unit = 0
                for hp in range(DBG.get("hp_n", 8)):
                    for gi, (win, d) in enumerate(GROUPS):
                        if gi not in DBG.get("groups", (0, 1, 2)):
                            continue
                        nblk = NT // d
                        ui = unit % 2
                        unit += 1
                        if bg:
                            S.dma(*bg.pop(0))
                        wsrc = wqkv[ia, hp, gi].rearrange("(k p) c -> p k c", p=128)
                        for kh in range(2):
                            S.dma(wst[:], wsrc[:, kh * 4:(kh + 1) * 4, :], writes=[Bwst])
                            cast(wbf[ui][:, kh * 4:(kh + 1) * 4, :], wst[:], [Bwst], [Bwbf[ui]])
                        S.dma(rt[ui][:], rope[gi], writes=[Brt[ui]])
                        w = wbf[ui]
                        r_t = rt[ui]
                        if with_sample:
                            for k in range(8):
                                S.op("pe", L("matmul", P[5][0:NS, 0:384], lhsT=xs_T[:, k, :], rhs=w[:, k, :], start=(k == 0), stop=(k == 7)),
                                     reads=[Bwbf[ui], BxsT], writes=[BP[5]], signal=(k == 7))
                            qs = qsr[ui]; ts_ = tbs[ui]
                            ssrc = P[5][0:NS, 0:256].rearrange("p (j h f) -> p j h f", j=4, h=2)
                            scos = rs_t[:, 0, :].unsqueeze(1).unsqueeze(1).to_broadcast([NS, 4, 2, 32])
                            ssin = rs_t[:, 1, :].unsqueeze(1).to_broadcast([NS, 4, 32])
                            sdst = qs[:, 0:256].rearrange("p (j h f) -> p j h f", j=4, h=2)
                            S.op("dve", L("tensor_tensor", out=sdst, in0=ssrc, in1=scos, op=ALU.mult), reads=[BP[5], Brs], writes=[Bqsr[ui]])
                            S.op("dve", L("tensor_tensor", out=ts_[:, :, 0, :], in0=ssrc[:, :, 1, :], in1=ssin, op=ALU.mult), reads=[BP[5], Brs], writes=[Btbs[ui]])
                            S.op("dve", L("tensor_tensor", out=ts_[:, :, 1, :], in0=ssrc[:, :, 0, :], in1=ssin, op=ALU.mult), reads=[BP[5], Brs], writes=[Btbs[ui]])
                            S.op("act", L("copy", out=qs[:, 256:384], in_=P[5][0:NS, 256:384]), reads=[BP[5]], writes=[Bqsr[ui]])
                            S.op("dve", L("tensor_tensor", out=sdst[:, :, 0, :], in0=sdst[:, :, 0, :], in1=ts_[:, :, 0, :], op=ALU.subtract), reads=[Bqsr[ui], Btbs[ui]], writes=[Bqsr[ui]])
                            S.op("dve", L("tensor_tensor", out=sdst[:, :, 1, :], in0=sdst[:, :, 1, :], in1=ts_[:, :, 1, :], op=ALU.add), reads=[Bqsr[ui], Btbs[ui]], writes=[Bqsr[ui]])
                            S.dma(qkvs_scr[:, gi, :, hp * 128:(hp + 1) * 128], qs[:].rearrange("p (s c) -> p s c", s=3), reads=[Bqsr[ui]], writes=[Bqkvs])
                        PSTG = DBG.get("proj_stage", 9)

                        def proj_mm(bt):
                            for bi in range(4):
                                blk = bt * 4 + bi
                                r, nb = blk // nblk, blk % nblk
                                tok0 = r + d * 128 * nb
                                toks = slice(tok0, tok0 + d * 127 + 1, d)
                                tl = sorted(set([tok0 // 128 + x for x in range(0, (d * 127) // 128 + 1)]))
                                if d == 1:
                                    lsrc = lambda k, toks=toks: xT[:, k, toks]
                                    lreads = [BxT[x] for x in tl if x < NT]
                                else:
                                    gsl = gcount[0] % 4
                                    gcount[0] += 1
                                    xg_ = xg[gsl]
                                    S.op(("pool", "act")[gsl % 2], L(("tensor_copy", "copy")[gsl % 2], out=xg_[:], in_=xT[:, :, toks]),
                                         reads=[BxT[x] for x in tl if x < NT], writes=[Bxg[gsl]])
                                    lsrc = lambda k, xg_=xg_: xg_[:, k, :]
                                    lreads = [Bxg[gsl]]
                                for k in range(8):
                                    S.op("pe", L("matmul", P[bi][:, 0:384], lhsT=lsrc(k), rhs=w[:, k, :], start=(k == 0), stop=(k == 7)),
                                         reads=[Bwbf[ui]] + lreads, writes=[BP[bi]], signal=(k == 7))

                        def proj_post_a(bt):
                            i = bt % 2
                            q_r = qkr[i]; t_b = tb[i]; q_b = qkb[i]; rw = raw[i]
                            for bi in range(4):
                                S.op("act", L("copy", out=rw[:, bi, :], in_=P[bi][:, 0:384]), reads=[BP[bi]], writes=[Braw[i]])
                            rqk = rw[:, :, 0:256].rearrange("p b (j h f) -> p b j h f", j=4, h=2)
                            qd = q_r[:].rearrange("p b (j h f) -> p b j h f", j=4, h=2)
                            cosb = r_t[:, bt * 4:bt * 4 + 4, 0, :].unsqueeze(2).to_broadcast([128, 4, 4, 32])
                            sinb = r_t[:, bt * 4:bt * 4 + 4, 1, :].unsqueeze(2).to_broadcast([128, 4, 4, 32])
                            for h_ in range(2):
                                S.op("dve", L("tensor_tensor", out=qd[:, :, :, h_, :], in0=rqk[:, :, :, h_, :], in1=cosb, op=ALU.mult), reads=[Braw[i], Brt[ui]], writes=[Bqkr[i]])
                                S.op("dve", L("tensor_tensor", out=t_b[:, :, :, h_, :], in0=rqk[:, :, :, 1 - h_, :], in1=sinb, op=ALU.mult), reads=[Braw[i], Brt[ui]], writes=[Btb[i]])
                            S.op("pool", L("tensor_tensor", out=qd[:, :, :, 0, :], in0=qd[:, :, :, 0, :], in1=t_b[:, :, :, 0, :], op=ALU.subtract), reads=[Bqkr[i], Btb[i]], writes=[Bqkr[i]])
                            S.op("dve", L("tensor_tensor", out=qd[:, :, :, 1, :], in0=qd[:, :, :, 1, :], in1=t_b[:, :, :, 1, :], op=ALU.add), reads=[Bqkr[i], Btb[i]], writes=[Bqkr[i]])
                            S.op("act", L("copy", out=q_b[:], in_=q_r[:]), reads=[Bqkr[i]], writes=[Bqkb[i]])
                            S.op("pool", L("tensor_copy", out=V[:, bt * 4:bt * 4 + 4, :], in_=rw[:, :, 256:384]), reads=[Braw[i]], writes=BV[bt * 4:bt * 4 + 4])
                            for bi in range(4):
                                blk = bt * 4 + bi
                                r, nb = blk // nblk, blk % nblk
                                tok0 = r + d * 128 * nb
                                if tok0 >= T - win and DBG.get("kvout", True):
                                    row0 = tok0 - (T - win)
                                    rows = slice(row0, row0 + d * 127 + 1, d)
                                    S.dma(kvp[gi][ia, rows, hp * 128:(hp + 1) * 128], q_r[:, bi, 128:256], reads=[Bqkr[i]])
                                    S.dma(kvp[gi][ia, rows, 1024 + hp * 128:1024 + (hp + 1) * 128], rw[:, bi, 256:384], reads=[Braw[i]])

                        def proj_post_b(bt):
                            i = bt % 2
                            q_b = qkb[i]
                            pbk = PB[i]
                            for bi in range(4):
                                for s_ in range(2):
                                    S.op("pe", L("transpose", out=pbk[:, (bi * 2 + s_) * 128:(bi * 2 + s_ + 1) * 128], in_=q_b[:, bi, s_ * 128:(s_ + 1) * 128], identity=idb[:]),
                                         reads=[Bqkb[i], Bconst], writes=[BPB[i]], signal=(bi == 3 and s_ == 1))
                            pv = pbk[:].rearrange("p (b s n) -> p b s n", b=4, s=2)
                            S.op("dve", L("tensor_copy", out=QT[:, bt * 4:bt * 4 + 4, :], in_=pv[:, :, 0, :]), reads=[BPB[i]], writes=BQ[bt * 4:bt * 4 + 4])
                            S.op("act", L("copy", out=KTt[:, bt * 4:bt * 4 + 4, :], in_=pv[:, :, 1, :]), reads=[BPB[i]], writes=BK[bt * 4:bt * 4 + 4])

                        nbt = NT // 4 if PSTG >= 1 else 0
                        if nbt:
                            proj_mm(0)
                        for bt in range(nbt):
                            if PSTG >= 2:
                                proj_post_a(bt)
                            if bt + 1 < nbt:
                                proj_mm(bt + 1)
                            if PSTG >= 3:
                                proj_post_b(bt)

                        ecount = [0]
                        emidx = {}

                        def scores(blk):
                            r, nb = blk // nblk, blk % nblk
                            ncol = 256 if nb + 1 < nblk else 128
                            nqb = ncol // 128
                            for h in range(2):
                                pi = 4 + (ecount[0] % 2)
                                mi = ecount[0] % 6
                                ecount[0] += 1
                                hs = slice(64 * h, 64 * h + 64)
                                S.op("pe", L("matmul", P[pi][:, 0:ncol], lhsT=KTt[hs, blk, :], rhs=QT[hs, blk:blk + nqb, :], start=True, stop=False),
                                     reads=[BK[blk]] + BQ[blk:blk + nqb], writes=[BP[pi]], signal=False)
                                S.op("pe", L("matmul", P[pi][:, 0:ncol], lhsT=idb[:, :], rhs=mbias[:, 0:ncol], start=False, stop=True),
                                     reads=[Bconst], writes=[BP[pi]])
                                S.op("act", L("activation", out=Em[mi][:, 0:ncol], in_=P[pi][:, 0:ncol], func=AF.Exp, scale=0.125), reads=[BP[pi]], writes=[BEm[mi]])
                                emidx[(blk, h)] = mi

                        def pvstep(blk):
                            r, nb = blk // nblk, blk % nblk
                            qi = blk % 4
                            bnk = (blk // 4) % 2
                            pn, pd = (0, 1) if bnk == 0 else (2, 3)
                            for h in range(2):
                                hs = slice(64 * h, 64 * h + 64)
                                srcs = []
                                if nb >= 1:
                                    srcs.append((blk - 1, emidx[(blk - 1, h)], slice(128, 256)))
                                srcs.append((blk, emidx[(blk, h)], slice(0, 128)))
                                for (pp, lhs_fn) in ((pn, lambda kb, hs=hs: V[:, kb, hs]), (pd, lambda kb: onesb[:, :])):
                                    for si, (kb, mi, cs) in enumerate(srcs):
                                        S.op("pe", L("matmul", P[pp][hs, qi * 128:(qi + 1) * 128], lhsT=lhs_fn(kb), rhs=Em[mi][:, cs], start=(si == 0), stop=(si == len(srcs) - 1)),
                                             reads=[BV[kb], BEm[mi], Bconst], writes=[BP[pp]], signal=(si == len(srcs) - 1))
                            if qi == 3:
                                b0 = blk - 3
                                r0, nb0 = b0 // nblk, b0 % nblk
                                for (pp, acc) in ((pn, accN), (pd, accD)):
                                    accv = acc[:].rearrange("p (n r) -> p r n", r=d)
                                    if nblk >= 4:
                                        dst = accv[:, r0, 128 * nb0:128 * nb0 + 512]
                                        src = P[pp][:, :]
                                    else:
                                        dst = accv[:, r0:r0 + 2, 0:256]
                                        src = P[pp][:, :].rearrange("p (a n) -> p a n", a=2)
                                    if gi == 0:
                                        S.op("dve", L("tensor_copy", out=dst, in_=src), reads=[BP[pp]], writes=Bacc)
                                    else:
                                        S.op("dve", L("tensor_tensor", out=dst, in0=src, in1=dst, op=ALU.add), reads=[BP[pp]] + Bacc, writes=Bacc)

                        ncore = NT if DBG.get("core", True) else 0
                        if ncore:
                            scores(0)
                        for blk in range(ncore):
                            if blk + 1 < ncore:
                                scores(blk + 1)
                            pvstep(blk)
                    if not DBG.get("norm", True):
                        continue
                    for q in range(8):
                        cs = slice(q * 512, (q + 1) * 512)
                        S.op("dve", L("reciprocal", out=accD[:, cs], in_=accD[:, cs]), reads=Bacc, writes=Bacc)
                        S.op("dve", L("tensor_tensor", out=mixT[q % 2][:], in0=accN[:, cs], in1=accD[:, cs], op=ALU.mult), reads=Bacc, writes=[Bmix[q % 2]])
                        S.dma(mixT_scr[:, hp, cs], mixT[q % 2][:], reads=[Bmix[q % 2]], writes=[BmixT[hp]])
            if not DBG.get("wo", True):
                return
            with ExitStack() as st:
                mT = sbt(st, "mT", [128, 8, T], BF16); BmT = Buf()
                for hp in range(8):
                    S.dma(mT[:, hp, :], mixT_scr[:, hp, :], reads=[BmixT[hp]], writes=[BmT])
                def attn_sample(st2):
                    qk = sbt(st2, "qkvs", [NS, 3, 3, D], F32); Bqk = Buf()
                    S.dma(qk[:], qkvs_scr[:, :, :, :], reads=[Bqkvs], writes=[Bqk])
                    Kc = sbt(st2, "Kc", [128, D], F32); Vc = sbt(st2, "Vc", [128, D], F32); BKc, BVc = Buf(), Buf()
                    prod = sbt(st2, "prod", [128, D], F32); Bprod = Buf()
                    tmpv = sbt(st2, "tmpv", [128, D], F32); Btmpv = Buf()
                    Ssc = sbt(st2, "Ssc", [128, 16], F32); Es = sbt(st2, "Es", [128, 16], F32); BSs, BEs = Buf(), Buf()
                    pself = sbt(st2, "pself", [NS, D], F32); Bps = Buf()
                    sself = sbt(st2, "sself", [NS, 16], F32); eself = sbt(st2, "eself", [NS, 16], F32); Bss, Bes = Buf(), Buf()
                    first = True
                    for gi, (win, d) in enumerate(GROUPS):
                        for s_ in range(NS):
                            S.dma(Kc[:], c_in[gi][ia, s_, 0:win:d, 0:1024], writes=[BKc])
                            S.dma(Vc[:], c_in[gi][ia, s_, 0:win:d, 1024:2048], writes=[BVc])
                            for c in range(2):
                                S.op("pe", L("matmul", P[c][:, :], lhsT=sel_t[0:NS, s_, :], rhs=qk[:, gi, 0, c * 512:(c + 1) * 512], start=True, stop=True),
                                     reads=[Bqk, Bconst], writes=[BP[c]])
                                S.op("dve", L("tensor_tensor", out=prod[:, c * 512:(c + 1) * 512], in0=Kc[:, c * 512:(c + 1) * 512], in1=P[c][:, :], op=ALU.mult), reads=[BKc, BP[c]], writes=[Bprod])
                            S.op("dve", L("tensor_reduce", out=Ssc[:], in_=prod[:].rearrange("p (h f) -> p h f", h=16), axis=AX.X, op=ALU.add), reads=[Bprod], writes=[BSs])
                            S.op("act", L("activation", out=Es[:], in_=Ssc[:], func=AF.Exp, scale=0.125), reads=[BSs], writes=[BEs])
                            S.op("dve", L("tensor_tensor", out=tmpv[:].rearrange("p (h f) -> p h f", h=16), in0=Vc[:].rearrange("p (h f) -> p h f", h=16), in1=Es[:].unsqueeze(2).to_broadcast([128, 16, 64]), op=ALU.mult), reads=[BVc, BEs], writes=[Btmpv])
                            for c in range(2):
                                S.op("pe", L("matmul", P[2 + c][0:NS, :], lhsT=oneh_t[:, s_, :], rhs=tmpv[:, c * 512:(c + 1) * 512], start=first, stop=False),
                                     reads=[Btmpv, Bconst], writes=[BP[2 + c]])
                            S.op("pe", L("matmul", P[4][0:NS, 0:16], lhsT=oneh_t[:, s_, :], rhs=Es[:], start=first, stop=False), reads=[BEs, Bconst], writes=[BP[4]])
                            first = False
                        S.op("dve", L("tensor_tensor", out=pself[:], in0=qk[:, gi, 0, :], in1=qk[:, gi, 1, :], op=ALU.mult), reads=[Bqk], writes=[Bps])
                        S.op("dve", L("tensor_reduce", out=sself[:], in_=pself[:].rearrange("p (h f) -> p h f", h=16), axis=AX.X, op=ALU.add), reads=[Bps], writes=[Bss])
                        S.op("act", L("activation", out=eself[:], in_=sself[:], func=AF.Exp, scale=0.125), reads=[Bss], writes=[Bes])
                        S.op("dve", L("tensor_tensor", out=pself[:].rearrange("p (h f) -> p h f", h=16), in0=qk[:, gi, 2, :].rearrange("p (h f) -> p h f", h=16), in1=eself[:].unsqueeze(2).to_broadcast([NS, 16, 64]), op=ALU.mult), reads=[Bqk, Bes, Bss], writes=[Bps])
                        last = (gi == len(GROUPS) - 1)
                        for c in range(2):
                            S.op("pe", L("matmul", P[2 + c][0:NS, :], lhsT=idf[0:NS, 0:NS], rhs=pself[:, c * 512:(c + 1) * 512], start=False, stop=last), reads=[Bps, Bconst], writes=[BP[2 + c]])
                        S.op("pe", L("matmul", P[4][0:NS, 0:16], lhsT=idf[0:NS, 0:NS], rhs=eself[:], start=False, stop=last), reads=[Bes, Bconst], writes=[BP[4]])
                        S.dma(kvs[gi][ia, :, win - 1, 0:1024], qk[:, gi, 1, :], reads=[Bqk])
                        S.dma(kvs[gi][ia, :, win - 1, 1024:2048], qk[:, gi, 2, :], reads=[Bqk])
                    rden = sbt(st2, "rden", [NS, 16], F32); Brd = Buf()
                    mixs = sbt(st2, "mixs", [NS, D], F32); Bmx = Buf()
                    S.op("dve", L("reciprocal", out=rden[:], in_=P[4][0:NS, 0:16]), reads=[BP[4]], writes=[Brd])
                    for c in range(2):
                        S.op("dve", L("tensor_tensor", out=mixs[:, c * 512:(c + 1) * 512].rearrange("p (h f) -> p h f", h=8), in0=P[2 + c][0:NS, :].rearrange("p (h f) -> p h f", h=8), in1=rden[:, c * 8:(c + 1) * 8].unsqueeze(2).to_broadcast([NS, 8, 64]), op=ALU.mult), reads=[BP[2 + c], Brd], writes=[Bmx])
                    return to_T(st2, mixs[:], Bmx, 8, "mixs")
                proj_ln_phase(li, 0, mT, lambda t: [BmT], 8, wo[ia], st, sample_fn=attn_sample)

        def pool_phase(li):
            with ExitStack() as st0:
                pT = sbt(st0, "pooledT", [128, 8, T], BF16); BpT = [Buf() for _ in range(8)]
                us = sbt(st0, "us", [NS, D], F32); Bus = Buf()
                with ExitStack() as st:
                    xT, BxT = load_xT(st)
                    stg = sbt(st, "wf_stg", [128, 512], F32); Bstg = Buf()
                    w, Bw = load_wfull(st, "pwin", pwin, 8, stg, Bstg, dbl=False)
                    rc = sbt(st, "rc", [128, 8, 16], F32); Brc = Buf()
                    S.dma(rc[:], prcnt[:, :, :], writes=[Brc])
                    prow = sbt(st, "prow", [15, D], F32); Bprow = Buf()
                    ub = sbt(st, "ub", [128, 16 + T], F32); sA = sbt(st, "sA", [128, 16 + T], F32); sB = sbt(st, "sB", [128, 16 + T], F32)
                    Bub, BsA, BsB = Buf(), Buf(), Buf()
                    for (bu, Bb) in ((ub, Bub), (sA, BsA), (sB, BsB)):
                        S.op("pool", L("memset", bu[:, 0:16], 0.0), writes=[Bb])
                    if with_sample:
                        xs_T, BxsT = load_xTs(st)
                        for c in range(2):
                            for k in range(8):
                                S.op("pe", L("matmul", P[4 + c][0:NS, :], lhsT=xs_T[:, k, :], rhs=w[:, k, c * 512:(c + 1) * 512], start=(k == 0), stop=(k == 7)),
                                     reads=[Bw, BxsT], writes=[BP[4 + c]], signal=(k == 7))
                            S.op("act", L("copy", out=us[:, c * 512:(c + 1) * 512], in_=P[4 + c][0:NS, :]), reads=[BP[4 + c]], writes=[Bus])
                    for j in range(8):
                        for blk in range(8):
                            pi = blk % 4
                            for k in range(8):
                                S.op("pe", L("matmul", P[pi][:, :], lhsT=w[:, k, j * 128:(j + 1) * 128], rhs=xT[:, k, blk * 512:(blk + 1) * 512], start=(k == 0), stop=(k == 7)),
                                     reads=[Bw] + BxT[4 * blk:4 * blk + 4], writes=[BP[pi]], signal=(k == 7))
                            S.op("act", L("copy", out=ub[:, 16 + blk * 512:16 + (blk + 1) * 512], in_=P[pi][:, :]), reads=[BP[pi]], writes=[Bub])
                        wj = POOL_W[j // 2]
                        cur, Bcur = ub, Bub
                        step = 1
                        bufs = [(sA, BsA), (sB, BsB)]
                        bi = 0
                        while step < wj:
                            nxt, Bn = bufs[bi % 2]
                            bi += 1
                            for hh in range(2):
                                cs = slice(16 + hh * 2048, 16 + (hh + 1) * 2048)
                                cs2 = slice(16 - step + hh * 2048, 16 - step + (hh + 1) * 2048)
                                S.op(("dve", "pool")[hh], L("tensor_tensor", out=nxt[:, cs], in0=cur[:, cs], in1=cur[:, cs2], op=ALU.add), reads=[Bcur], writes=[Bn])
                            cur, Bcur = nxt, Bn
                            step *= 2
                        S.op("dve", L("tensor_tensor", out=cur[:, 16:32], in0=cur[:, 16:32], in1=rc[:, j, :], op=ALU.mult), reads=[Bcur, Brc], writes=[Bcur])
                        S.op("dve", L("tensor_scalar", out=cur[:, 16:32], in0=cur[:, 16:32], scalar1=float(wj), scalar2=None, op0=ALU.mult), reads=[Bcur], writes=[Bcur])
                        for hh in range(2):
                            cs = slice(16 + hh * 2048, 16 + (hh + 1) * 2048)
                            co = slice(hh * 2048, (hh + 1) * 2048)
                            S.op("dve", L("scalar_tensor_tensor", out=pT[:, j, co], in0=cur[:, cs], scalar=1.0 / wj, in1=ub[:, cs], op0=ALU.mult, op1=ALU.subtract), reads=[Bcur, Bub], writes=[BpT[j]])
                        fm_block_to_rows(ub[:, 16 + T - 15:16 + T], Bub, 15, prow, Bprow, j)
                    S.dma(poolp[:, :], prow[:], reads=[Bprow])
                with ExitStack() as st:
                    zT = pT
                    wgs = sbt(st, "wgs", [128, 4, 2, 256], F32); wgb = sbt(st, "wgb", [128, 4, 2, 256], BF16); Bwg = Buf()
                    psc = sbt(st, "psc", [128, 8], F32)
                    S.dma(wgs[:], pwgrp.rearrange("g (k p) c -> p g k c", p=128), writes=[Bwg])
                    S.dma(psc[:], pscale[:, :], writes=[Bwg])
                    S.op("dve", L("tensor_copy", out=wgb[:], in_=wgs[:]), reads=[Bwg], writes=[Bwg])
                    n = 0
                    for blk in range(8):
                        for gi in range(4):
                            pis = (0, 1) if n % 2 == 0 else (2, 3)
                            n += 1
                            for mt in range(2):
                                pi = pis[mt]
                                for kt in range(2):
                                    S.op("pe", L("matmul", P[pi][:, :], lhsT=wgb[:, gi, kt, mt * 128:(mt + 1) * 128], rhs=pT[:, 2 * gi + kt, blk * 512:(blk + 1) * 512], start=(kt == 0), stop=(kt == 1)),
                                         reads=[Bwg, BpT[2 * gi], BpT[2 * gi + 1]], writes=[BP[pi]], signal=(kt == 1))
                            for mt in range(2):
                                pi = pis[mt]
                                jt = 2 * gi + mt
                                S.op("act", L("activation", out=zT[:, jt, blk * 512:(blk + 1) * 512], in_=P[pi][:, :], func=AF.Copy, scale=psc[:, jt:jt + 1]), reads=[BP[pi], Bwg], writes=[BpT[jt]])
                    def pool_sample(st2):
                        sp = sbt(st2, "sp", [NS, 15, 256], F32); Bsp = Buf()
                        ws = sbt(st2, "ws", [NS, D], F32); Bws = Buf()
                        for gi, w_ in enumerate(POOL_W):
                            cs = slice(gi * 256, (gi + 1) * 256)
                            S.dma(sp[:, 0:w_ - 1, :], st_pool[:, 15 - (w_ - 1):15, cs], writes=[Bsp])
                            S.op("dve", L("tensor_reduce", out=ws[:, cs], in_=sp[:, 0:w_ - 1, :].rearrange("p r c -> p c r"), axis=AX.X, op=ALU.add), reads=[Bsp], writes=[Bws])
                            S.op("dve", L("tensor_tensor", out=ws[:, cs], in0=ws[:, cs], in1=us[:, cs], op=ALU.add), reads=[Bws, Bus], writes=[Bws])
                            S.op("dve", L("scalar_tensor_tensor", out=ws[:, cs], in0=ws[:, cs], scalar=1.0 / w_, in1=us[:, cs], op0=ALU.mult, op1=ALU.subtract), reads=[Bws, Bus], writes=[Bws])
                        pTs, BpTs = to_T(st2, ws[:], Bws, 8, "pls")
                        zTs = sbt(st2, "zTs", [128, 8, NS], BF16); BzTs = Buf()
                        for gi in range(4):
                            for mt in range(2):
                                for kt in range(2):
                                    S.op("pe", L("matmul", P[4][:, 0:NS], lhsT=wgb[:, gi, kt, mt * 128:(mt + 1) * 128], rhs=pTs[:, 2 * gi + kt, :], start=(kt == 0), stop=(kt == 1)),
                                         reads=[Bwg, BpTs], writes=[BP[4]], signal=(kt == 1))
                                jt = 2 * gi + mt
                                S.op("act", L("activation", out=zTs[:, jt, :], in_=P[4][:, 0:NS], func=AF.Copy, scale=psc[:, jt:jt + 1]), reads=[BP[4], Bwg], writes=[BzTs])
                        S.dma(pools[:, 14, :], us[:], reads=[Bus])
                        return zTs, BzTs
                    proj_ln_phase(li, 0, zT, lambda t: BpT, 8, pwout, st, sample_fn=pool_sample)

        def sconv_phase(li):
            with ExitStack() as st0:
                yT = sbt(st0, "yT", [128, 8, T], BF16); ByT = Buf()
                gs3 = sbt(st0, "gs3", [NS, 8, 384], F32); Bgs3 = Buf()
                with ExitStack() as st:
                    xT, BxT = load_xT(st)
                    wst = sbt(st, "wst", [128, 8, 384], F32); Bwst = Buf()
                    wbf = [sbt(st, f"wbf{i}", [128, 8, 384], BF16) for i in range(2)]; Bwbf = [Buf(), Buf()]
                    skt = sbt(st, "skt", [128, 8, 3], F32); Bsk = Buf()
                    S.dma(skt[:], sk[:, :, :], writes=[Bsk])
                    pb2 = [sbt(st, f"pb2_{i}", [128, 514], F32) for i in range(2)]; Bpb2 = [Buf(), Buf()]
                    srow = sbt(st, "srow", [2, D], F32); Bsrow = Buf()
                    hb = [sbt(st, f"hb{i}", [128, 512], F32) for i in range(2)]; Bhb = [Buf(), Buf()]
                    tmp = [sbt(st, f"stmp{i}", [128, 512], F32) for i in range(2)]; Btmp = [Buf(), Buf()]
                    if with_sample:
                        xs_T, BxsT = load_xTs(st)
                    for j in range(8):
                        wi = j % 2
                        S.dma(wst[:], swin[j].rearrange("(k p) c -> p k c", p=128), writes=[Bwst])
                        cast(wbf[wi][:], wst[:], [Bwst], [Bwbf[wi]])
                        if with_sample:
                            for k in range(8):
                                S.op("pe", L("matmul", P[5][0:NS, 0:384], lhsT=xs_T[:, k, :], rhs=wbf[wi][:, k, :], start=(k == 0), stop=(k == 7)),
                                     reads=[Bwbf[wi], BxsT], writes=[BP[5]], signal=(k == 7))
                            S.op("act", L("copy", out=gs3[:, j, :], in_=P[5][0:NS, 0:384]), reads=[BP[5]], writes=[Bgs3])
                        for blk in range(8):
                            i = blk % 2
                            pis = (0, 1, 2) if i == 0 else (3, 4, 5)
                            for s_, pi in enumerate(pis):
                                for k in range(8):
                                    S.op("pe", L("matmul", P[pi][:, :], lhsT=wbf[wi][:, k, s_ * 128:(s_ + 1) * 128], rhs=xT[:, k, blk * 512:(blk + 1) * 512], start=(k == 0), stop=(k == 7)),
                                         reads=[Bwbf[wi]] + BxT[4 * blk:4 * blk + 4], writes=[BP[pi]], signal=(k == 7))
                            pb_ = pb2[i]
                            if blk == 0:
                                S.op("pool", L("memset", pb_[:, 0:2], 0.0), writes=[Bpb2[i]])
                            else:
                                S.op("pool", L("tensor_copy", out=pb_[:, 0:2], in_=pb2[1 - i][:, 512:514]), reads=[Bpb2[1 - i]], writes=[Bpb2[i]])
                            S.op("act", L("copy", out=hb[i][:], in_=P[pis[2]][:, :]), reads=[BP[pis[2]]], writes=[Bhb[i]])
                            S.op("dve", L("tensor_tensor", out=pb_[:, 2:514], in0=P[pis[1]][:, :], in1=hb[i][:], op=ALU.mult), reads=[BP[pis[1]], Bhb[i]], writes=[Bpb2[i]])
                            tm = tmp[i]
                            S.op("dve", L("tensor_scalar", out=tm[:], in0=pb_[:, 0:512], scalar1=skt[:, j, 0:1], scalar2=None, op0=ALU.mult), reads=[Bpb2[i], Bsk], writes=[Btmp[i]])
                            S.op("dve", L("scalar_tensor_tensor", out=tm[:], in0=pb_[:, 1:513], scalar=skt[:, j, 1:2], in1=tm[:], op0=ALU.mult, op1=ALU.add), reads=[Bpb2[i], Bsk, Btmp[i]], writes=[Btmp[i]])
                            S.op("dve", L("scalar_tensor_tensor", out=tm[:], in0=pb_[:, 2:514], scalar=skt[:, j, 2:3], in1=tm[:], op0=ALU.mult, op1=ALU.add), reads=[Bpb2[i], Bsk, Btmp[i]], writes=[Btmp[i]])
                            S.op("dve", L("tensor_tensor", out=yT[:, j, blk * 512:(blk + 1) * 512], in0=P[pis[0]][:, :], in1=tm[:], op=ALU.mult), reads=[BP[pis[0]], Btmp[i]], writes=[ByT])
                            if blk == 7:
                                fm_block_to_rows(pb_[:, 512:514], Bpb2[i], 2, srow, Bsrow, j)
                    S.dma(sconvp[:, :], srow[:], reads=[Bsrow])
                with ExitStack() as st:
                    def sconv_sample(st2):
                        sst = sbt(st2, "sst", [NS, 2, D], F32); Bsst = Buf()
                        S.dma(sst[:], st_sconv[:, :, :], writes=[Bsst])
                        ktm = sbt(st2, "sktm", [NS, 3, D], F32); Bktm = Buf()
                        for tap in range(3):
                            S.dma(ktm[:, tap, :], sk_tm[tap:tap + 1, :].partition_broadcast(NS), writes=[Bktm])
                        p_s = sbt(st2, "p_s", [NS, D], F32); t1 = sbt(st2, "t1", [NS, D], F32); t2 = sbt(st2, "t2", [NS, D], F32)
                        Bp_s, Bt1, Bt2 = Buf(), Buf(), Buf()
                        tt_ = lambda o, a, b, op, rd, wr: S.op("dve", L("tensor_tensor", out=o, in0=a, in1=b, op=op), reads=rd, writes=wr)
                        v3 = lambda t_: t_[:].rearrange("p (j c) -> p j c", j=8)
                        tt_(v3(p_s), gs3[:, :, 128:256], gs3[:, :, 256:384], ALU.mult, [Bgs3], [Bp_s])
                        tt_(t1[:], sst[:, 0, :], ktm[:, 0, :], ALU.mult, [Bsst, Bktm], [Bt1])
                        tt_(t2[:], sst[:, 1, :], ktm[:, 1, :], ALU.mult, [Bsst, Bktm], [Bt2])
                        tt_(t1[:], t1[:], t2[:], ALU.add, [Bt1, Bt2], [Bt1])
                        tt_(t2[:], p_s[:], ktm[:, 2, :], ALU.mult, [Bp_s, Bktm, Bt1], [Bt2])
                        tt_(t1[:], t1[:], t2[:], ALU.add, [Bt1, Bt2], [Bt1])
                        tt_(v3(t1), v3(t1), gs3[:, :, 0:128], ALU.mult, [Bt1, Bgs3], [Bt1])
                        S.dma(sconvs[:, 1, :], p_s[:], reads=[Bp_s])
                        return to_T(st2, t1[:], Bt1, 8, "scs")
                    proj_ln_phase(li, 0, yT, lambda t: [ByT], 8, swout, st, sample_fn=sconv_sample)

        on = lambda name: phases is None or name in phases
        if on("init"):
            init_phase()
        ia = 0
        for li in range(DEPTH):
            kind = li % 3
            if kind == 0:
                if on(f"mix{li}"):
                    attn_phase(li, ia)
                ia += 1
            elif kind == 1:
                if on(f"mix{li}"):
                    pool_phase(li)
            else:
                if on(f"mix{li}"):
                    sconv_phase(li)
            if on(f"ffn{li}"):
                ffn_phase(li)
        while bg:
            S.dma(*bg.pop(0))
        S.final_wait_dmas()
        print("ops recorded:", S.nops, {e: len(v) for e, v in S.ops.items()})
        S.replay()
    return nc


def _rope_tables():
    half = 32
    inv = (np.float32(10000.0) ** (-np.arange(half, dtype=np.float32) / np.float32(half))).astype(np.float32)
    out = {}
    for (_, d) in GROUPS:
        nblk = NT // d
        tab = np.zeros((128, NT, 2, 32), np.float32)
        for blk in range(NT):
            r, nb = blk // nblk, blk % nblk
            pos = (r + d * (128 * nb + np.arange(128))).astype(np.float32)
            ang = pos[:, None] * inv[None, :]
            tab[:, blk, 0, :] = np.cos(ang)
            tab[:, blk, 1, :] = np.sin(ang)
        out[d] = tab
    angs = (np.float32(PAST) * inv).astype(np.float32)
    rs = np.zeros((NS, 2, 32), np.float32)
    rs[:, 0, :] = np.cos(angs)[None]
    rs[:, 1, :] = np.sin(angs)[None]
    return out, rs


def _consts():
    j = np.arange(128)[:, None]
    c = np.arange(256)[None, :]
    band = np.where(c < 128, (c >= j), ((c - 128) <= j)).astype(np.float32)
    ident = np.eye(128, dtype=np.float32)
    prcnt = np.zeros((128, 8, 16), np.float32)
    for jt in range(8):
        w = POOL_W[jt // 2]
        cnt = np.minimum(w, np.arange(16) + 1).astype(np.float32)
        prcnt[:, jt, :] = (1.0 / cnt)[None, :] / np.float32(w) * np.float32(w) / 1.0
    return band, ident, prcnt


_NC_CACHE = {}


def kernel(x_prompt, x_sample, cache_kv_w128, cache_kv_w512, cache_kv_w2048,
           state_pool, state_sconv, state_ffn_conv,
           attn_w_qkv, attn_w_o, pool_w_in, pool_w_grp, pool_scale, pool_w_out,
           sconv_w_in, sconv_k, sconv_w_out, ffn_w_up, ffn_k, ffn_w_down, ln_g, ln_b):
    f = lambda a: np.ascontiguousarray(np.asarray(a, dtype=np.float32))
    if "nc" not in _NC_CACHE:
        _NC_CACHE["nc"] = build()
    nc = _NC_CACHE["nc"]
    ropes, rope_s = _rope_tables()
    band, ident, prcnt = _consts()
    wq = f(attn_w_qkv).reshape(2, D, 3, 3, 8, 2, 64)
    wqkv = np.ascontiguousarray(wq.transpose(0, 4, 2, 1, 3, 5, 6)).reshape(2, 8, 3, D, 384)
    wu = f(ffn_w_up).reshape(DEPTH, D, 2, NJ, 128)
    wup = np.ascontiguousarray(wu.transpose(0, 3, 1, 2, 4)).reshape(DEPTH, NJ, D, 256)
    ffk = np.ascontiguousarray(f(ffn_k).reshape(DEPTH, 3, NJ, 128).transpose(0, 3, 2, 1))
    sw = f(sconv_w_in)[0].reshape(D, 3, 8, 128)
    swin = np.ascontiguousarray(sw.transpose(2, 0, 1, 3)).reshape(8, D, 384)
    skf = np.ascontiguousarray(f(sconv_k)[0].reshape(3, 8, 128).transpose(2, 1, 0))
    psc = np.ascontiguousarray(f(pool_scale)[0].reshape(8, 128).T)
    common = {
        "wqkv": wqkv, "wo": f(attn_w_o), "wup": wup, "wdn": f(ffn_w_down), "ffk": ffk, "ffk_tm": f(ffn_k),
        "pwin": f(pool_w_in)[0], "pwgrp": f(pool_w_grp)[0], "pscale": psc, "pwout": f(pool_w_out)[0], "prcnt": prcnt,
        "swin": swin, "sk": skf, "sk_tm": f(sconv_k)[0], "swout": f(sconv_w_out)[0],
        "lng": f(ln_g), "lnb": f(ln_b), "rope_s": rope_s, "band": band, "ident": ident,
        "sel": np.ascontiguousarray(np.broadcast_to(np.eye(NS, dtype=np.float32)[:, :, None], (NS, NS, 128))),
        "oneh": np.ascontiguousarray(np.broadcast_to(np.eye(NS, dtype=np.float32)[None, :, :], (128, NS, NS))),
    }
    for (_, d) in GROUPS:
        common[f"rope{d}"] = ropes[d]
    caches = {128: f(cache_kv_w128), 512: f(cache_kv_w512), 2048: f(cache_kv_w2048)}
    in_maps = []
    for c in range(8):
        seq = c // 2
        ss = slice(NS * c, NS * (c + 1))
        m = dict(common)
        m["x"] = f(x_prompt[seq])
        m["xs"] = f(x_sample[ss, 0, :])
        for w in (128, 512, 2048):
            m[f"c{w}"] = np.ascontiguousarray(caches[w][:, ss].reshape(2, NS, w, 2048))
        m["st_pool"] = f(state_pool[0, ss])
        m["st_sconv"] = f(state_sconv[0, ss])
        m["st_ffn"] = f(state_ffn_conv[:, ss])
        in_maps.append(m)
    res = run_bass_kernel_spmd(nc, in_maps, core_ids=list(range(8)))
    R = res.results
    B = 4
    y_prompt = np.stack([R[2 * b]["y"] for b in range(B)])
    y_sample = np.concatenate([R[c]["ys"] for c in range(8)], 0).reshape(32, 1, D)
    outs = [y_prompt, y_sample]
    for (w, _) in GROUPS:
        kp = np.stack([R[2 * b][f"kv{w}p"] for b in range(B)], 1).reshape(2, B, w, 2, 16, 64)
        ks = np.concatenate([R[c][f"kv{w}s"] for c in range(8)], 1).reshape(2, 32, w, 2, 16, 64)
        outs += [kp, ks]
    outs.append(np.stack([R[2 * b]["poolp"] for b in range(B)])[None])
    outs.append(np.concatenate([R[c]["pools"] for c in range(8)], 0)[None])
    outs.append(np.stack([R[2 * b]["sconvp"] for b in range(B)])[None])
    outs.append(np.concatenate([R[c]["sconvs"] for c in range(8)], 0)[None])
    outs.append(np.stack([R[2 * b]["ffnp"] for b in range(B)], 1))
    outs.append(np.concatenate([R[c]["ffns"] for c in range(8)], 1))
    return tuple(np.ascontiguousarray(o, dtype=np.float32) for o in outs)
```

```python
import numpy as np
import concourse.bass as bass
import concourse.mybir as mybir
from concourse.bass_utils import run_bass_kernel_spmd

F32 = mybir.dt.float32
BF16 = mybir.dt.bfloat16
ALU = mybir.AluOpType
AF = mybir.ActivationFunctionType
AX = mybir.AxisListType


class Buf:
    __slots__ = ("name", "w", "r", "excl")

    def __init__(self, name="", excl=False):
        self.name = name
        self.w = None
        self.r = []
        self.excl = excl


class Ev:
    __slots__ = ("eng", "sem", "val", "resolved")

    def __init__(self, eng):
        self.eng = eng
        self.sem = None
        self.val = None
        self.resolved = False


def L(method, *args, **kw):
    return lambda e: getattr(e, method)(*args, **kw)


class Sched:
    ROLL = 30000
    NDMA = 24

    def __init__(self, nc, stack):
        self.nc = nc
        self.stack = stack
        self.engs = ["pe", "act", "dve", "pool", "sp"]
        self.ops = {e: [] for e in self.engs}
        self.cur_sem = {}
        self.cur_cnt = {}
        self.pending = {e: [] for e in self.engs}
        self.waited = {e: {} for e in self.engs}
        for e in self.engs:
            self._new_sem(e)
        self.dma_sems = [stack.enter_context(nc.semaphore(f"dq{i}")) for i in range(self.NDMA)]
        self.dma_cnt = [0] * self.NDMA
        self.dma_rr = 0
        self.nops = 0

    def _new_sem(self, e):
        self.cur_sem[e] = self.stack.enter_context(self.nc.semaphore(f"s_{e}_{len(self.ops[e])}"))
        self.cur_cnt[e] = 0

    def _collect(self, eng, reads, writes):
        evs = []
        for b in reads:
            if b.w is not None:
                evs.append(b.w)
            if b.excl:
                evs.extend(e for e in b.r if e.eng != eng)
        for b in writes:
            if b.w is not None:
                evs.append(b.w)
            evs.extend(b.r)
        waits = []
        for ev in evs:
            if ev.eng == eng and ev.eng == "pe":
                continue
            if not ev.resolved:
                if ev.eng == eng:
                    continue
                raise RuntimeError("dependency on unresolved (unsignaled) event")
            sid = id(ev.sem)
            if self.waited[eng].get(sid, -1) >= ev.val:
                continue
            self.waited[eng][sid] = ev.val
            waits.append((ev.sem, ev.val))
        return waits

    def _mark(self, ev, reads, writes):
        for b in reads:
            b.r.append(ev)
        for b in writes:
            b.w = ev
            b.r = []

    def op(self, eng, fn, reads=(), writes=(), signal=True):
        self.nops += 1
        waits = self._collect(eng, reads, writes)
        ev = Ev(eng)
        if signal:
            if self.cur_cnt[eng] >= self.ROLL:
                self._new_sem(eng)
            self.cur_cnt[eng] += 1
            ev.sem = self.cur_sem[eng]
            ev.val = self.cur_cnt[eng]
            ev.resolved = True
            for p in self.pending[eng]:
                p.sem, p.val, p.resolved = ev.sem, ev.val, True
            self.pending[eng] = []
            self.ops[eng].append((waits, fn, (ev.sem, 1)))
        else:
            self.pending[eng].append(ev)
            self.ops[eng].append((waits, fn, None))
        self._mark(ev, reads, writes)
        return ev

    def dma(self, out_ap, in_ap, reads=(), writes=(), eng="sp", **kw):
        self.nops += 1
        waits = self._collect(eng, reads, writes)
        s = self.dma_rr
        self.dma_rr = (self.dma_rr + 1) % self.NDMA
        sem = self.dma_sems[s]
        if self.dma_cnt[s] > 0:
            sid = id(sem)
            if self.waited[eng].get(sid, -1) < self.dma_cnt[s]:
                self.waited[eng][sid] = self.dma_cnt[s]
                waits.append((sem, self.dma_cnt[s]))
        self.dma_cnt[s] += 16
        ev = Ev("dma")
        ev.sem, ev.val, ev.resolved = sem, self.dma_cnt[s], True
        fn = L("dma_start", out=out_ap, in_=in_ap, **kw)
        self.ops[eng].append((waits, fn, (sem, 16)))
        self._mark(ev, reads, writes)
        return ev

    def wait_all(self, eng, evs):
        waits = []
        for ev in evs:
            waits.append((ev.sem, ev.val))
        self.ops[eng].append((waits, None, None))

    def final_wait_dmas(self, eng="sp"):
        waits = [(self.dma_sems[s], self.dma_cnt[s]) for s in range(self.NDMA) if self.dma_cnt[s] > 0]
        self.ops[eng].append((waits, None, None))

    def replay(self):
        nc = self.nc
        with nc.Block() as block:
            def run(e_name):
                def body(eng):
                    for waits, fn, sig in self.ops[e_name]:
                        for (sem, val) in waits:
                            eng.wait_ge(sem, val)
                        if fn is None:
                            continue
                        ins = fn(eng)
                        if sig is not None:
                            ins.then_inc(sig[0], sig[1])
                return body
            block.tensor(run("pe"))
            block.scalar(run("act"))
            block.vector(run("dve"))
            block.gpsimd(run("pool"))
            block.sync(run("sp"))

from contextlib import ExitStack

T = 4096
NT = 32
D = 1024
DFF = 2816
NJ = 22
DEPTH = 4
ALPHA = (2.0 * DEPTH) ** 0.25
LN_EPS = 1e-5
GROUPS = ((128, 1), (512, 4), (2048, 16))
POOL_W = (2, 4, 8, 16)
PAST = 8192
NS = 4
FCH = (12, 12, 8)
DBG = {}


def build(with_sample=True, phases=None):
    nc = bass.Bass("TRN2", target_bir_lowering=False)
    din = lambda n, shp: nc.dram_tensor(n, list(shp), F32, kind="ExternalInput").ap()
    dout = lambda n, shp: nc.dram_tensor(n, list(shp), F32, kind="ExternalOutput").ap()
    x_in = din("x", [T, D]); xs_in = din("xs", [NS, D])
    wqkv = din("wqkv", [2, 8, 3, D, 384]); wo = din("wo", [2, D, D])
    wup = din("wup", [DEPTH, NJ, D, 256]); wdn = din("wdn", [DEPTH, DFF, D]); ffk = din("ffk", [DEPTH, 128, NJ, 3])
    ffk_tm = din("ffk_tm", [DEPTH, 3, DFF])
    pwin = din("pwin", [D, D]); pwgrp = din("pwgrp", [4, 256, 256]); pscale = din("pscale", [128, 8]); pwout = din("pwout", [D, D])
    prcnt = din("prcnt", [128, 8, 16])
    swin = din("swin", [8, D, 384]); sk = din("sk", [128, 8, 3]); sk_tm = din("sk_tm", [3, D]); swout = din("swout", [D, D])
    lng = din("lng", [DEPTH, 2, D]); lnb = din("lnb", [DEPTH, 2, D])
    rope = [din(f"rope{d}", [128, NT, 2, 32]) for (_, d) in GROUPS]
    rope_s = din("rope_s", [NS, 2, 32])
    band_in = din("band", [128, 256]); ident_in = din("ident", [128, 128])
    sel_in = din("sel", [NS, NS, 128]); oneh_in = din("oneh", [128, NS, NS])
    c_in = [din(f"c{w}", [2, NS, w, 2048]) for (w, _) in GROUPS]
    st_pool = din("st_pool", [NS, 15, D]); st_sconv = din("st_sconv", [NS, 2, D]); st_ffn = din("st_ffn", [DEPTH, NS, 2, DFF])

    y_out = dout("y", [T, D]); ys_out = dout("ys", [NS, D])
    kvp = [dout(f"kv{w}p", [2, w, 2048]) for (w, _) in GROUPS]
    kvs = [dout(f"kv{w}s", [2, NS, w, 2048]) for (w, _) in GROUPS]
    poolp = dout("poolp", [15, D]); pools = dout("pools", [NS, 15, D])
    sconvp = dout("sconvp", [2, D]); sconvs = dout("sconvs", [NS, 2, D])
    ffnp = dout("ffnp", [DEPTH, 2, DFF]); ffns = dout("ffns", [DEPTH, NS, 2, DFF])

    xres = nc.dram_tensor("xres_scr", [T, D], F32).ap()
    xT_scr = nc.dram_tensor("xT_scr", [128, 8, T], BF16).ap()
    mixT_scr = nc.dram_tensor("mixT_scr", [128, 8, T], BF16).ap()
    xres_s = nc.dram_tensor("xres_s_scr", [NS, D], F32).ap()
    xTs_scr = nc.dram_tensor("xTs_scr", [128, 8, NS], BF16).ap()
    qkvs_scr = nc.dram_tensor("qkvs_scr", [NS, 3, 3, D], F32).ap()
    ab_scr = nc.dram_tensor("ab_scr", [NS, 2, DFF], F32).ap()

    top = ExitStack()
    with top:
        S = Sched(nc, top)
        P = [top.enter_context(nc.psum_tensor(f"P{i}", [128, 512], F32)) for i in range(6)]
        PB = [top.enter_context(nc.psum_tensor(f"PB{i}", [128, 1024], BF16)) for i in range(2)]
        BP = [Buf(f"P{i}", excl=True) for i in range(6)]
        BPB = [Buf(f"PB{i}", excl=True) for i in range(2)]
        uid = [0]
        def sbt(st, name, shape, dt):
            uid[0] += 1
            return st.enter_context(nc.sbuf_tensor(f"sb_{name}_{uid[0]}", list(shape), dt))
        idf = sbt(top, "idf", [128, 128], F32); idb = sbt(top, "idb", [128, 128], BF16)
        bandf = sbt(top, "bandf", [128, 256], F32); band = sbt(top, "band", [128, 256], BF16)
        onesb = sbt(top, "onesb", [128, 64], BF16)
        Bconst = Buf("const")
        S.dma(idf[:], ident_in[:, :], writes=[Bconst])
        S.dma(bandf[:], band_in[:, :], writes=[Bconst])
        S.op("dve", L("tensor_copy", out=idb[:], in_=idf[:]), reads=[Bconst], writes=[Bconst])
        S.op("dve", L("tensor_copy", out=band[:], in_=bandf[:]), reads=[Bconst], writes=[Bconst])
        S.op("dve", L("memset", onesb[:], 1.0), writes=[Bconst])
        mbias = sbt(top, "mbias", [128, 256], BF16)
        S.op("dve", L("tensor_scalar", out=bandf[:], in0=bandf[:], scalar1=-1.0, scalar2=30000.0, op0=ALU.add, op1=ALU.mult), reads=[Bconst], writes=[Bconst])
        S.op("dve", L("tensor_copy", out=mbias[:], in_=bandf[:]), reads=[Bconst], writes=[Bconst])
        Bxres = [Buf(f"xres{t}") for t in range(NT)]
        BxTs = [Buf(f"xTs{t}") for t in range(NT)]
        BmixT = [Buf(f"mixT{h}") for h in range(8)]
        Bxres_s = Buf("xres_s"); BxTs_s = Buf("xTs_s"); Bqkvs = Buf("qkvs"); Babs = Buf("abs")
        sel_t = sbt(top, "sel", [NS, NS, 128], F32); oneh_t = sbt(top, "oneh", [128, NS, NS], F32)
        S.dma(sel_t[:], sel_in[:, :, :], writes=[Bconst])
        S.dma(oneh_t[:], oneh_in[:, :, :], writes=[Bconst])

        def load_xTs(st):
            t_ = sbt(st, "xTs", [128, 8, NS], BF16); B_ = Buf()
            S.dma(t_[:], xTs_scr[:, :, :], reads=[BxTs_s], writes=[B_])
            return t_, B_

        def to_T(st, src, Bsrc, n, name):
            sbb = sbt(st, name + "_b", [NS, n * 128], BF16); Bb = Buf()
            S.op("act", L("copy", out=sbb[:], in_=src), reads=[Bsrc], writes=[Bb])
            dst = sbt(st, name + "_T", [128, n, NS], BF16); Bd = Buf()
            for k in range(n):
                S.op("pe", L("transpose", out=PB[0][:, k * NS:(k + 1) * NS], in_=sbb[:, k * 128:(k + 1) * 128], identity=idb[0:NS, 0:NS]),
                     reads=[Bb, Bconst], writes=[BPB[0]], signal=(k == n - 1))
            S.op("dve", L("tensor_copy", out=dst[:].rearrange("p n s -> p (n s)"), in_=PB[0][:, 0:n * NS]), reads=[BPB[0]], writes=[Bd])
            return dst, Bd

        def ln_sample(ln, li, which, actT_s, Bact_s, nk, w, Bw):
            for c, pi in enumerate((4, 5)):
                for k in range(nk):
                    S.op("pe", L("matmul", P[pi][0:NS, :], lhsT=actT_s[:, k, :], rhs=w[:, k, c * 512:(c + 1) * 512], start=(k == 0), stop=(k == nk - 1)),
                         reads=[Bw, Bact_s], writes=[BP[pi]], signal=(k == nk - 1))
            src = xs_in[:, :] if (li == 0 and which == 0) else xres_s[:, :]
            srcb = [] if (li == 0 and which == 0) else [Bxres_s]
            dst = ys_out[:, :] if (li == DEPTH - 1 and which == 1) else xres_s[:, :]
            ln.tile(P[4][0:NS, :], P[5][0:NS, :], BP[4], BP[5], src, srcb, dst, [Bxres_s], xTs_scr[:, :, :], [BxTs_s], np_=NS)
        rr = {"cast": 0}
        bg = []

        def cast(out_ap, in_ap, reads, writes):
            eng = ("pool", "act")[rr["cast"] % 2]
            rr["cast"] += 1
            if eng == "act":
                return S.op("act", L("copy", out=out_ap, in_=in_ap), reads=reads, writes=writes)
            return S.op("pool", L("tensor_copy", out=out_ap, in_=in_ap), reads=reads, writes=writes)

        def load_xT(st):
            xT = sbt(st, "xT", [128, 8, T], BF16)
            BxT = [Buf(f"xT{t}") for t in range(NT)]
            for q in range(8):
                S.dma(xT[:, :, q * 512:(q + 1) * 512], xT_scr[:, :, q * 512:(q + 1) * 512],
                      reads=BxTs[4 * q:4 * q + 4], writes=BxT[4 * q:4 * q + 4])
            return xT, BxT

        def fm_block_to_rows(src_ap, Bsrc, r, rowbuf, Brow, j):
            S.op("pe", L("matmul", P[5][0:r, 0:128], lhsT=src_ap, rhs=idf[:, :], start=True, stop=True), reads=[Bsrc, Bconst], writes=[BP[5]])
            S.op("dve", L("tensor_copy", out=rowbuf[0:r, j * 128:(j + 1) * 128], in_=P[5][0:r, 0:128]), reads=[BP[5]], writes=[Brow])

        class LN:
            def __init__(self, st, li, which):
                self.xr = [sbt(st, f"ln_xr{i}", [128, D], F32) for i in range(2)]
                self.xb = [sbt(st, f"ln_xb{i}", [128, D], BF16) for i in range(2)]
                self.stats = [sbt(st, f"ln_st{i}", [128, 2, 6], F32) for i in range(2)]
                self.mv = [sbt(st, f"ln_mv{i}", [128, 2], F32) for i in range(2)]
                self.rs = [sbt(st, f"ln_rs{i}", [128, 1], F32) for i in range(2)]
                self.xo = [sbt(st, f"ln_xo{i}", [128, 8, 128], BF16) for i in range(2)]
                self.gb = sbt(st, "ln_gb", [128, 2, D], F32)
                self.B = [[Buf() for _ in range(6)] for _ in range(2)]
                self.Bgb = Buf()
                S.dma(self.gb[:, 0, :], lng[li, which:which + 1, :].partition_broadcast(128), writes=[self.Bgb])
                S.dma(self.gb[:, 1, :], lnb[li, which:which + 1, :].partition_broadcast(128), writes=[self.Bgb])
                self.n = 0

            def tile(self, psA, psB, BpA, BpB, src_ap, src_bufs, dst_ap, dst_bufs, xT_dst_ap, xT_bufs, np_=128):
                i = self.n % 2
                self.n += 1
                xr, xb, stats, mv, rs, xo = self.xr[i], self.xb[i], self.stats[i], self.mv[i], self.rs[i], self.xo[i]
                Bx, Bb, Bs, Bm, Br, Bo = self.B[i]
                S.dma(xr[0:np_, :], src_ap, reads=src_bufs, writes=[Bx])
                for c, (ps, Bp) in enumerate(((psA, BpA), (psB, BpB))):
                    S.op("dve", L("scalar_tensor_tensor", out=xr[0:np_, c * 512:(c + 1) * 512], in0=xr[0:np_, c * 512:(c + 1) * 512], scalar=ALPHA, in1=ps, op0=ALU.mult, op1=ALU.add),
                         reads=[Bx, Bp], writes=[Bx])
                for c in range(2):
                    S.op("dve", L("bn_stats", out=stats[0:np_, c, :], in_=xr[0:np_, c * 512:(c + 1) * 512]), reads=[Bx], writes=[Bs])
                S.op("dve", L("bn_aggr", out=mv[0:np_, :], in_=stats[0:np_]), reads=[Bs], writes=[Bm])
                S.op("act", L("activation", out=rs[0:np_, :], in_=mv[0:np_, 1:2], func=AF.Sqrt, bias=LN_EPS, scale=1.0), reads=[Bm], writes=[Br])
                S.op("dve", L("reciprocal", out=rs[0:np_, :], in_=rs[0:np_, :]), reads=[Br], writes=[Br])
                S.op("dve", L("tensor_scalar", out=mv[0:np_, 1:2], in0=mv[0:np_, 0:1], scalar1=rs[0:np_, 0:1], scalar2=-1.0, op0=ALU.mult, op1=ALU.mult), reads=[Bm, Br], writes=[Bm])
                S.op("act", L("activation", out=xr[0:np_, :], in_=xr[0:np_, :], func=AF.Identity, bias=mv[0:np_, 1:2], scale=rs[0:np_, 0:1]), reads=[Bx, Bm, Br], writes=[Bx])
                S.op("dve", L("tensor_tensor", out=xr[0:np_, :], in0=xr[0:np_, :], in1=self.gb[0:np_, 0, :], op=ALU.mult), reads=[Bx, self.Bgb], writes=[Bx])
                S.op("pool", L("tensor_tensor", out=xr[0:np_, :], in0=xr[0:np_, :], in1=self.gb[0:np_, 1, :], op=ALU.add), reads=[Bx, self.Bgb], writes=[Bx])
                S.dma(dst_ap, xr[0:np_, :], reads=[Bx], writes=dst_bufs)
                S.op("act", L("copy", out=xb[0:np_, :], in_=xr[0:np_, :]), reads=[Bx], writes=[Bb])
                pb = PB[i]
                for k in range(8):
                    S.op("pe", L("transpose", out=pb[:, k * 128:k * 128 + np_], in_=xb[0:np_, k * 128:(k + 1) * 128], identity=idb[0:np_, 0:np_]),
                         reads=[Bb, Bconst], writes=[BPB[i]], signal=(k == 7))
                S.op("dve", L("tensor_copy", out=xo[:, :, 0:np_], in_=pb[:].rearrange("p (k n) -> p k n", k=8)[:, :, 0:np_]), reads=[BPB[i]], writes=[Bo])
                S.dma(xT_dst_ap, xo[:, :, 0:np_], reads=[Bo], writes=xT_bufs)

        def ln_dst(li, which, t):
            if li == DEPTH - 1 and which == 1:
                return y_out[t * 128:(t + 1) * 128, :]
            return xres[t * 128:(t + 1) * 128, :]

        def ln_src(li, which, t):
            if li == 0 and which == 0:
                return x_in[t * 128:(t + 1) * 128, :], []
            return xres[t * 128:(t + 1) * 128, :], [Bxres[t]]

        def load_wrow(w, Bw, src, k, stgs):
            for c in range(2):
                stg_, Bstg_ = stgs[c]
                S.dma(stg_[:, 0:512], src[k * 128:(k + 1) * 128, c * 512:(c + 1) * 512], writes=[Bstg_])
                cast(w[:, k, c * 512:(c + 1) * 512], stg_[:, 0:512], [Bstg_], [Bw])

        def load_wfull(st, name, src, nk, stg, Bstg, dbl=True):
            w = sbt(st, name, [128, nk, D], BF16)
            Bw = Buf(name)
            if dbl:
                stg2 = sbt(st, "wf_stg2", [128, 512], F32); Bstg2 = Buf()
                stgs = ((stg, Bstg), (stg2, Bstg2))
            else:
                stgs = ((stg, Bstg), (stg, Bstg))
            for k in range(nk):
                load_wrow(w, Bw, src, k, stgs)
            return w, Bw

        def proj_ln_phase(li, which, actT, BactT_fn, nk, w_src, st, sample_fn=None):
            stg = sbt(st, "wf_stg", [128, 512], F32); Bstg = Buf()
            w, Bw = load_wfull(st, "wfull", w_src, nk, stg, Bstg)
            ln = LN(st, li, which)
            def emit_mm(t):
                pa, pb_ = (0, 1) if t % 2 == 0 else (2, 3)
                for c, pi in enumerate((pa, pb_)):
                    for k in range(nk):
                        S.op("pe", L("matmul", P[pi][:, :], lhsT=actT[:, k, t * 128:(t + 1) * 128], rhs=w[:, k, c * 512:(c + 1) * 512], start=(k == 0), stop=(k == nk - 1)),
                             reads=[Bw] + BactT_fn(t), writes=[BP[pi]], signal=(k == nk - 1))
            emit_mm(0)
            for t in range(NT):
                pa, pb_ = (0, 1) if t % 2 == 0 else (2, 3)
                if t + 1 < NT:
                    emit_mm(t + 1)
                src, sb_ = ln_src(li, which, t)
                ln.tile(P[pa][:, :], P[pb_][:, :], BP[pa], BP[pb_], src, sb_, ln_dst(li, which, t), [Bxres[t]],
                        xT_scr[:, :, t * 128:(t + 1) * 128], [BxTs[t]])
            if sample_fn is not None and with_sample:
                aT, BaT = sample_fn(st)
                ln_sample(ln, li, which, aT, BaT, nk, w, Bw)

        def init_phase():
            with ExitStack() as st:
                xr = [sbt(st, f"i_xr{i}", [128, D], F32) for i in range(2)]
                xb = [sbt(st, f"i_xb{i}", [128, D], BF16) for i in range(2)]
                xo = [sbt(st, f"i_xo{i}", [128, 8, 128], BF16) for i in range(2)]
                Bs = [[Buf() for _ in range(3)] for _ in range(2)]
                for t in range(NT):
                    i = t % 2
                    S.dma(xr[i][:], x_in[t * 128:(t + 1) * 128, :], writes=[Bs[i][0]])
                    S.op("act", L("copy", out=xb[i][:], in_=xr[i][:]), reads=[Bs[i][0]], writes=[Bs[i][1]])
                    for k in range(8):
                        S.op("pe", L("transpose", out=PB[i][:, k * 128:(k + 1) * 128], in_=xb[i][:, k * 128:(k + 1) * 128], identity=idb[:]),
                             reads=[Bs[i][1], Bconst], writes=[BPB[i]], signal=(k == 7))
                    S.op("dve", L("tensor_copy", out=xo[i][:], in_=PB[i][:].rearrange("p (k n) -> p k n", k=8)), reads=[BPB[i]], writes=[Bs[i][2]])
                    S.dma(xT_scr[:, :, t * 128:(t + 1) * 128], xo[i][:], reads=[Bs[i][2]], writes=[BxTs[t]])
                if with_sample:
                    xs0 = sbt(st, "xs0", [NS, D], F32); Bxs0 = Buf()
                    S.dma(xs0[:], xs_in[:, :], writes=[Bxs0])
                    xsT, BxsT0 = to_T(st, xs0[:], Bxs0, 8, "xs0")
                    S.dma(xTs_scr[:, :, :], xsT[:], reads=[BxsT0], writes=[BxTs_s])
                    for gi, (win, d) in enumerate(GROUPS):
                        for ia_ in range(2):
                            for s_ in range(NS):
                                bg.append((kvs[gi][ia_, s_, 0:win - 1, :], c_in[gi][ia_, s_, 1:win, :]))
                    for s_ in range(NS):
                        S.dma(pools[s_, 0:14, :], st_pool[s_, 1:15, :])
                    S.dma(sconvs[:, 0, :], st_sconv[:, 1, :])
                    for l_ in range(DEPTH):
                        S.dma(ffns[l_, :, 0, :], st_ffn[l_, :, 1, :])

        def ffn_phase(li):
            with ExitStack() as st:
                wd = sbt(st, "wdn", [128, NJ, D], BF16); Bwd = Buf("wdn")
                wd_stgs = tuple((sbt(st, f"wf_stg{i}", [128, 512], F32), Buf()) for i in range(2))
                fk = sbt(st, "fk", [128, NJ, 3], F32); Bfk = Buf()
                S.dma(fk[:], ffk[li], writes=[Bfk])
                ah = sbt(st, "ah", [128, NJ, 2], F32); Bah = Buf()
                S.op("dve", L("memset", ah[:], 0.0), writes=[Bah])
                ln = LN(st, li, 1)
                if with_sample:
                    xs_T, BxsT = load_xTs(st)
                    abt = [sbt(st, f"abt{i}", [NS, 256], F32) for i in range(2)]; Babt = [Buf(), Buf()]
                sti = ExitStack()
                wst2 = [sbt(sti, f"wst{i}", [128, 8, 256], F32) for i in range(2)]; Bwst2 = [Buf(), Buf()]
                wbf = [sbt(sti, f"wbf{i}", [128, 8, 256], BF16) for i in range(2)]; Bwbf = [Buf(), Buf()]
                asb = [sbt(sti, f"asb{i}", [128, 514], F32) for i in range(2)]; Basb = [Buf(), Buf()]
                tmp = [sbt(sti, f"ftmp{i}", [128, 512], F32) for i in range(2)]; Btmp = [Buf(), Buf()]
                uu = [sbt(sti, f"fu{i}", [128, 512], F32) for i in range(2)]; Bu = [Buf(), Buf()]
                g = sbt(sti, "g", [128, NJ, 12 * 128], BF16)
                xTc = sbt(sti, "xTc", [128, 8, 12 * 128], BF16)
                t0 = 0
                it = 0
                BxTc_all = [Buf() for _ in range(12)]
                for ch in FCH:
                    ntok = ch * 128
                    Bg = [Buf() for _ in range(ch)]
                    BxTc = [Buf() for _ in range(ch)]
                    def load_xTc(t0_, ch_):
                        for q in range(ch_ // 4):
                            S.dma(xTc[:, :, q * 512:(q + 1) * 512], xT_scr[:, :, t0_ * 128 + q * 512:t0_ * 128 + (q + 1) * 512],
                                  reads=BxTs[t0_ + 4 * q:t0_ + 4 * q + 4], writes=BxTc[4 * q:4 * q + 4])
                    load_xTc(t0, ch)
                    FSTG = DBG.get("ffn_stage", 9)
                    def load_wup(j):
                        wi = j % 2
                        S.dma(wst2[wi][:], wup[li, j].rearrange("(k p) c -> p k c", p=128), writes=[Bwst2[wi]])
                        cast(wbf[wi][:], wst2[wi][:], [Bwst2[wi]], [Bwbf[wi]])
                    for j in range(NJ if FSTG >= 1 else 0):
                        wi = j % 2
                        if not (t0 > 0 and j < 2):
                            load_wup(j)
                        if t0 == 0:
                            load_wrow(wd, Bwd, wdn[li], j, wd_stgs)
                        if with_sample and t0 == 0:
                            for k in range(8):
                                S.op("pe", L("matmul", P[5][0:NS, 0:256], lhsT=xs_T[:, k, :], rhs=wbf[wi][:, k, :], start=(k == 0), stop=(k == 7)),
                                     reads=[Bwbf[wi], BxsT], writes=[BP[5]], signal=(k == 7))
                            S.op("act", L("copy", out=abt[wi][:], in_=P[5][0:NS, 0:256]), reads=[BP[5]], writes=[Babt[wi]])
                            S.dma(ab_scr[:, :, j * 128:(j + 1) * 128], abt[wi][:].rearrange("p (s c) -> p s c", s=2), reads=[Babt[wi]], writes=[Babs])
                        for b in range(ch // 4):
                            i = it % 2
                            it += 1
                            pa, pb_ = (0, 1) if i == 0 else (2, 3)
                            cols = slice(b * 512, (b + 1) * 512)
                            for half, pi in ((0, pa), (1, pb_)):
                                for k in range(8):
                                    S.op("pe", L("matmul", P[pi][:, :], lhsT=wbf[wi][:, k, half * 128:(half + 1) * 128], rhs=xTc[:, k, cols], start=(k == 0), stop=(k == 7)),
                                         reads=[Bwbf[wi]] + BxTc[4 * b:4 * b + 4], writes=[BP[pi]], signal=(k == 7))
                            a = asb[i]
                            S.op("dve", L("tensor_copy", out=a[:, 0:2], in_=ah[:, j, :]), reads=[Bah], writes=[Basb[i]])
                            S.op("act", L("copy", out=a[:, 2:514], in_=P[pa][:, :]), reads=[BP[pa]], writes=[Basb[i]])
                            S.op("dve", L("tensor_copy", out=ah[:, j, :], in_=a[:, 512:514]), reads=[Basb[i]], writes=[Bah])
                            tm = tmp[i]
                            S.op("dve", L("tensor_scalar", out=tm[:], in0=a[:, 0:512], scalar1=fk[:, j, 0:1], scalar2=None, op0=ALU.mult), reads=[Basb[i], Bfk], writes=[Btmp[i]])
                            S.op("dve", L("scalar_tensor_tensor", out=tm[:], in0=a[:, 1:513], scalar=fk[:, j, 1:2], in1=tm[:], op0=ALU.mult, op1=ALU.add), reads=[Basb[i], Bfk, Btmp[i]], writes=[Btmp[i]])
                            S.op("dve", L("scalar_tensor_tensor", out=tm[:], in0=a[:, 2:514], scalar=fk[:, j, 2:3], in1=tm[:], op0=ALU.mult, op1=ALU.add), reads=[Basb[i], Bfk, Btmp[i]], writes=[Btmp[i]])
                            u = uu[i]
                            S.op("act", L("activation", out=u[:], in_=tm[:], func=AF.Gelu), reads=[Btmp[i]], writes=[Bu[i]])
                            S.op("dve", L("tensor_tensor", out=g[:, j, cols], in0=P[pb_][:, :], in1=u[:], op=ALU.mult), reads=[Bu[i], BP[pb_]], writes=Bg[4 * b:4 * b + 4])
                    def emit_dn(tt):
                        pa, pb_ = (0, 1) if tt % 2 == 0 else (2, 3)
                        for c, pi in enumerate((pa, pb_)):
                            for j in range(NJ):
                                S.op("pe", L("matmul", P[pi][:, :], lhsT=g[:, j, tt * 128:(tt + 1) * 128], rhs=wd[:, j, c * 512:(c + 1) * 512], start=(j == 0), stop=(j == NJ - 1)),
                                     reads=[Bwd, Bg[tt]], writes=[BP[pi]], signal=(j == NJ - 1))
                    if t0 + ch < NT and FSTG >= 1:
                        load_wup(0)
                        load_wup(1)
                    if FSTG >= 2:
                        emit_dn(0)
                    for tt in range(ch if FSTG >= 2 else 0):
                        t = t0 + tt
                        pa, pb_ = (0, 1) if tt % 2 == 0 else (2, 3)
                        if tt + 1 < ch:
                            emit_dn(tt + 1)
                        ln.tile(P[pa][:, :], P[pb_][:, :], BP[pa], BP[pb_], xres[t * 128:(t + 1) * 128, :], [Bxres[t]], ln_dst(li, 1, t), [Bxres[t]],
                                xT_scr[:, :, t * 128:(t + 1) * 128], [BxTs[t]])
                    t0 += ch
                sti.close()
                with ExitStack() as st3:
                    rowb = sbt(st3, "rowb", [2, DFF], F32); Browb = Buf()
                    for j in range(NJ if FSTG >= 3 else 0):
                        fm_block_to_rows(ah[:, j, :], Bah, 2, rowb, Browb, j)
                    if FSTG >= 3:
                        S.dma(ffnp[li, :, :], rowb[:], reads=[Browb])
                if with_sample:
                    with ExitStack() as st2:
                        a_b = sbt(st2, "ab_s", [NS, 2, DFF], F32); Bab = Buf()
                        S.dma(a_b[:], ab_scr[:, :, :], reads=[Babs], writes=[Bab])
                        stf = sbt(st2, "stf", [NS, 2, DFF], F32); Bstf = Buf()
                        S.dma(stf[:], st_ffn[li], writes=[Bstf])
                        ktm = sbt(st2, "ktm", [NS, 3, DFF], F32); Bktm = Buf()
                        for tap in range(3):
                            S.dma(ktm[:, tap, :], ffk_tm[li, tap:tap + 1, :].partition_broadcast(NS), writes=[Bktm])
                        t1 = sbt(st2, "t1", [NS, DFF], F32); t2 = sbt(st2, "t2", [NS, DFF], F32); Bt1, Bt2 = Buf(), Buf()
                        tt_ = lambda o, a, b, op, rd, wr: S.op("dve", L("tensor_tensor", out=o, in0=a, in1=b, op=op), reads=rd, writes=wr)
                        tt_(t1[:], stf[:, 0, :], ktm[:, 0, :], ALU.mult, [Bstf, Bktm], [Bt1])
                        tt_(t2[:], stf[:, 1, :], ktm[:, 1, :], ALU.mult, [Bstf, Bktm], [Bt2])
                        tt_(t1[:], t1[:], t2[:], ALU.add, [Bt1, Bt2], [Bt1])
                        tt_(t2[:], a_b[:, 0, :], ktm[:, 2, :], ALU.mult, [Bab, Bktm, Bt1], [Bt2])
                        tt_(t1[:], t1[:], t2[:], ALU.add, [Bt1, Bt2], [Bt1])
                        S.op("act", L("activation", out=t2[:], in_=t1[:], func=AF.Gelu), reads=[Bt1], writes=[Bt2])
                        tt_(t1[:], t2[:], a_b[:, 1, :], ALU.mult, [Bt2, Bab], [Bt1])
                        gsT, BgsT = to_T(st2, t1[:], Bt1, NJ, "gs")
                        ln_sample(ln, li, 1, gsT, BgsT, NJ, wd, Bwd)
                        S.dma(ffns[li, :, 1, :], a_b[:, 0, :], reads=[Bab])

        def attn_phase(li, ia):
            with ExitStack() as st:
                xT, BxT = load_xT(st)
                wst = sbt(st, "wst", [128, 4, 384], F32); Bwst = Buf()
                wbf = [sbt(st, f"wbf{i}", [128, 8, 384], BF16) for i in range(2)]; Bwbf = [Buf(), Buf()]
                rt0 = sbt(st, "rt0", [128, NT, 2, 32], F32); rt = [rt0, rt0]; Brt0 = Buf(); Brt = [Brt0, Brt0]
                QT = sbt(st, "QT", [128, NT, 128], BF16); KTt = sbt(st, "KT", [128, NT, 128], BF16)
                V = sbt(st, "V", [128, NT, 128], BF16)
                BQ = [Buf() for _ in range(NT)]; BK = [Buf() for _ in range(NT)]; BV = [Buf() for _ in range(NT)]
                accN = sbt(st, "accN", [128, T], F32); accD = sbt(st, "accD", [128, T], F32)
                Bacc = [Buf() for _ in range(8)]
                qkr = [sbt(st, f"qkr{i}", [128, 4, 256], F32) for i in range(2)]; Bqkr = [Buf(), Buf()]
                tb = [sbt(st, f"tb{i}", [128, 4, 4, 2, 32], F32) for i in range(2)]; Btb = [Buf(), Buf()]
                qkb = [sbt(st, f"qkb{i}", [128, 4, 256], BF16) for i in range(2)]; Bqkb = [Buf(), Buf()]
                raw = [sbt(st, f"raw{i}", [128, 4, 384], F32) for i in range(2)]; Braw = [Buf(), Buf()]
                Et = [sbt(st, f"E{i}", [128, 256], BF16) for i in range(4)]; BE = [Buf() for _ in range(4)]
                Em = [sbt(st, f"Em{i}", [128, 256], BF16) for i in range(6)]; BEm = [Buf() for _ in range(6)]
                mixT = [sbt(st, f"mixThp{i}", [128, 512], BF16) for i in range(2)]; Bmix = [Buf(), Buf()]
                xg = [sbt(st, f"xg{i}", [128, 8, 128], BF16) for i in range(4)]; Bxg = [Buf() for _ in range(4)]
                gcount = [0]
                if with_sample:
                    xs_T, BxsT = load_xTs(st)
                    rs_t = sbt(st, "rs_t", [NS, 2, 32], F32); Brs = Buf()
                    S.dma(rs_t[:], rope_s[:, :, :], writes=[Brs])
                    qsr = [sbt(st, f"qsr{i}", [NS, 384], F32) for i in range(2)]; Bqsr = [Buf(), Buf()]
                    tbs = [sbt(st, f"tbs{i}", [NS, 4, 2, 32], F32) for i in range(2)]; Btbs = [Buf(), Buf()]
                unit = 0
                for hp in range(DBG.get("hp_n", 8)):
                    for gi, (win, d) in enumerate(GROUPS):
                        if gi not in DBG.get("groups", (0, 1, 2)):
                            continue
                        nblk = NT // d
                        ui = unit % 2
                        unit += 1
                        if bg:
                            S.dma(*bg.pop(0))
                        wsrc = wqkv[ia, hp, gi].rearrange("(k p) c -> p k c", p=128)
                        for kh in range(2):
                            S.dma(wst[:], wsrc[:, kh * 4:(kh + 1) * 4, :], writes=[Bwst])
                            cast(wbf[ui][:, kh * 4:(kh + 1) * 4, :], wst[:], [Bwst], [Bwbf[ui]])
                        S.dma(rt[ui][:], rope[gi], writes=[Brt[ui]])
                        w = wbf[ui]
                        r_t = rt[ui]
                        if with_sample:
                            for k in range(8):
                                S.op("pe", L("matmul", P[5][0:NS, 0:384], lhsT=xs_T[:, k, :], rhs=w[:, k, :], start=(k == 0), stop=(k == 7)),
                                     reads=[Bwbf[ui], BxsT], writes=[BP[5]], signal=(k == 7))
                            qs = qsr[ui]; ts_ = tbs[ui]
                            ssrc = P[5][0:NS, 0:256].rearrange("p (j h f) -> p j h f", j=4, h=2)
                            scos = rs_t[:, 0, :].unsqueeze(1).unsqueeze(1).to_broadcast([NS, 4, 2, 32])
                            ssin = rs_t[:, 1, :].unsqueeze(1).to_broadcast([NS, 4, 32])
                            sdst = qs[:, 0:256].rearrange("p (j h f) -> p j h f", j=4, h=2)
                            S.op("dve", L("tensor_tensor", out=sdst, in0=ssrc, in1=scos, op=ALU.mult), reads=[BP[5], Brs], writes=[Bqsr[ui]])
                            S.op("dve", L("tensor_tensor", out=ts_[:, :, 0, :], in0=ssrc[:, :, 1, :], in1=ssin, op=ALU.mult), reads=[BP[5], Brs], writes=[Btbs[ui]])
                            S.op("dve", L("tensor_tensor", out=ts_[:, :, 1, :], in0=ssrc[:, :, 0, :], in1=ssin, op=ALU.mult), reads=[BP[5], Brs], writes=[Btbs[ui]])
                            S.op("act", L("copy", out=qs[:, 256:384], in_=P[5][0:NS, 256:384]), reads=[BP[5]], writes=[Bqsr[ui]])
                            S.op("dve", L("tensor_tensor", out=sdst[:, :, 0, :], in0=sdst[:, :, 0, :], in1=ts_[:, :, 0, :], op=ALU.subtract), reads=[Bqsr[ui], Btbs[ui]], writes=[Bqsr[ui]])
                            S.op("dve", L("tensor_tensor", out=sdst[:, :, 1, :], in0=sdst[:, :, 1, :], in1=ts_[:, :, 1, :], op=ALU.add), reads=[Bqsr[ui], Btbs[ui]], writes=[Bqsr[ui]])
                            S.dma(qkvs_scr[:, gi, :, hp * 128:(hp + 1) * 128], qs[:].rearrange("p (s c) -> p s c", s=3), reads=[Bqsr[ui]], writes=[Bqkvs])
                        PSTG = DBG.get("proj_stage", 9)

                        def proj_mm(bt):
                            for bi in range(4):
                                blk = bt * 4 + bi
                                r, nb = blk // nblk, blk % nblk
                                tok0 = r + d * 128 * nb
                                toks = slice(tok0, tok0 + d * 127 + 1, d)
                                tl = sorted(set([tok0 // 128 + x for x in range(0, (d * 127) // 128 + 1)]))
                                if d == 1:
                                    lsrc = lambda k, toks=toks: xT[:, k, toks]
                                    lreads = [BxT[x] for x in tl if x < NT]
                                else:
                                    gsl = gcount[0] % 4
                                    gcount[0] += 1
                                    xg_ = xg[gsl]
                                    S.op(("pool", "act")[gsl % 2], L(("tensor_copy", "copy")[gsl % 2], out=xg_[:], in_=xT[:, :, toks]),
                                         reads=[BxT[x] for x in tl if x < NT], writes=[Bxg[gsl]])
                                    lsrc = lambda k, xg_=xg_: xg_[:, k, :]
                                    lreads = [Bxg[gsl]]
                                for k in range(8):
                                    S.op("pe", L("matmul", P[bi][:, 0:384], lhsT=lsrc(k), rhs=w[:, k, :], start=(k == 0), stop=(k == 7)),
                                         reads=[Bwbf[ui]] + lreads, writes=[BP[bi]], signal=(k == 7))

                        def proj_post_a(bt):
                            i = bt % 2
                            q_r = qkr[i]; t_b = tb[i]; q_b = qkb[i]; rw = raw[i]
                            for bi in range(4):
                                S.op("act", L("copy", out=rw[:, bi, :], in_=P[bi][:, 0:384]), reads=[BP[bi]], writes=[Braw[i]])
                            rqk = rw[:, :, 0:256].rearrange("p b (j h f) -> p b j h f", j=4, h=2)
                            qd = q_r[:].rearrange("p b (j h f) -> p b j h f", j=4, h=2)
                            cosb = r_t[:, bt * 4:bt * 4 + 4, 0, :].unsqueeze(2).to_broadcast([128, 4, 4, 32])
                            sinb = r_t[:, bt * 4:bt * 4 + 4, 1, :].unsqueeze(2).to_broadcast([128, 4, 4, 32])
                            for h_ in range(2):
                                S.op("dve", L("tensor_tensor", out=qd[:, :, :, h_, :], in0=rqk[:, :, :, h_, :], in1=cosb, op=ALU.mult), reads=[Braw[i], Brt[ui]], writes=[Bqkr[i]])
                                S.op("dve", L("tensor_tensor", out=t_b[:, :, :, h_, :], in0=rqk[:, :, :, 1 - h_, :], in1=sinb, op=ALU.mult), reads=[Braw[i], Brt[ui]], writes=[Btb[i]])
                            S.op("pool", L("tensor_tensor", out=qd[:, :, :, 0, :], in0=qd[:, :, :, 0, :], in1=t_b[:, :, :, 0, :], op=ALU.subtract), reads=[Bqkr[i], Btb[i]], writes=[Bqkr[i]])
                            S.op("dve", L("tensor_tensor", out=qd[:, :, :, 1, :], in0=qd[:, :, :, 1, :], in1=t_b[:, :, :, 1, :], op=ALU.add), reads=[Bqkr[i], Btb[i]], writes=[Bqkr[i]])
                            S.op("act", L("copy", out=q_b[:], in_=q_r[:]), reads=[Bqkr[i]], writes=[Bqkb[i]])
                            S.op("pool", L("tensor_copy", out=V[:, bt * 4:bt * 4 + 4, :], in_=rw[:, :, 256:384]), reads=[Braw[i]], writes=BV[bt * 4:bt * 4 + 4])
                            for bi in range(4):
                                blk = bt * 4 + bi
                                r, nb = blk // nblk, blk % nblk
                                tok0 = r + d * 128 * nb
                                if tok0 >= T - win and DBG.get("kvout", True):
                                    row0 = tok0 - (T - win)
                                    rows = slice(row0, row0 + d * 127 + 1, d)
                                    S.dma(kvp[gi][ia, rows, hp * 128:(hp + 1) * 128], q_r[:, bi, 128:256], reads=[Bqkr[i]])
                                    S.dma(kvp[gi][ia, rows, 1024 + hp * 128:1024 + (hp + 1) * 128], rw[:, bi, 256:384], reads=[Braw[i]])

                        def proj_post_b(bt):
                            i = bt % 2
                            q_b = qkb[i]
                            pbk = PB[i]
                            for bi in range(4):
                                for s_ in range(2):
                                    S.op("pe", L("transpose", out=pbk[:, (bi * 2 + s_) * 128:(bi * 2 + s_ + 1) * 128], in_=q_b[:, bi, s_ * 128:(s_ + 1) * 128], identity=idb[:]),
                                         reads=[Bqkb[i], Bconst], writes=[BPB[i]], signal=(bi == 3 and s_ == 1))
                            pv = pbk[:].rearrange("p (b s n) -> p b s n", b=4, s=2)
                            S.op("dve", L("tensor_copy", out=QT[:, bt * 4:bt * 4 + 4, :], in_=pv[:, :, 0, :]), reads=[BPB[i]], writes=BQ[bt * 4:bt * 4 + 4])
                            S.op("act", L("copy", out=KTt[:, bt * 4:bt * 4 + 4, :], in_=pv[:, :, 1, :]), reads=[BPB[i]], writes=BK[bt * 4:bt * 4 + 4])

                        nbt = NT // 4 if PSTG >= 1 else 0
                        if nbt:
                            proj_mm(0)
                        for bt in range(nbt):
                            if PSTG >= 2:
                                proj_post_a(bt)
                            if bt + 1 < nbt:
                                proj_mm(bt + 1)
                            if PSTG >= 3:
                                proj_post_b(bt)

                        ecount = [0]
                        emidx = {}

                        def scores(blk):
                            r, nb = blk // nblk, blk % nblk
                            ncol = 256 if nb + 1 < nblk else 128
                            nqb = ncol // 128
                            for h in range(2):
                                pi = 4 + (ecount[0] % 2)
                                mi = ecount[0] % 6
                                ecount[0] += 1
                                hs = slice(64 * h, 64 * h + 64)
                                S.op("pe", L("matmul", P[pi][:, 0:ncol], lhsT=KTt[hs, blk, :], rhs=QT[hs, blk:blk + nqb, :], start=True, stop=False),
                                     reads=[BK[blk]] + BQ[blk:blk + nqb], writes=[BP[pi]], signal=False)
                                S.op("pe", L("matmul", P[pi][:, 0:ncol], lhsT=idb[:, :], rhs=mbias[:, 0:ncol], start=False, stop=True),
                                     reads=[Bconst], writes=[BP[pi]])
                                S.op("act", L("activation", out=Em[mi][:, 0:ncol], in_=P[pi][:, 0:ncol], func=AF.Exp, scale=0.125), reads=[BP[pi]], writes=[BEm[mi]])
                                emidx[(blk, h)] = mi

                        def pvstep(blk):
                            r, nb = blk // nblk, blk % nblk
                            qi = blk % 4
                            bnk = (blk // 4) % 2
                            pn, pd = (0, 1) if bnk == 0 else (2, 3)
                            for h in range(2):
                                hs = slice(64 * h, 64 * h + 64)
                                srcs = []
                                if nb >= 1:
                                    srcs.append((blk - 1, emidx[(blk - 1, h)], slice(128, 256)))
                                srcs.append((blk, emidx[(blk, h)], slice(0, 128)))
                                for (pp, lhs_fn) in ((pn, lambda kb, hs=hs: V[:, kb, hs]), (pd, lambda kb: onesb[:, :])):
                                    for si, (kb, mi, cs) in enumerate(srcs):
                                        S.op("pe", L("matmul", P[pp][hs, qi * 128:(qi + 1) * 128], lhsT=lhs_fn(kb), rhs=Em[mi][:, cs], start=(si == 0), stop=(si == len(srcs) - 1)),
                                             reads=[BV[kb], BEm[mi], Bconst], writes=[BP[pp]], signal=(si == len(srcs) - 1))
                            if qi == 3:
                                b0 = blk - 3
                                r0, nb0 = b0 // nblk, b0 % nblk
                                for (pp, acc) in ((pn, accN), (pd, accD)):
                                    accv = acc[:].rearrange("p (n r) -> p r n", r=d)
                                    if nblk >= 4:
                                        dst = accv[:, r0, 128 * nb0:128 * nb0 + 512]
                                        src = P[pp][:, :]
                                    else:
                                        dst = accv[:, r0:r0 + 2, 0:256]
                                        src = P[pp][:, :].rearrange("p (a n) -> p a n", a=2)
                                    if gi == 0:
                                        S.op("dve", L("tensor_copy", out=dst, in_=src), reads=[BP[pp]], writes=Bacc)
                                    else:
                                        S.op("dve", L("tensor_tensor", out=dst, in0=src, in1=dst, op=ALU.add), reads=[BP[pp]] + Bacc, writes=Bacc)

                        ncore = NT if DBG.get("core", True) else 0
                        if ncore:
                            scores(0)
                        for blk in range(ncore):
                            if blk + 1 < ncore:
                                scores(blk + 1)
                            pvstep(blk)
                    if not DBG.get("norm", True):
                        continue
                    for q in range(8):
                        cs = slice(q * 512, (q + 1) * 512)
                        S.op("dve", L("reciprocal", out=accD[:, cs], in_=accD[:, cs]), reads=Bacc, writes=Bacc)
                        S.op("dve", L("tensor_tensor", out=mixT[q % 2][:], in0=accN[:, cs], in1=accD[:, cs], op=ALU.mult), reads=Bacc, writes=[Bmix[q % 2]])
                        S.dma(mixT_scr[:, hp, cs], mixT[q % 2][:], reads=[Bmix[q % 2]], writes=[BmixT[hp]])
            if not DBG.get("wo", True):
                return
            with ExitStack() as st:
                mT = sbt(st, "mT", [128, 8, T], BF16); BmT = Buf()
                for hp in range(8):
                    S.dma(mT[:, hp, :], mixT_scr[:, hp, :], reads=[BmixT[hp]], writes=[BmT])
                def attn_sample(st2):
                    qk = sbt(st2, "qkvs", [NS, 3, 3, D], F32); Bqk = Buf()
                    S.dma(qk[:], qkvs_scr[:, :, :, :], reads=[Bqkvs], writes=[Bqk])
                    Kc = sbt(st2, "Kc", [128, D], F32); Vc = sbt(st2, "Vc", [128, D], F32); BKc, BVc = Buf(), Buf()
                    prod = sbt(st2, "prod", [128, D], F32); Bprod = Buf()
                    tmpv = sbt(st2, "tmpv", [128, D], F32); Btmpv = Buf()
                    Ssc = sbt(st2, "Ssc", [128, 16], F32); Es = sbt(st2, "Es", [128, 16], F32); BSs, BEs = Buf(), Buf()
                    pself = sbt(st2, "pself", [NS, D], F32); Bps = Buf()
                    sself = sbt(st2, "sself", [NS, 16], F32); eself = sbt(st2, "eself", [NS, 16], F32); Bss, Bes = Buf(), Buf()
                    first = True
                    for gi, (win, d) in enumerate(GROUPS):
                        for s_ in range(NS):
                            S.dma(Kc[:], c_in[gi][ia, s_, 0:win:d, 0:1024], writes=[BKc])
                            S.dma(Vc[:], c_in[gi][ia, s_, 0:win:d, 1024:2048], writes=[BVc])
                            for c in range(2):
                                S.op("pe", L("matmul", P[c][:, :], lhsT=sel_t[0:NS, s_, :], rhs=qk[:, gi, 0, c * 512:(c + 1) * 512], start=True, stop=True),
                                     reads=[Bqk, Bconst], writes=[BP[c]])
                                S.op("dve", L("tensor_tensor", out=prod[:, c * 512:(c + 1) * 512], in0=Kc[:, c * 512:(c + 1) * 512], in1=P[c][:, :], op=ALU.mult), reads=[BKc, BP[c]], writes=[Bprod])
                            S.op("dve", L("tensor_reduce", out=Ssc[:], in_=prod[:].rearrange("p (h f) -> p h f", h=16), axis=AX.X, op=ALU.add), reads=[Bprod], writes=[BSs])
                            S.op("act", L("activation", out=Es[:], in_=Ssc[:], func=AF.Exp, scale=0.125), reads=[BSs], writes=[BEs])
                            S.op("dve", L("tensor_tensor", out=tmpv[:].rearrange("p (h f) -> p h f", h=16), in0=Vc[:].rearrange("p (h f) -> p h f", h=16), in1=Es[:].unsqueeze(2).to_broadcast([128, 16, 64]), op=ALU.mult), reads=[BVc, BEs], writes=[Btmpv])
                            for c in range(2):
                                S.op("pe", L("matmul", P[2 + c][0:NS, :], lhsT=oneh_t[:, s_, :], rhs=tmpv[:, c * 512:(c + 1) * 512], start=first, stop=False),
                                     reads=[Btmpv, Bconst], writes=[BP[2 + c]])
                            S.op("pe", L("matmul", P[4][0:NS, 0:16], lhsT=oneh_t[:, s_, :], rhs=Es[:], start=first, stop=False), reads=[BEs, Bconst], writes=[BP[4]])
                            first = False
                        S.op("dve", L("tensor_tensor", out=pself[:], in0=qk[:, gi, 0, :], in1=qk[:, gi, 1, :], op=ALU.mult), reads=[Bqk], writes=[Bps])
                        S.op("dve", L("tensor_reduce", out=sself[:], in_=pself[:].rearrange("p (h f) -> p h f", h=16), axis=AX.X, op=ALU.add), reads=[Bps], writes=[Bss])
                        S.op("act", L("activation", out=eself[:], in_=sself[:], func=AF.Exp, scale=0.125), reads=[Bss], writes=[Bes])
                        S.op("dve", L("tensor_tensor", out=pself[:].rearrange("p (h f) -> p h f", h=16), in0=qk[:, gi, 2, :].rearrange("p (h f) -> p h f", h=16), in1=eself[:].unsqueeze(2).to_broadcast([NS, 16, 64]), op=ALU.mult), reads=[Bqk, Bes, Bss], writes=[Bps])
                        last = (gi == len(GROUPS) - 1)
                        for c in range(2):
                            S.op("pe", L("matmul", P[2 + c][0:NS, :], lhsT=idf[0:NS, 0:NS], rhs=pself[:, c * 512:(c + 1) * 512], start=False, stop=last), reads=[Bps, Bconst], writes=[BP[2 + c]])
                        S.op("pe", L("matmul", P[4][0:NS, 0:16], lhsT=idf[0:NS, 0:NS], rhs=eself[:], start=False, stop=last), reads=[Bes, Bconst], writes=[BP[4]])
                        S.dma(kvs[gi][ia, :, win - 1, 0:1024], qk[:, gi, 1, :], reads=[Bqk])
                        S.dma(kvs[gi][ia, :, win - 1, 1024:2048], qk[:, gi, 2, :], reads=[Bqk])
                    rden = sbt(st2, "rden", [NS, 16], F32); Brd = Buf()
                    mixs = sbt(st2, "mixs", [NS, D], F32); Bmx = Buf()
                    S.op("dve", L("reciprocal", out=rden[:], in_=P[4][0:NS, 0:16]), reads=[BP[4]], writes=[Brd])
                    for c in range(2):
                        S.op("dve", L("tensor_tensor", out=mixs[:, c * 512:(c + 1) * 512].rearrange("p (h f) -> p h f", h=8), in0=P[2 + c][0:NS, :].rearrange("p (h f) -> p h f", h=8), in1=rden[:, c * 8:(c + 1) * 8].unsqueeze(2).to_broadcast([NS, 8, 64]), op=ALU.mult), reads=[BP[2 + c], Brd], writes=[Bmx])
                    return to_T(st2, mixs[:], Bmx, 8, "mixs")
                proj_ln_phase(li, 0, mT, lambda t: [BmT], 8, wo[ia], st, sample_fn=attn_sample)

        def pool_phase(li):
            with ExitStack() as st0:
                pT = sbt(st0, "pooledT", [128, 8, T], BF16); BpT = [Buf() for _ in range(8)]
                us = sbt(st0, "us", [NS, D], F32); Bus = Buf()
                with ExitStack() as st:
                    xT, BxT = load_xT(st)
                    stg = sbt(st, "wf_stg", [128, 512], F32); Bstg = Buf()
                    w, Bw = load_wfull(st, "pwin", pwin, 8, stg, Bstg, dbl=False)
                    rc = sbt(st, "rc", [128, 8, 16], F32); Brc = Buf()
                    S.dma(rc[:], prcnt[:, :, :], writes=[Brc])
                    prow = sbt(st, "prow", [15, D], F32); Bprow = Buf()
                    ub = sbt(st, "ub", [128, 16 + T], F32); sA = sbt(st, "sA", [128, 16 + T], F32); sB = sbt(st, "sB", [128, 16 + T], F32)
                    Bub, BsA, BsB = Buf(), Buf(), Buf()
                    for (bu, Bb) in ((ub, Bub), (sA, BsA), (sB, BsB)):
                        S.op("pool", L("memset", bu[:, 0:16], 0.0), writes=[Bb])
                    if with_sample:
                        xs_T, BxsT = load_xTs(st)
                        for c in range(2):
                            for k in range(8):
                                S.op("pe", L("matmul", P[4 + c][0:NS, :], lhsT=xs_T[:, k, :], rhs=w[:, k, c * 512:(c + 1) * 512], start=(k == 0), stop=(k == 7)),
                                     reads=[Bw, BxsT], writes=[BP[4 + c]], signal=(k == 7))
                            S.op("act", L("copy", out=us[:, c * 512:(c + 1) * 512], in_=P[4 + c][0:NS, :]), reads=[BP[4 + c]], writes=[Bus])
                    for j in range(8):
                        for blk in range(8):
                            pi = blk % 4
                            for k in range(8):
                                S.op("pe", L("matmul", P[pi][:, :], lhsT=w[:, k, j * 128:(j + 1) * 128], rhs=xT[:, k, blk * 512:(blk + 1) * 512], start=(k == 0), stop=(k == 7)),
                                     reads=[Bw] + BxT[4 * blk:4 * blk + 4], writes=[BP[pi]], signal=(k == 7))
                            S.op("act", L("copy", out=ub[:, 16 + blk * 512:16 + (blk + 1) * 512], in_=P[pi][:, :]), reads=[BP[pi]], writes=[Bub])
                        wj = POOL_W[j // 2]
                        cur, Bcur = ub, Bub
                        step = 1
                        bufs = [(sA, BsA), (sB, BsB)]
                        bi = 0
                        while step < wj:
                            nxt, Bn = bufs[bi % 2]
                            bi += 1
                            for hh in range(2):
                                cs = slice(16 + hh * 2048, 16 + (hh + 1) * 2048)
                                cs2 = slice(16 - step + hh * 2048, 16 - step + (hh + 1) * 2048)
                                S.op(("dve", "pool")[hh], L("tensor_tensor", out=nxt[:, cs], in0=cur[:, cs], in1=cur[:, cs2], op=ALU.add), reads=[Bcur], writes=[Bn])
                            cur, Bcur = nxt, Bn
                            step *= 2
                        S.op("dve", L("tensor_tensor", out=cur[:, 16:32], in0=cur[:, 16:32], in1=rc[:, j, :], op=ALU.mult), reads=[Bcur, Brc], writes=[Bcur])
                        S.op("dve", L("tensor_scalar", out=cur[:, 16:32], in0=cur[:, 16:32], scalar1=float(wj), scalar2=None, op0=ALU.mult), reads=[Bcur], writes=[Bcur])
                        for hh in range(2):
                            cs = slice(16 + hh * 2048, 16 + (hh + 1) * 2048)
                            co = slice(hh * 2048, (hh + 1) * 2048)
                            S.op("dve", L("scalar_tensor_tensor", out=pT[:, j, co], in0=cur[:, cs], scalar=1.0 / wj, in1=ub[:, cs], op0=ALU.mult, op1=ALU.subtract), reads=[Bcur, Bub], writes=[BpT[j]])
                        fm_block_to_rows(ub[:, 16 + T - 15:16 + T], Bub, 15, prow, Bprow, j)
                    S.dma(poolp[:, :], prow[:], reads=[Bprow])
                with ExitStack() as st:
                    zT = pT
                    wgs = sbt(st, "wgs", [128, 4, 2, 256], F32); wgb = sbt(st, "wgb", [128, 4, 2, 256], BF16); Bwg = Buf()
                    psc = sbt(st, "psc", [128, 8], F32)
                    S.dma(wgs[:], pwgrp.rearrange("g (k p) c -> p g k c", p=128), writes=[Bwg])
                    S.dma(psc[:], pscale[:, :], writes=[Bwg])
                    S.op("dve", L("tensor_copy", out=wgb[:], in_=wgs[:]), reads=[Bwg], writes=[Bwg])
                    n = 0
                    for blk in range(8):
                        for gi in range(4):
                            pis = (0, 1) if n % 2 == 0 else (2, 3)
                            n += 1
                            for mt in range(2):
                                pi = pis[mt]
                                for kt in range(2):
                                    S.op("pe", L("matmul", P[pi][:, :], lhsT=wgb[:, gi, kt, mt * 128:(mt + 1) * 128], rhs=pT[:, 2 * gi + kt, blk * 512:(blk + 1) * 512], start=(kt == 0), stop=(kt == 1)),
                                         reads=[Bwg, BpT[2 * gi], BpT[2 * gi + 1]], writes=[BP[pi]], signal=(kt == 1))
                            for mt in range(2):
                                pi = pis[mt]
                                jt = 2 * gi + mt
                                S.op("act", L("activation", out=zT[:, jt, blk * 512:(blk + 1) * 512], in_=P[pi][:, :], func=AF.Copy, scale=psc[:, jt:jt + 1]), reads=[BP[pi], Bwg], writes=[BpT[jt]])
                    def pool_sample(st2):
                        sp = sbt(st2, "sp", [NS, 15, 256], F32); Bsp = Buf()
                        ws = sbt(st2, "ws", [NS, D], F32); Bws = Buf()
                        for gi, w_ in enumerate(POOL_W):
                            cs = slice(gi * 256, (gi + 1) * 256)
                            S.dma(sp[:, 0:w_ - 1, :], st_pool[:, 15 - (w_ - 1):15, cs], writes=[Bsp])
                            S.op("dve", L("tensor_reduce", out=ws[:, cs], in_=sp[:, 0:w_ - 1, :].rearrange("p r c -> p c r"), axis=AX.X, op=ALU.add), reads=[Bsp], writes=[Bws])
                            S.op("dve", L("tensor_tensor", out=ws[:, cs], in0=ws[:, cs], in1=us[:, cs], op=ALU.add), reads=[Bws, Bus], writes=[Bws])
                            S.op("dve", L("scalar_tensor_tensor", out=ws[:, cs], in0=ws[:, cs], scalar=1.0 / w_, in1=us[:, cs], op0=ALU.mult, op1=ALU.subtract), reads=[Bws, Bus], writes=[Bws])
                        pTs, BpTs = to_T(st2, ws[:], Bws, 8, "pls")
                        zTs = sbt(st2, "zTs", [128, 8, NS], BF16); BzTs = Buf()
                        for gi in range(4):
                            for mt in range(2):
                                for kt in range(2):
                                    S.op("pe", L("matmul", P[4][:, 0:NS], lhsT=wgb[:, gi, kt, mt * 128:(mt + 1) * 128], rhs=pTs[:, 2 * gi + kt, :], start=(kt == 0), stop=(kt == 1)),
                                         reads=[Bwg, BpTs], writes=[BP[4]], signal=(kt == 1))
                                jt = 2 * gi + mt
                                S.op("act", L("activation", out=zTs[:, jt, :], in_=P[4][:, 0:NS], func=AF.Copy, scale=psc[:, jt:jt + 1]), reads=[BP[4], Bwg], writes=[BzTs])
                        S.dma(pools[:, 14, :], us[:], reads=[Bus])
                        return zTs, BzTs
                    proj_ln_phase(li, 0, zT, lambda t: BpT, 8, pwout, st, sample_fn=pool_sample)

        def sconv_phase(li):
            with ExitStack() as st0:
                yT = sbt(st0, "yT", [128, 8, T], BF16); ByT = Buf()
                gs3 = sbt(st0, "gs3", [NS, 8, 384], F32); Bgs3 = Buf()
                with ExitStack() as st:
                    xT, BxT = load_xT(st)
                    wst = sbt(st, "wst", [128, 8, 384], F32); Bwst = Buf()
                    wbf = [sbt(st, f"wbf{i}", [128, 8, 384], BF16) for i in range(2)]; Bwbf = [Buf(), Buf()]
                    skt = sbt(st, "skt", [128, 8, 3], F32); Bsk = Buf()
                    S.dma(skt[:], sk[:, :, :], writes=[Bsk])
                    pb2 = [sbt(st, f"pb2_{i}", [128, 514], F32) for i in range(2)]; Bpb2 = [Buf(), Buf()]
                    srow = sbt(st, "srow", [2, D], F32); Bsrow = Buf()
                    hb = [sbt(st, f"hb{i}", [128, 512], F32) for i in range(2)]; Bhb = [Buf(), Buf()]
                    tmp = [sbt(st, f"stmp{i}", [128, 512], F32) for i in range(2)]; Btmp = [Buf(), Buf()]
                    if with_sample:
                        xs_T, BxsT = load_xTs(st)
                    for j in range(8):
                        wi = j % 2
                        S.dma(wst[:], swin[j].rearrange("(k p) c -> p k c", p=128), writes=[Bwst])
                        cast(wbf[wi][:], wst[:], [Bwst], [Bwbf[wi]])
                        if with_sample:
                            for k in range(8):
                                S.op("pe", L("matmul", P[5][0:NS, 0:384], lhsT=xs_T[:, k, :], rhs=wbf[wi][:, k, :], start=(k == 0), stop=(k == 7)),
                                     reads=[Bwbf[wi], BxsT], writes=[BP[5]], signal=(k == 7))
                            S.op("act", L("copy", out=gs3[:, j, :], in_=P[5][0:NS, 0:384]), reads=[BP[5]], writes=[Bgs3])
                        for blk in range(8):
                            i = blk % 2
                            pis = (0, 1, 2) if i == 0 else (3, 4, 5)
                            for s_, pi in enumerate(pis):
                                for k in range(8):
                                    S.op("pe", L("matmul", P[pi][:, :], lhsT=wbf[wi][:, k, s_ * 128:(s_ + 1) * 128], rhs=xT[:, k, blk * 512:(blk + 1) * 512], start=(k == 0), stop=(k == 7)),
                                         reads=[Bwbf[wi]] + BxT[4 * blk:4 * blk + 4], writes=[BP[pi]], signal=(k == 7))
                            pb_ = pb2[i]
                            if blk == 0:
                                S.op("pool", L("memset", pb_[:, 0:2], 0.0), writes=[Bpb2[i]])
                            else:
                                S.op("pool", L("tensor_copy", out=pb_[:, 0:2], in_=pb2[1 - i][:, 512:514]), reads=[Bpb2[1 - i]], writes=[Bpb2[i]])
                            S.op("act", L("copy", out=hb[i][:], in_=P[pis[2]][:, :]), reads=[BP[pis[2]]], writes=[Bhb[i]])
                            S.op("dve", L("tensor_tensor", out=pb_[:, 2:514], in0=P[pis[1]][:, :], in1=hb[i][:], op=ALU.mult), reads=[BP[pis[1]], Bhb[i]], writes=[Bpb2[i]])
                            tm = tmp[i]
                            S.op("dve", L("tensor_scalar", out=tm[:], in0=pb_[:, 0:512], scalar1=skt[:, j, 0:1], scalar2=None, op0=ALU.mult), reads=[Bpb2[i], Bsk], writes=[Btmp[i]])
                            S.op("dve", L("scalar_tensor_tensor", out=tm[:], in0=pb_[:, 1:513], scalar=skt[:, j, 1:2], in1=tm[:], op0=ALU.mult, op1=ALU.add), reads=[Bpb2[i], Bsk, Btmp[i]], writes=[Btmp[i]])
                            S.op("dve", L("scalar_tensor_tensor", out=tm[:], in0=pb_[:, 2:514], scalar=skt[:, j, 2:3], in1=tm[:], op0=ALU.mult, op1=ALU.add), reads=[Bpb2[i], Bsk, Btmp[i]], writes=[Btmp[i]])
                            S.op("dve", L("tensor_tensor", out=yT[:, j, blk * 512:(blk + 1) * 512], in0=P[pis[0]][:, :], in1=tm[:], op=ALU.mult), reads=[BP[pis[0]], Btmp[i]], writes=[ByT])
                            if blk == 7:
                                fm_block_to_rows(pb_[:, 512:514], Bpb2[i], 2, srow, Bsrow, j)
                    S.dma(sconvp[:, :], srow[:], reads=[Bsrow])
                with ExitStack() as st:
                    def sconv_sample(st2):
                        sst = sbt(st2, "sst", [NS, 2, D], F32); Bsst = Buf()
                        S.dma(sst[:], st_sconv[:, :, :], writes=[Bsst])
                        ktm = sbt(st2, "sktm", [NS, 3, D], F32); Bktm = Buf()
                        for tap in range(3):
                            S.dma(ktm[:, tap, :], sk_tm[tap:tap + 1, :].partition_broadcast(NS), writes=[Bktm])
                        p_s = sbt(st2, "p_s", [NS, D], F32); t1 = sbt(st2, "t1", [NS, D], F32); t2 = sbt(st2, "t2", [NS, D], F32)
                        Bp_s, Bt1, Bt2 = Buf(), Buf(), Buf()
                        tt_ = lambda o, a, b, op, rd, wr: S.op("dve", L("tensor_tensor", out=o, in0=a, in1=b, op=op), reads=rd, writes=wr)
                        v3 = lambda t_: t_[:].rearrange("p (j c) -> p j c", j=8)
                        tt_(v3(p_s), gs3[:, :, 128:256], gs3[:, :, 256:384], ALU.mult, [Bgs3], [Bp_s])
                        tt_(t1[:], sst[:, 0, :], ktm[:, 0, :], ALU.mult, [Bsst, Bktm], [Bt1])
                        tt_(t2[:], sst[:, 1, :], ktm[:, 1, :], ALU.mult, [Bsst, Bktm], [Bt2])
                        tt_(t1[:], t1[:], t2[:], ALU.add, [Bt1, Bt2], [Bt1])
                        tt_(t2[:], p_s[:], ktm[:, 2, :], ALU.mult, [Bp_s, Bktm, Bt1], [Bt2])
                        tt_(t1[:], t1[:], t2[:], ALU.add, [Bt1, Bt2], [Bt1])
                        tt_(v3(t1), v3(t1), gs3[:, :, 0:128], ALU.mult, [Bt1, Bgs3], [Bt1])
                        S.dma(sconvs[:, 1, :], p_s[:], reads=[Bp_s])
                        return to_T(st2, t1[:], Bt1, 8, "scs")
                    proj_ln_phase(li, 0, yT, lambda t: [ByT], 8, swout, st, sample_fn=sconv_sample)

        on = lambda name: phases is None or name in phases
        if on("init"):
            init_phase()
        ia = 0
        for li in range(DEPTH):
            kind = li % 3
            if kind == 0:
                if on(f"mix{li}"):
                    attn_phase(li, ia)
                ia += 1
            elif kind == 1:
                if on(f"mix{li}"):
                    pool_phase(li)
            else:
                if on(f"mix{li}"):
                    sconv_phase(li)
            if on(f"ffn{li}"):
                ffn_phase(li)
        while bg:
            S.dma(*bg.pop(0))
        S.final_wait_dmas()
        print("ops recorded:", S.nops, {e: len(v) for e, v in S.ops.items()})
        S.replay()
    return nc


def _rope_tables():
    half = 32
    inv = (np.float32(10000.0) ** (-np.arange(half, dtype=np.float32) / np.float32(half))).astype(np.float32)
    out = {}
    for (_, d) in GROUPS:
        nblk = NT // d
        tab = np.zeros((128, NT, 2, 32), np.float32)
        for blk in range(NT):
            r, nb = blk // nblk, blk % nblk
            pos = (r + d * (128 * nb + np.arange(128))).astype(np.float32)
            ang = pos[:, None] * inv[None, :]
            tab[:, blk, 0, :] = np.cos(ang)
            tab[:, blk, 1, :] = np.sin(ang)
        out[d] = tab
    angs = (np.float32(PAST) * inv).astype(np.float32)
    rs = np.zeros((NS, 2, 32), np.float32)
    rs[:, 0, :] = np.cos(angs)[None]
    rs[:, 1, :] = np.sin(angs)[None]
    return out, rs


def _consts():
    j = np.arange(128)[:, None]
    c = np.arange(256)[None, :]
    band = np.where(c < 128, (c >= j), ((c - 128) <= j)).astype(np.float32)
    ident = np.eye(128, dtype=np.float32)
    prcnt = np.zeros((128, 8, 16), np.float32)
    for jt in range(8):
        w = POOL_W[jt // 2]
        cnt = np.minimum(w, np.arange(16) + 1).astype(np.float32)
        prcnt[:, jt, :] = (1.0 / cnt)[None, :] / np.float32(w) * np.float32(w) / 1.0
    return band, ident, prcnt


_NC_CACHE = {}


def kernel(x_prompt, x_sample, cache_kv_w128, cache_kv_w512, cache_kv_w2048,
           state_pool, state_sconv, state_ffn_conv,
           attn_w_qkv, attn_w_o, pool_w_in, pool_w_grp, pool_scale, pool_w_out,
           sconv_w_in, sconv_k, sconv_w_out, ffn_w_up, ffn_k, ffn_w_down, ln_g, ln_b):
    f = lambda a: np.ascontiguousarray(np.asarray(a, dtype=np.float32))
    if "nc" not in _NC_CACHE:
        _NC_CACHE["nc"] = build()
    nc = _NC_CACHE["nc"]
    ropes, rope_s = _rope_tables()
    band, ident, prcnt = _consts()
    wq = f(attn_w_qkv).reshape(2, D, 3, 3, 8, 2, 64)
    wqkv = np.ascontiguousarray(wq.transpose(0, 4, 2, 1, 3, 5, 6)).reshape(2, 8, 3, D, 384)
    wu = f(ffn_w_up).reshape(DEPTH, D, 2, NJ, 128)
    wup = np.ascontiguousarray(wu.transpose(0, 3, 1, 2, 4)).reshape(DEPTH, NJ, D, 256)
    ffk = np.ascontiguousarray(f(ffn_k).reshape(DEPTH, 3, NJ, 128).transpose(0, 3, 2, 1))
    sw = f(sconv_w_in)[0].reshape(D, 3, 8, 128)
    swin = np.ascontiguousarray(sw.transpose(2, 0, 1, 3)).reshape(8, D, 384)
    skf = np.ascontiguousarray(f(sconv_k)[0].reshape(3, 8, 128).transpose(2, 1, 0))
    psc = np.ascontiguousarray(f(pool_scale)[0].reshape(8, 128).T)
    common = {
        "wqkv": wqkv, "wo": f(attn_w_o), "wup": wup, "wdn": f(ffn_w_down), "ffk": ffk, "ffk_tm": f(ffn_k),
        "pwin": f(pool_w_in)[0], "pwgrp": f(pool_w_grp)[0], "pscale": psc, "pwout": f(pool_w_out)[0], "prcnt": prcnt,
        "swin": swin, "sk": skf, "sk_tm": f(sconv_k)[0], "swout": f(sconv_w_out)[0],
        "lng": f(ln_g), "lnb": f(ln_b), "rope_s": rope_s, "band": band, "ident": ident,
        "sel": np.ascontiguousarray(np.broadcast_to(np.eye(NS, dtype=np.float32)[:, :, None], (NS, NS, 128))),
        "oneh": np.ascontiguousarray(np.broadcast_to(np.eye(NS, dtype=np.float32)[None, :, :], (128, NS, NS))),
    }
    for (_, d) in GROUPS:
        common[f"rope{d}"] = ropes[d]
    caches = {128: f(cache_kv_w128), 512: f(cache_kv_w512), 2048: f(cache_kv_w2048)}
    in_maps = []
    for c in range(8):
        seq = c // 2
        ss = slice(NS * c, NS * (c + 1))
        m = dict(common)
        m["x"] = f(x_prompt[seq])
        m["xs"] = f(x_sample[ss, 0, :])
        for w in (128, 512, 2048):
            m[f"c{w}"] = np.ascontiguousarray(caches[w][:, ss].reshape(2, NS, w, 2048))
        m["st_pool"] = f(state_pool[0, ss])
        m["st_sconv"] = f(state_sconv[0, ss])
        m["st_ffn"] = f(state_ffn_conv[:, ss])
        in_maps.append(m)
    res = run_bass_kernel_spmd(nc, in_maps, core_ids=list(range(8)))
    R = res.results
    B = 4
    y_prompt = np.stack([R[2 * b]["y"] for b in range(B)])
    y_sample = np.concatenate([R[c]["ys"] for c in range(8)], 0).reshape(32, 1, D)
    outs = [y_prompt, y_sample]
    for (w, _) in GROUPS:
        kp = np.stack([R[2 * b][f"kv{w}p"] for b in range(B)], 1).reshape(2, B, w, 2, 16, 64)
        ks = np.concatenate([R[c][f"kv{w}s"] for c in range(8)], 1).reshape(2, 32, w, 2, 16, 64)
        outs += [kp, ks]
    outs.append(np.stack([R[2 * b]["poolp"] for b in range(B)])[None])
    outs.append(np.concatenate([R[c]["pools"] for c in range(8)], 0)[None])
    outs.append(np.stack([R[2 * b]["sconvp"] for b in range(B)])[None])
    outs.append(np.concatenate([R[c]["sconvs"] for c in range(8)], 0)[None])
    outs.append(np.stack([R[2 * b]["ffnp"] for b in range(B)], 1))
    outs.append(np.concatenate([R[c]["ffns"] for c in range(8)], 1))
    return tuple(np.ascontiguousarray(o, dtype=np.float32) for o in outs)
```

```python
import numpy as np
import concourse.bass as bass
import concourse.mybir as mybir
from concourse.bass_utils import run_bass_kernel_spmd

F32 = mybir.dt.float32
BF16 = mybir.dt.bfloat16
ALU = mybir.AluOpType
AF = mybir.ActivationFunctionType
AX = mybir.AxisListType


class Buf:
    __slots__ = ("name", "w", "r", "excl")

    def __init__(self, name="", excl=False):
        self.name = name
        self.w = None
        self.r = []
        self.excl = excl


class Ev:
    __slots__ = ("eng", "sem", "val", "resolved")

    def __init__(self, eng):
        self.eng = eng
        self.sem = None
        self.val = None
        self.resolved = False


def L(method, *args, **kw):
    return lambda e: getattr(e, method)(*args, **kw)


class Sched:
    ROLL = 30000
    NDMA = 24

    def __init__(self, nc, stack):
        self.nc = nc
        self.stack = stack
        self.engs = ["pe", "act", "dve", "pool", "sp"]
        self.ops = {e: [] for e in self.engs}
        self.cur_sem = {}
        self.cur_cnt = {}
        self.pending = {e: [] for e in self.engs}
        self.waited = {e: {} for e in self.engs}
        for e in self.engs:
            self._new_sem(e)
        self.dma_sems = [stack.enter_context(nc.semaphore(f"dq{i}")) for i in range(self.NDMA)]
        self.dma_cnt = [0] * self.NDMA
        self.dma_rr = 0
        self.nops = 0

    def _new_sem(self, e):
        self.cur_sem[e] = self.stack.enter_context(self.nc.semaphore(f"s_{e}_{len(self.ops[e])}"))
        self.cur_cnt[e] = 0

    def _collect(self, eng, reads, writes):
        evs = []
        for b in reads:
            if b.w is not None:
                evs.append(b.w)
            if b.excl:
                evs.extend(e for e in b.r if e.eng != eng)
        for b in writes:
            if b.w is not None:
                evs.append(b.w)
            evs.extend(b.r)
        need = {}
        for ev in evs:
            if ev.eng == eng and ev.eng == "pe":
                continue
            if not ev.resolved:
                if ev.eng == eng:
                    continue
                raise RuntimeError("dependency on unresolved (unsignaled) event")
            sid = id(ev.sem)
            if sid not in need or need[sid][1] < ev.val:
                need[sid] = (ev.sem, ev.val)
        waits = []
        for sid, (sem, val) in need.items():
            if self.waited[eng].get(sid, -1) >= val:
                continue
            self.waited[eng][sid] = val
            waits.append((sem, val))
        return waits

    def _mark(self, ev, reads, writes):
        for b in reads:
            b.r.append(ev)
        for b in writes:
            b.w = ev
            b.r = []

    def op(self, eng, fn, reads=(), writes=(), signal=True):
        self.nops += 1
        waits = self._collect(eng, reads, writes)
        ev = Ev(eng)
        if signal:
            if self.cur_cnt[eng] >= self.ROLL:
                self._new_sem(eng)
            self.cur_cnt[eng] += 1
            ev.sem = self.cur_sem[eng]
            ev.val = self.cur_cnt[eng]
            ev.resolved = True
            for p in self.pending[eng]:
                p.sem, p.val, p.resolved = ev.sem, ev.val, True
            self.pending[eng] = []
            self.ops[eng].append((waits, fn, (ev.sem, 1)))
        else:
            self.pending[eng].append(ev)
            self.ops[eng].append((waits, fn, None))
        self._mark(ev, reads, writes)
        return ev

    def dma(self, out_ap, in_ap, reads=(), writes=(), eng="sp", **kw):
        self.nops += 1
        waits = self._collect(eng, reads, writes)
        s = self.dma_rr
        self.dma_rr = (self.dma_rr + 1) % self.NDMA
        sem = self.dma_sems[s]
        if self.dma_cnt[s] > 0:
            sid = id(sem)
            if self.waited[eng].get(sid, -1) < self.dma_cnt[s]:
                self.waited[eng][sid] = self.dma_cnt[s]
                waits.append((sem, self.dma_cnt[s]))
        self.dma_cnt[s] += 16
        ev = Ev("dma")
        ev.sem, ev.val, ev.resolved = sem, self.dma_cnt[s], True
        fn = L("dma_start", out=out_ap, in_=in_ap, **kw)
        self.ops[eng].append((waits, fn, (sem, 16)))
        self._mark(ev, reads, writes)
        return ev

    def wait_all(self, eng, evs):
        waits = []
        for ev in evs:
            waits.append((ev.sem, ev.val))
        self.ops[eng].append((waits, None, None))

    def final_wait_dmas(self, eng="sp"):
        waits = [(self.dma_sems[s], self.dma_cnt[s]) for s in range(self.NDMA) if self.dma_cnt[s] > 0]
        self.ops[eng].append((waits, None, None))

    def replay(self):
        nc = self.nc
        with nc.Block() as block:
            def run(e_name):
                def body(eng):
                    for waits, fn, sig in self.ops[e_name]:
                        for (sem, val) in waits:
                            eng.wait_ge(sem, val)
                        if fn is None:
                            continue
                        ins = fn(eng)
                        if sig is not None:
                            ins.then_inc(sig[0], sig[1])
                return body
            block.tensor(run("pe"))
            block.scalar(run("act"))
            block.vector(run("dve"))
            block.gpsimd(run("pool"))
            block.sync(run("sp"))

from contextlib import ExitStack

T = 4096
NT = 32
D = 1024
DFF = 2816
NJ = 22
DEPTH = 4
ALPHA = (2.0 * DEPTH) ** 0.25
LN_EPS = 1e-5
GROUPS = ((128, 1), (512, 4), (2048, 16))
POOL_W = (2, 4, 8, 16)
PAST = 8192
NS = 4
FCH = (12, 12, 8)
DBG = {}


def build(with_sample=True, phases=None):
    nc = bass.Bass("TRN2", target_bir_lowering=False)
    din = lambda n, shp: nc.dram_tensor(n, list(shp), F32, kind="ExternalInput").ap()
    dout = lambda n, shp: nc.dram_tensor(n, list(shp), F32, kind="ExternalOutput").ap()
    x_in = din("x", [T, D]); xs_in = din("xs", [NS, D])
    wqkv = din("wqkv", [2, 8, 3, D, 384]); wo = din("wo", [2, D, D])
    wup = din("wup", [DEPTH, NJ, D, 256]); wdn = din("wdn", [DEPTH, DFF, D]); ffk = din("ffk", [DEPTH, 128, NJ, 3])
    ffk_tm = din("ffk_tm", [DEPTH, 3, DFF])
    pwin = din("pwin", [D, D]); pwgrp = din("pwgrp", [4, 256, 256]); pscale = din("pscale", [128, 8]); pwout = din("pwout", [D, D])
    prcnt = din("prcnt", [128, 8, 16])
    swin = din("swin", [8, D, 384]); sk = din("sk", [128, 8, 3]); sk_tm = din("sk_tm", [3, D]); swout = din("swout", [D, D])
    lng = din("lng", [DEPTH, 2, D]); lnb = din("lnb", [DEPTH, 2, D])
    rope = [din(f"rope{d}", [128, NT, 2, 32]) for (_, d) in GROUPS]
    rope_s = din("rope_s", [NS, 2, 32])
    band_in = din("band", [128, 256]); ident_in = din("ident", [128, 128])
    sel_in = din("sel", [NS, NS, 128]); oneh_in = din("oneh", [128, NS, NS])
    c_in = [din(f"c{w}", [2, NS, w, 2048]) for (w, _) in GROUPS]
    st_pool = din("st_pool", [NS, 15, D]); st_sconv = din("st_sconv", [NS, 2, D]); st_ffn = din("st_ffn", [DEPTH, NS, 2, DFF])

    y_out = dout("y", [T, D]); ys_out = dout("ys", [NS, D])
    kvp = [dout(f"kv{w}p", [2, w, 2048]) for (w, _) in GROUPS]
    kvs = [dout(f"kv{w}s", [2, NS, w, 2048]) for (w, _) in GROUPS]
    poolp = dout("poolp", [15, D]); pools = dout("pools", [NS, 15, D])
    sconvp = dout("sconvp", [2, D]); sconvs = dout("sconvs", [NS, 2, D])
    ffnp = dout("ffnp", [DEPTH, 2, DFF]); ffns = dout("ffns", [DEPTH, NS, 2, DFF])

    xres = nc.dram_tensor("xres_scr", [T, D], F32).ap()
    xT_scr = nc.dram_tensor("xT_scr", [128, 8, T], BF16).ap()
    mixT_scr = nc.dram_tensor("mixT_scr", [128, 8, T], BF16).ap()
    xres_s = nc.dram_tensor("xres_s_scr", [NS, D], F32).ap()
    xTs_scr = nc.dram_tensor("xTs_scr", [128, 8, NS], BF16).ap()
    qkvs_scr = nc.dram_tensor("qkvs_scr", [NS, 3, 3, D], F32).ap()
    ab_scr = nc.dram_tensor("ab_scr", [NS, 2, DFF], F32).ap()

    top = ExitStack()
    with top:
        S = Sched(nc, top)
        P = [top.enter_context(nc.psum_tensor(f"P{i}", [128, 512], F32)) for i in range(6)]
        PB = [top.enter_context(nc.psum_tensor(f"PB{i}", [128, 1024], BF16)) for i in range(2)]
        BP = [Buf(f"P{i}", excl=True) for i in range(6)]
        BPB = [Buf(f"PB{i}", excl=True) for i in range(2)]
        uid = [0]
        def sbt(st, name, shape, dt):
            uid[0] += 1
            return st.enter_context(nc.sbuf_tensor(f"sb_{name}_{uid[0]}", list(shape), dt))
        idf = sbt(top, "idf", [128, 128], F32); idb = sbt(top, "idb", [128, 128], BF16)
        bandf = sbt(top, "bandf", [128, 256], F32); band = sbt(top, "band", [128, 256], BF16)
        onesb = sbt(top, "onesb", [128, 64], BF16)
        Bconst = Buf("const")
        S.dma(idf[:], ident_in[:, :], writes=[Bconst])
        S.dma(bandf[:], band_in[:, :], writes=[Bconst])
        S.op("dve", L("tensor_copy", out=idb[:], in_=idf[:]), reads=[Bconst], writes=[Bconst])
        S.op("dve", L("tensor_copy", out=band[:], in_=bandf[:]), reads=[Bconst], writes=[Bconst])
        S.op("dve", L("memset", onesb[:], 1.0), writes=[Bconst])
        mbias = sbt(top, "mbias", [128, 256], BF16)
        S.op("dve", L("tensor_scalar", out=bandf[:], in0=bandf[:], scalar1=-1.0, scalar2=30000.0, op0=ALU.add, op1=ALU.mult), reads=[Bconst], writes=[Bconst])
        S.op("dve", L("tensor_copy", out=mbias[:], in_=bandf[:]), reads=[Bconst], writes=[Bconst])
        Bxres = [Buf(f"xres{t}") for t in range(NT)]
        BxTs = [Buf(f"xTs{t}") for t in range(NT)]
        BmixT = [Buf(f"mixT{h}") for h in range(8)]
        Bxres_s = Buf("xres_s"); BxTs_s = Buf("xTs_s"); Bqkvs = Buf("qkvs"); Babs = Buf("abs")
        sel_t = sbt(top, "sel", [NS, NS, 128], F32); oneh_t = sbt(top, "oneh", [128, NS, NS], F32)
        S.dma(sel_t[:], sel_in[:, :, :], writes=[Bconst])
        S.dma(oneh_t[:], oneh_in[:, :, :], writes=[Bconst])

        def load_xTs(st):
            t_ = sbt(st, "xTs", [128, 8, NS], BF16); B_ = Buf()
            S.dma(t_[:], xTs_scr[:, :, :], reads=[BxTs_s], writes=[B_])
            return t_, B_

        def to_T(st, src, Bsrc, n, name):
            sbb = sbt(st, name + "_b", [NS, n * 128], BF16); Bb = Buf()
            S.op("act", L("copy", out=sbb[:], in_=src), reads=[Bsrc], writes=[Bb])
            dst = sbt(st, name + "_T", [128, n, NS], BF16); Bd = Buf()
            for k in range(n):
                S.op("pe", L("transpose", out=PB[0][:, k * NS:(k + 1) * NS], in_=sbb[:, k * 128:(k + 1) * 128], identity=idb[0:NS, 0:NS]),
                     reads=[Bb, Bconst], writes=[BPB[0]], signal=(k == n - 1))
            S.op("dve", L("tensor_copy", out=dst[:].rearrange("p n s -> p (n s)"), in_=PB[0][:, 0:n * NS]), reads=[BPB[0]], writes=[Bd])
            return dst, Bd

        def ln_sample(ln, li, which, actT_s, Bact_s, nk, w, Bw):
            for c, pi in enumerate((4, 5)):
                for k in range(nk):
                    S.op("pe", L("matmul", P[pi][0:NS, :], lhsT=actT_s[:, k, :], rhs=w[:, k, c * 512:(c + 1) * 512], start=(k == 0), stop=(k == nk - 1)),
                         reads=[Bw, Bact_s], writes=[BP[pi]], signal=(k == nk - 1))
            src = xs_in[:, :] if (li == 0 and which == 0) else xres_s[:, :]
            srcb = [] if (li == 0 and which == 0) else [Bxres_s]
            dst = ys_out[:, :] if (li == DEPTH - 1 and which == 1) else xres_s[:, :]
            ln.tile(P[4][0:NS, :], P[5][0:NS, :], BP[4], BP[5], src, srcb, dst, [Bxres_s], xTs_scr[:, :, :], [BxTs_s], np_=NS)
        rr = {"cast": 0}
        bg = []

        def cast(out_ap, in_ap, reads, writes):
            eng = ("pool", "act")[rr["cast"] % 2]
            rr["cast"] += 1
            if eng == "act":
                return S.op("act", L("copy", out=out_ap, in_=in_ap), reads=reads, writes=writes)
            return S.op("pool", L("tensor_copy", out=out_ap, in_=in_ap), reads=reads, writes=writes)

        def load_xT(st):
            xT = sbt(st, "xT", [128, 8, T], BF16)
            BxT = [Buf(f"xT{t}") for t in range(NT)]
            for q in range(8):
                S.dma(xT[:, :, q * 512:(q + 1) * 512], xT_scr[:, :, q * 512:(q + 1) * 512],
                      reads=BxTs[4 * q:4 * q + 4], writes=BxT[4 * q:4 * q + 4])
            return xT, BxT

        def fm_block_to_rows(src_ap, Bsrc, r, rowbuf, Brow, j):
            S.op("pe", L("matmul", P[5][0:r, 0:128], lhsT=src_ap, rhs=idf[:, :], start=True, stop=True), reads=[Bsrc, Bconst], writes=[BP[5]])
            S.op("dve", L("tensor_copy", out=rowbuf[0:r, j * 128:(j + 1) * 128], in_=P[5][0:r, 0:128]), reads=[BP[5]], writes=[Brow])

        class LN:
            def __init__(self, st, li, which):
                self.xr = [sbt(st, f"ln_xr{i}", [128, D], F32) for i in range(2)]
                self.xb = [sbt(st, f"ln_xb{i}", [128, D], BF16) for i in range(2)]
                self.stats = [sbt(st, f"ln_st{i}", [128, 2, 6], F32) for i in range(2)]
                self.mv = [sbt(st, f"ln_mv{i}", [128, 2], F32) for i in range(2)]
                self.rs = [sbt(st, f"ln_rs{i}", [128, 1], F32) for i in range(2)]
                self.xo = [sbt(st, f"ln_xo{i}", [128, 8, 128], BF16) for i in range(2)]
                self.gb = sbt(st, "ln_gb", [128, 2, D], F32)
                self.B = [[Buf() for _ in range(6)] for _ in range(2)]
                self.Bgb = Buf()
                S.dma(self.gb[:, 0, :], lng[li, which:which + 1, :].partition_broadcast(128), writes=[self.Bgb])
                S.dma(self.gb[:, 1, :], lnb[li, which:which + 1, :].partition_broadcast(128), writes=[self.Bgb])
                self.n = 0

            def tile(self, psA, psB, BpA, BpB, src_ap, src_bufs, dst_ap, dst_bufs, xT_dst_ap, xT_bufs, np_=128):
                i = self.n % 2
                self.n += 1
                xr, xb, stats, mv, rs, xo = self.xr[i], self.xb[i], self.stats[i], self.mv[i], self.rs[i], self.xo[i]
                Bx, Bb, Bs, Bm, Br, Bo = self.B[i]
                S.dma(xr[0:np_, :], src_ap, reads=src_bufs, writes=[Bx])
                for c, (ps, Bp) in enumerate(((psA, BpA), (psB, BpB))):
                    S.op("dve", L("scalar_tensor_tensor", out=xr[0:np_, c * 512:(c + 1) * 512], in0=xr[0:np_, c * 512:(c + 1) * 512], scalar=ALPHA, in1=ps, op0=ALU.mult, op1=ALU.add),
                         reads=[Bx, Bp], writes=[Bx])
                for c in range(2):
                    S.op("dve", L("bn_stats", out=stats[0:np_, c, :], in_=xr[0:np_, c * 512:(c + 1) * 512]), reads=[Bx], writes=[Bs])
                S.op("dve", L("bn_aggr", out=mv[0:np_, :], in_=stats[0:np_]), reads=[Bs], writes=[Bm])
                S.op("act", L("activation", out=rs[0:np_, :], in_=mv[0:np_, 1:2], func=AF.Sqrt, bias=LN_EPS, scale=1.0), reads=[Bm], writes=[Br])
                S.op("dve", L("reciprocal", out=rs[0:np_, :], in_=rs[0:np_, :]), reads=[Br], writes=[Br])
                S.op("dve", L("tensor_scalar", out=mv[0:np_, 1:2], in0=mv[0:np_, 0:1], scalar1=rs[0:np_, 0:1], scalar2=-1.0, op0=ALU.mult, op1=ALU.mult), reads=[Bm, Br], writes=[Bm])
                S.op("act", L("activation", out=xr[0:np_, :], in_=xr[0:np_, :], func=AF.Identity, bias=mv[0:np_, 1:2], scale=rs[0:np_, 0:1]), reads=[Bx, Bm, Br], writes=[Bx])
                S.op("dve", L("tensor_tensor", out=xr[0:np_, :], in0=xr[0:np_, :], in1=self.gb[0:np_, 0, :], op=ALU.mult), reads=[Bx, self.Bgb], writes=[Bx])
                S.op("pool", L("tensor_tensor", out=xr[0:np_, :], in0=xr[0:np_, :], in1=self.gb[0:np_, 1, :], op=ALU.add), reads=[Bx, self.Bgb], writes=[Bx])
                S.dma(dst_ap, xr[0:np_, :], reads=[Bx], writes=dst_bufs)
                S.op("act", L("copy", out=xb[0:np_, :], in_=xr[0:np_, :]), reads=[Bx], writes=[Bb])
                pb = PB[i]
                for k in range(8):
                    S.op("pe", L("transpose", out=pb[:, k * 128:k * 128 + np_], in_=xb[0:np_, k * 128:(k + 1) * 128], identity=idb[0:np_, 0:np_]),
                         reads=[Bb, Bconst], writes=[BPB[i]], signal=(k == 7))
                S.op("dve", L("tensor_copy", out=xo[:, :, 0:np_], in_=pb[:].rearrange("p (k n) -> p k n", k=8)[:, :, 0:np_]), reads=[BPB[i]], writes=[Bo])
                S.dma(xT_dst_ap, xo[:, :, 0:np_], reads=[Bo], writes=xT_bufs)

        def ln_dst(li, which, t):
            if li == DEPTH - 1 and which == 1:
                return y_out[t * 128:(t + 1) * 128, :]
            return xres[t * 128:(t + 1) * 128, :]

        def ln_src(li, which, t):
            if li == 0 and which == 0:
                return x_in[t * 128:(t + 1) * 128, :], []
            return xres[t * 128:(t + 1) * 128, :], [Bxres[t]]

        def load_wrow(w, Bw, src, k, stgs):
            for c in range(2):
                stg_, Bstg_ = stgs[c]
                S.dma(stg_[:, 0:512], src[k * 128:(k + 1) * 128, c * 512:(c + 1) * 512], writes=[Bstg_])
                cast(w[:, k, c * 512:(c + 1) * 512], stg_[:, 0:512], [Bstg_], [Bw])

        def load_wfull(st, name, src, nk, stg, Bstg, dbl=True):
            w = sbt(st, name, [128, nk, D], BF16)
            Bw = Buf(name)
            if dbl:
                stg2 = sbt(st, "wf_stg2", [128, 512], F32); Bstg2 = Buf()
                stgs = ((stg, Bstg), (stg2, Bstg2))
            else:
                stgs = ((stg, Bstg), (stg, Bstg))
            for k in range(nk):
                load_wrow(w, Bw, src, k, stgs)
            return w, Bw

        def proj_ln_phase(li, which, actT, BactT_fn, nk, w_src, st, sample_fn=None):
            stg = sbt(st, "wf_stg", [128, 512], F32); Bstg = Buf()
            w, Bw = load_wfull(st, "wfull", w_src, nk, stg, Bstg)
            ln = LN(st, li, which)
            def emit_mm(t):
                pa, pb_ = (0, 1) if t % 2 == 0 else (2, 3)
                for c, pi in enumerate((pa, pb_)):
                    for k in range(nk):
                        S.op("pe", L("matmul", P[pi][:, :], lhsT=actT[:, k, t * 128:(t + 1) * 128], rhs=w[:, k, c * 512:(c + 1) * 512], start=(k == 0), stop=(k == nk - 1)),
                             reads=[Bw] + BactT_fn(t), writes=[BP[pi]], signal=(k == nk - 1))
            emit_mm(0)
            for t in range(NT):
                pa, pb_ = (0, 1) if t % 2 == 0 else (2, 3)
                if t + 1 < NT:
                    emit_mm(t + 1)
                src, sb_ = ln_src(li, which, t)
                ln.tile(P[pa][:, :], P[pb_][:, :], BP[pa], BP[pb_], src, sb_, ln_dst(li, which, t), [Bxres[t]],
                        xT_scr[:, :, t * 128:(t + 1) * 128], [BxTs[t]])
            if sample_fn is not None and with_sample:
                aT, BaT = sample_fn(st)
                ln_sample(ln, li, which, aT, BaT, nk, w, Bw)

        def init_phase():
            with ExitStack() as st:
                xr = [sbt(st, f"i_xr{i}", [128, D], F32) for i in range(2)]
                xb = [sbt(st, f"i_xb{i}", [128, D], BF16) for i in range(2)]
                xo = [sbt(st, f"i_xo{i}", [128, 8, 128], BF16) for i in range(2)]
                Bs = [[Buf() for _ in range(3)] for _ in range(2)]
                for t in range(NT):
                    i = t % 2
                    S.dma(xr[i][:], x_in[t * 128:(t + 1) * 128, :], writes=[Bs[i][0]])
                    S.op("act", L("copy", out=xb[i][:], in_=xr[i][:]), reads=[Bs[i][0]], writes=[Bs[i][1]])
                    for k in range(8):
                        S.op("pe", L("transpose", out=PB[i][:, k * 128:(k + 1) * 128], in_=xb[i][:, k * 128:(k + 1) * 128], identity=idb[:]),
                             reads=[Bs[i][1], Bconst], writes=[BPB[i]], signal=(k == 7))
                    S.op("dve", L("tensor_copy", out=xo[i][:], in_=PB[i][:].rearrange("p (k n) -> p k n", k=8)), reads=[BPB[i]], writes=[Bs[i][2]])
                    S.dma(xT_scr[:, :, t * 128:(t + 1) * 128], xo[i][:], reads=[Bs[i][2]], writes=[BxTs[t]])
                if with_sample:
                    xs0 = sbt(st, "xs0", [NS, D], F32); Bxs0 = Buf()
                    S.dma(xs0[:], xs_in[:, :], writes=[Bxs0])
                    xsT, BxsT0 = to_T(st, xs0[:], Bxs0, 8, "xs0")
                    S.dma(xTs_scr[:, :, :], xsT[:], reads=[BxsT0], writes=[BxTs_s])
                    for gi, (win, d) in enumerate(GROUPS):
                        for ia_ in range(2):
                            for s_ in range(NS):
                                bg.append((kvs[gi][ia_, s_, 0:win - 1, :], c_in[gi][ia_, s_, 1:win, :]))
                    for s_ in range(NS):
                        S.dma(pools[s_, 0:14, :], st_pool[s_, 1:15, :])
                    S.dma(sconvs[:, 0, :], st_sconv[:, 1, :])
                    for l_ in range(DEPTH):
                        S.dma(ffns[l_, :, 0, :], st_ffn[l_, :, 1, :])

        def ffn_phase(li):
            with ExitStack() as st:
                wd = sbt(st, "wdn", [128, NJ, D], BF16); Bwd = Buf("wdn")
                wd_stgs = tuple((sbt(st, f"wf_stg{i}", [128, 512], F32), Buf()) for i in range(2))
                fk = sbt(st, "fk", [128, NJ, 3], F32); Bfk = Buf()
                S.dma(fk[:], ffk[li], writes=[Bfk])
                ah = sbt(st, "ah", [128, NJ, 2], F32); Bah = Buf()
                S.op("dve", L("memset", ah[:], 0.0), writes=[Bah])
                ln = LN(st, li, 1)
                if with_sample:
                    xs_T, BxsT = load_xTs(st)
                    abt = [sbt(st, f"abt{i}", [NS, 256], F32) for i in range(2)]; Babt = [Buf(), Buf()]
                sti = ExitStack()
                wst2 = [sbt(sti, f"wst{i}", [128, 8, 256], F32) for i in range(2)]; Bwst2 = [Buf(), Buf()]
                wbf = [sbt(sti, f"wbf{i}", [128, 8, 256], BF16) for i in range(2)]; Bwbf = [Buf(), Buf()]
                asb = [sbt(sti, f"asb{i}", [128, 514], F32) for i in range(2)]; Basb = [Buf(), Buf()]
                tmp = [sbt(sti, f"ftmp{i}", [128, 512], F32) for i in range(2)]; Btmp = [Buf(), Buf()]
                uu = [sbt(sti, f"fu{i}", [128, 512], F32) for i in range(2)]; Bu = [Buf(), Buf()]
                g = sbt(sti, "g", [128, NJ, 12 * 128], BF16)
                xTc = sbt(sti, "xTc", [128, 8, 12 * 128], BF16)
                t0 = 0
                it = 0
                BxTc_all = [Buf() for _ in range(12)]
                for ch in FCH:
                    ntok = ch * 128
                    Bg = [Buf() for _ in range(ch)]
                    BxTc = [Buf() for _ in range(ch)]
                    def load_xTc(t0_, ch_):
                        for q in range(ch_ // 4):
                            S.dma(xTc[:, :, q * 512:(q + 1) * 512], xT_scr[:, :, t0_ * 128 + q * 512:t0_ * 128 + (q + 1) * 512],
                                  reads=BxTs[t0_ + 4 * q:t0_ + 4 * q + 4], writes=BxTc[4 * q:4 * q + 4])
                    load_xTc(t0, ch)
                    FSTG = DBG.get("ffn_stage", 9)
                    def load_wup(j):
                        wi = j % 2
                        S.dma(wst2[wi][:], wup[li, j].rearrange("(k p) c -> p k c", p=128), writes=[Bwst2[wi]])
                        cast(wbf[wi][:], wst2[wi][:], [Bwst2[wi]], [Bwbf[wi]])
                    for j in range(NJ if FSTG >= 1 else 0):
                        wi = j % 2
                        if not (t0 > 0 and j < 2):
                            load_wup(j)
                        if t0 == 0:
                            load_wrow(wd, Bwd, wdn[li], j, wd_stgs)
                        if with_sample and t0 == 0:
                            for k in range(8):
                                S.op("pe", L("matmul", P[5][0:NS, 0:256], lhsT=xs_T[:, k, :], rhs=wbf[wi][:, k, :], start=(k == 0), stop=(k == 7)),
                                     reads=[Bwbf[wi], BxsT], writes=[BP[5]], signal=(k == 7))
                            S.op("act", L("copy", out=abt[wi][:], in_=P[5][0:NS, 0:256]), reads=[BP[5]], writes=[Babt[wi]])
                            S.dma(ab_scr[:, :, j * 128:(j + 1) * 128], abt[wi][:].rearrange("p (s c) -> p s c", s=2), reads=[Babt[wi]], writes=[Babs])
                        for b in range(ch // 4):
                            i = it % 2
                            it += 1
                            pa, pb_ = (0, 1) if i == 0 else (2, 3)
                            cols = slice(b * 512, (b + 1) * 512)
                            for half, pi in ((0, pa), (1, pb_)):
                                for k in range(8):
                                    S.op("pe", L("matmul", P[pi][:, :], lhsT=wbf[wi][:, k, half * 128:(half + 1) * 128], rhs=xTc[:, k, cols], start=(k == 0), stop=(k == 7)),
                                         reads=[Bwbf[wi]] + BxTc[4 * b:4 * b + 4], writes=[BP[pi]], signal=(k == 7))
                            a = asb[i]
                            S.op("dve", L("tensor_copy", out=a[:, 0:2], in_=ah[:, j, :]), reads=[Bah], writes=[Basb[i]])
                            S.op("act", L("copy", out=a[:, 2:514], in_=P[pa][:, :]), reads=[BP[pa]], writes=[Basb[i]])
                            S.op("dve", L("tensor_copy", out=ah[:, j, :], in_=a[:, 512:514]), reads=[Basb[i]], writes=[Bah])
                            tm = tmp[i]
                            S.op("dve", L("tensor_scalar", out=tm[:], in0=a[:, 0:512], scalar1=fk[:, j, 0:1], scalar2=None, op0=ALU.mult), reads=[Basb[i], Bfk], writes=[Btmp[i]])
                            S.op("dve", L("scalar_tensor_tensor", out=tm[:], in0=a[:, 1:513], scalar=fk[:, j, 1:2], in1=tm[:], op0=ALU.mult, op1=ALU.add), reads=[Basb[i], Bfk, Btmp[i]], writes=[Btmp[i]])
                            S.op("dve", L("scalar_tensor_tensor", out=tm[:], in0=a[:, 2:514], scalar=fk[:, j, 2:3], in1=tm[:], op0=ALU.mult, op1=ALU.add), reads=[Basb[i], Bfk, Btmp[i]], writes=[Btmp[i]])
                            u = uu[i]
                            S.op("act", L("activation", out=u[:], in_=tm[:], func=AF.Gelu), reads=[Btmp[i]], writes=[Bu[i]])
                            S.op("dve", L("tensor_tensor", out=g[:, j, cols], in0=P[pb_][:, :], in1=u[:], op=ALU.mult), reads=[Bu[i], BP[pb_]], writes=Bg[4 * b:4 * b + 4])
                    def emit_dn(tt):
                        pa, pb_ = (0, 1) if tt % 2 == 0 else (2, 3)
                        for c, pi in enumerate((pa, pb_)):
                            for j in range(NJ):
                                S.op("pe", L("matmul", P[pi][:, :], lhsT=g[:, j, tt * 128:(tt + 1) * 128], rhs=wd[:, j, c * 512:(c + 1) * 512], start=(j == 0), stop=(j == NJ - 1)),
                                     reads=[Bwd, Bg[tt]], writes=[BP[pi]], signal=(j == NJ - 1))
                    if t0 + ch < NT and FSTG >= 1:
                        load_wup(0)
                        load_wup(1)
                    if FSTG >= 2:
                        emit_dn(0)
                    for tt in range(ch if FSTG >= 2 else 0):
                        t = t0 + tt
                        pa, pb_ = (0, 1) if tt % 2 == 0 else (2, 3)
                        if tt + 1 < ch:
                            emit_dn(tt + 1)
                        ln.tile(P[pa][:, :], P[pb_][:, :], BP[pa], BP[pb_], xres[t * 128:(t + 1) * 128, :], [Bxres[t]], ln_dst(li, 1, t), [Bxres[t]],
                                xT_scr[:, :, t * 128:(t + 1) * 128], [BxTs[t]])
                    t0 += ch
                sti.close()
                with ExitStack() as st3:
                    rowb = sbt(st3, "rowb", [2, DFF], F32); Browb = Buf()
                    for j in range(NJ if FSTG >= 3 else 0):
                        fm_block_to_rows(ah[:, j, :], Bah, 2, rowb, Browb, j)
                    if FSTG >= 3:
                        S.dma(ffnp[li, :, :], rowb[:], reads=[Browb])
                if with_sample:
                    with ExitStack() as st2:
                        a_b = sbt(st2, "ab_s", [NS, 2, DFF], F32); Bab = Buf()
                        S.dma(a_b[:], ab_scr[:, :, :], reads=[Babs], writes=[Bab])
                        stf = sbt(st2, "stf", [NS, 2, DFF], F32); Bstf = Buf()
                        S.dma(stf[:], st_ffn[li], writes=[Bstf])
                        ktm = sbt(st2, "ktm", [NS, 3, DFF], F32); Bktm = Buf()
                        for tap in range(3):
                            S.dma(ktm[:, tap, :], ffk_tm[li, tap:tap + 1, :].partition_broadcast(NS), writes=[Bktm])
                        t1 = sbt(st2, "t1", [NS, DFF], F32); t2 = sbt(st2, "t2", [NS, DFF], F32); Bt1, Bt2 = Buf(), Buf()
                        tt_ = lambda o, a, b, op, rd, wr: S.op("dve", L("tensor_tensor", out=o, in0=a, in1=b, op=op), reads=rd, writes=wr)
                        tt_(t1[:], stf[:, 0, :], ktm[:, 0, :], ALU.mult, [Bstf, Bktm], [Bt1])
                        tt_(t2[:], stf[:, 1, :], ktm[:, 1, :], ALU.mult, [Bstf, Bktm], [Bt2])
                        tt_(t1[:], t1[:], t2[:], ALU.add, [Bt1, Bt2], [Bt1])
                        tt_(t2[:], a_b[:, 0, :], ktm[:, 2, :], ALU.mult, [Bab, Bktm, Bt1], [Bt2])
                        tt_(t1[:], t1[:], t2[:], ALU.add, [Bt1, Bt2], [Bt1])
                        S.op("act", L("activation", out=t2[:], in_=t1[:], func=AF.Gelu), reads=[Bt1], writes=[Bt2])
                        tt_(t1[:], t2[:], a_b[:, 1, :], ALU.mult, [Bt2, Bab], [Bt1])
                        gsT, BgsT = to_T(st2, t1[:], Bt1, NJ, "gs")
                        ln_sample(ln, li, 1, gsT, BgsT, NJ, wd, Bwd)
                        S.dma(ffns[li, :, 1, :], a_b[:, 0, :], reads=[Bab])

        def attn_phase(li, ia):
            with ExitStack() as st:
                xT, BxT = load_xT(st)
                wst = sbt(st, "wst", [128, 4, 384], F32); Bwst = Buf()
                wbf = [sbt(st, f"wbf{i}", [128, 8, 384], BF16) for i in range(2)]; Bwbf = [Buf(), Buf()]
                rt0 = sbt(st, "rt0", [128, NT, 2, 32], F32); rt = [rt0, rt0]; Brt0 = Buf(); Brt = [Brt0, Brt0]
                QT = sbt(st, "QT", [128, NT, 128], BF16); KTt = sbt(st, "KT", [128, NT, 128], BF16)
                V = sbt(st, "V", [128, NT, 128], BF16)
                BQ = [Buf() for _ in range(NT)]; BK = [Buf() for _ in range(NT)]; BV = [Buf() for _ in range(NT)]
                accN = sbt(st, "accN", [128, T], F32); accD = sbt(st, "accD", [128, T], F32)
                Bacc = [Buf() for _ in range(8)]
                qkr = [sbt(st, f"qkr{i}", [128, 4, 256], F32) for i in range(2)]; Bqkr = [Buf(), Buf()]
                tb = [sbt(st, f"tb{i}", [128, 4, 4, 2, 32], F32) for i in range(2)]; Btb = [Buf(), Buf()]
                qkb = [sbt(st, f"qkb{i}", [128, 4, 256], BF16) for i in range(2)]; Bqkb = [Buf(), Buf()]
                raw = [sbt(st, f"raw{i}", [128, 4, 384], F32) for i in range(2)]; Braw = [Buf(), Buf()]
                Et = [sbt(st, f"E{i}", [128, 256], BF16) for i in range(4)]; BE = [Buf() for _ in range(4)]
                Em = [sbt(st, f"Em{i}", [128, 256], BF16) for i in range(6)]; BEm = [Buf() for _ in range(6)]
                mixT = [sbt(st, f"mixThp{i}", [128, 512], BF16) for i in range(2)]; Bmix = [Buf(), Buf()]
                xg = [sbt(st, f"xg{i}", [128, 8, 128], BF16) for i in range(4)]; Bxg = [Buf() for _ in range(4)]
                gcount = [0]
                if with_sample:
                    xs_T, BxsT = load_xTs(st)
                    rs_t = sbt(st, "rs_t", [NS, 2, 32], F32); Brs = Buf()
                    S.dma(rs_t[:], rope_s[:, :, :], writes=[Brs])
                    qsr = [sbt(st, f"qsr{i}", [NS, 384], F32) for i in range(2)]; Bqsr = [Buf(), Buf()]
                    tbs = [sbt(st, f"tbs{i}", [NS, 4, 2, 32], F32) for i in range(2)]; Btbs = [Buf(), Buf()]
                unit = 0
                for hp in range(DBG.get("hp_n", 8)):
                    for gi, (win, d) in enumerate(GROUPS):
                        if gi not in DBG.get("groups", (0, 1, 2)):
                            continue
                        nblk = NT // d
                        ui = unit % 2
                        unit += 1
                        if bg:
                            S.dma(*bg.pop(0))
                        wsrc = wqkv[ia, hp, gi].rearrange("(k p) c -> p k c", p=128)
                        for kh in range(2):
                            S.dma(wst[:], wsrc[:, kh * 4:(kh + 1) * 4, :], writes=[Bwst])
                            cast(wbf[ui][:, kh * 4:(kh + 1) * 4, :], wst[:], [Bwst], [Bwbf[ui]])
                        S.dma(rt[ui][:], rope[gi], writes=[Brt[ui]])
                        w = wbf[ui]
                        r_t = rt[ui]
                        if with_sample:
                            for k in range(8):
                                S.op("pe", L("matmul", P[5][0:NS, 0:384], lhsT=xs_T[:, k, :], rhs=w[:, k, :], start=(k == 0), stop=(k == 7)),
                                     reads=[Bwbf[ui], BxsT], writes=[BP[5]], signal=(k == 7))
                            qs = qsr[ui]; ts_ = tbs[ui]
                            ssrc = P[5][0:NS, 0:256].rearrange("p (j h f) -> p j h f", j=4, h=2)
                            scos = rs_t[:, 0, :].unsqueeze(1).unsqueeze(1).to_broadcast([NS, 4, 2, 32])
                            ssin = rs_t[:, 1, :].unsqueeze(1).to_broadcast([NS, 4, 32])
                            sdst = qs[:, 0:256].rearrange("p (j h f) -> p j h f", j=4, h=2)
                            S.op("dve", L("tensor_tensor", out=sdst, in0=ssrc, in1=scos, op=ALU.mult), reads=[BP[5], Brs], writes=[Bqsr[ui]])
                            S.op("dve", L("tensor_tensor", out=ts_[:, :, 0, :], in0=ssrc[:, :, 1, :], in1=ssin, op=ALU.mult), reads=[BP[5], Brs], writes=[Btbs[ui]])
                            S.op("dve", L("tensor_tensor", out=ts_[:, :, 1, :], in0=ssrc[:, :, 0, :], in1=ssin, op=ALU.mult), reads=[BP[5], Brs], writes=[Btbs[ui]])
                            S.op("act", L("copy", out=qs[:, 256:384], in_=P[5][0:NS, 256:384]), reads=[BP[5]], writes=[Bqsr[ui]])
                            S.op("dve", L("tensor_tensor", out=sdst[:, :, 0, :], in0=sdst[:, :, 0, :], in1=ts_[:, :, 0, :], op=ALU.subtract), reads=[Bqsr[ui], Btbs[ui]], writes=[Bqsr[ui]])
                            S.op("dve", L("tensor_tensor", out=sdst[:, :, 1, :], in0=sdst[:, :, 1, :], in1=ts_[:, :, 1, :], op=ALU.add), reads=[Bqsr[ui], Btbs[ui]], writes=[Bqsr[ui]])
                            S.dma(qkvs_scr[:, gi, :, hp * 128:(hp + 1) * 128], qs[:].rearrange("p (s c) -> p s c", s=3), reads=[Bqsr[ui]], writes=[Bqkvs])
                        PSTG = DBG.get("proj_stage", 9)

                        def proj_mm(bt):
                            for bi in range(4):
                                blk = bt * 4 + bi
                                r, nb = blk // nblk, blk % nblk
                                tok0 = r + d * 128 * nb
                                toks = slice(tok0, tok0 + d * 127 + 1, d)
                                tl = sorted(set([tok0 // 128 + x for x in range(0, (d * 127) // 128 + 1)]))
                                if d == 1:
                                    lsrc = lambda k, toks=toks: xT[:, k, toks]
                                    lreads = [BxT[x] for x in tl if x < NT]
                                else:
                                    gsl = gcount[0] % 4
                                    gcount[0] += 1
                                    xg_ = xg[gsl]
                                    S.op(("pool", "act")[gsl % 2], L(("tensor_copy", "copy")[gsl % 2], out=xg_[:], in_=xT[:, :, toks]),
                                         reads=[BxT[x] for x in tl if x < NT], writes=[Bxg[gsl]])
                                    lsrc = lambda k, xg_=xg_: xg_[:, k, :]
                                    lreads = [Bxg[gsl]]
                                for k in range(8):
                                    S.op("pe", L("matmul", P[bi][:, 0:384], lhsT=lsrc(k), rhs=w[:, k, :], start=(k == 0), stop=(k == 7)),
                                         reads=[Bwbf[ui]] + lreads, writes=[BP[bi]], signal=(k == 7))

                        def proj_post_a(bt):
                            i = bt % 2
                            q_r = qkr[i]; t_b = tb[i]; q_b = qkb[i]; rw = raw[i]
                            for bi in range(4):
                                S.op("act", L("copy", out=rw[:, bi, :], in_=P[bi][:, 0:384]), reads=[BP[bi]], writes=[Braw[i]])
                            rqk = rw[:, :, 0:256].rearrange("p b (j h f) -> p b j h f", j=4, h=2)
                            qd = q_r[:].rearrange("p b (j h f) -> p b j h f", j=4, h=2)
                            cosb = r_t[:, bt * 4:bt * 4 + 4, 0, :].unsqueeze(2).to_broadcast([128, 4, 4, 32])
                            sinb = r_t[:, bt * 4:bt * 4 + 4, 1, :].unsqueeze(2).to_broadcast([128, 4, 4, 32])
                            for h_ in range(2):
                                S.op("dve", L("tensor_tensor", out=qd[:, :, :, h_, :], in0=rqk[:, :, :, h_, :], in1=cosb, op=ALU.mult), reads=[Braw[i], Brt[ui]], writes=[Bqkr[i]])
                                S.op("dve", L("tensor_tensor", out=t_b[:, :, :, h_, :], in0=rqk[:, :, :, 1 - h_, :], in1=sinb, op=ALU.mult), reads=[Braw[i], Brt[ui]], writes=[Btb[i]])
                            S.op("pool", L("tensor_tensor", out=qd[:, :, :, 0, :], in0=qd[:, :, :, 0, :], in1=t_b[:, :, :, 0, :], op=ALU.subtract), reads=[Bqkr[i], Btb[i]], writes=[Bqkr[i]])
                            S.op("dve", L("tensor_tensor", out=qd[:, :, :, 1, :], in0=qd[:, :, :, 1, :], in1=t_b[:, :, :, 1, :], op=ALU.add), reads=[Bqkr[i], Btb[i]], writes=[Bqkr[i]])
                            S.op("act", L("copy", out=q_b[:], in_=q_r[:]), reads=[Bqkr[i]], writes=[Bqkb[i]])
                            S.op("pool", L("tensor_copy", out=V[:, bt * 4:bt * 4 + 4, :], in_=rw[:, :, 256:384]), reads=[Braw[i]], writes=BV[bt * 4:bt * 4 + 4])
                            for bi in range(4):
                                blk = bt * 4 + bi
                                r, nb = blk // nblk, blk % nblk
                                tok0 = r + d * 128 * nb
                                if tok0 >= T - win and DBG.get("kvout", True):
                                    row0 = tok0 - (T - win)
                                    rows = slice(row0, row0 + d * 127 + 1, d)
                                    S.dma(kvp[gi][ia, rows, hp * 128:(hp + 1) * 128], q_r[:, bi, 128:256], reads=[Bqkr[i]])
                                    S.dma(kvp[gi][ia, rows, 1024 + hp * 128:1024 + (hp + 1) * 128], rw[:, bi, 256:384], reads=[Braw[i]])

                        def proj_post_b(bt):
                            i = bt % 2
                            q_b = qkb[i]
                            pbk = PB[i]
                            for bi in range(4):
                                for s_ in range(2):
                                    S.op("pe", L("transpose", out=pbk[:, (bi * 2 + s_) * 128:(bi * 2 + s_ + 1) * 128], in_=q_b[:, bi, s_ * 128:(s_ + 1) * 128], identity=idb[:]),
                                         reads=[Bqkb[i], Bconst], writes=[BPB[i]], signal=(bi == 3 and s_ == 1))
                            pv = pbk[:].rearrange("p (b s n) -> p b s n", b=4, s=2)
                            S.op("dve", L("tensor_copy", out=QT[:, bt * 4:bt * 4 + 4, :], in_=pv[:, :, 0, :]), reads=[BPB[i]], writes=BQ[bt * 4:bt * 4 + 4])
                            S.op("act", L("copy", out=KTt[:, bt * 4:bt * 4 + 4, :], in_=pv[:, :, 1, :]), reads=[BPB[i]], writes=BK[bt * 4:bt * 4 + 4])

                        nbt = NT // 4 if PSTG >= 1 else 0
                        if nbt:
                            proj_mm(0)
                        for bt in range(nbt):
                            if PSTG >= 2:
                                proj_post_a(bt)
                            if bt + 1 < nbt:
                                proj_mm(bt + 1)
                            if PSTG >= 3:
                                proj_post_b(bt)

                        ecount = [0]
                        emidx = {}

                        def scores(blk):
                            r, nb = blk // nblk, blk % nblk
                            ncol = 256 if nb + 1 < nblk else 128
                            nqb = ncol // 128
                            for h in range(2):
                                pi = 4 + (ecount[0] % 2)
                                mi = ecount[0] % 6
                                ecount[0] += 1
                                hs = slice(64 * h, 64 * h + 64)
                                S.op("pe", L("matmul", P[pi][:, 0:ncol], lhsT=KTt[hs, blk, :], rhs=QT[hs, blk:blk + nqb, :], start=True, stop=False),
                                     reads=[BK[blk]] + BQ[blk:blk + nqb], writes=[BP[pi]], signal=False)
                                S.op("pe", L("matmul", P[pi][:, 0:ncol], lhsT=idb[:, :], rhs=mbias[:, 0:ncol], start=False, stop=True),
                                     reads=[Bconst], writes=[BP[pi]])
                                S.op("act", L("activation", out=Em[mi][:, 0:ncol], in_=P[pi][:, 0:ncol], func=AF.Exp, scale=0.125), reads=[BP[pi]], writes=[BEm[mi]])
                                emidx[(blk, h)] = mi

                        def pvstep(blk):
                            r, nb = blk // nblk, blk % nblk
                            qi = blk % 4
                            bnk = (blk // 4) % 2
                            pn, pd = (0, 1) if bnk == 0 else (2, 3)
                            for h in range(2):
                                hs = slice(64 * h, 64 * h + 64)
                                srcs = []
                                if nb >= 1:
                                    srcs.append((blk - 1, emidx[(blk - 1, h)], slice(128, 256)))
                                srcs.append((blk, emidx[(blk, h)], slice(0, 128)))
                                for (pp, lhs_fn) in ((pn, lambda kb, hs=hs: V[:, kb, hs]), (pd, lambda kb: onesb[:, :])):
                                    for si, (kb, mi, cs) in enumerate(srcs):
                                        S.op("pe", L("matmul", P[pp][hs, qi * 128:(qi + 1) * 128], lhsT=lhs_fn(kb), rhs=Em[mi][:, cs], start=(si == 0), stop=(si == len(srcs) - 1)),
                                             reads=[BV[kb], BEm[mi], Bconst], writes=[BP[pp]], signal=(si == len(srcs) - 1))
                            if qi == 3:
                                b0 = blk - 3
                                r0, nb0 = b0 // nblk, b0 % nblk
                                for (pp, acc) in ((pn, accN), (pd, accD)):
                                    accv = acc[:].rearrange("p (n r) -> p r n", r=d)
                                    if nblk >= 4:
                                        dst = accv[:, r0, 128 * nb0:128 * nb0 + 512]
                                        src = P[pp][:, :]
                                    else:
                                        dst = accv[:, r0:r0 + 2, 0:256]
                                        src = P[pp][:, :].rearrange("p (a n) -> p a n", a=2)
                                    if gi == 0:
                                        S.op("dve", L("tensor_copy", out=dst, in_=src), reads=[BP[pp]], writes=Bacc)
                                    else:
                                        S.op("dve", L("tensor_tensor", out=dst, in0=src, in1=dst, op=ALU.add), reads=[BP[pp]] + Bacc, writes=Bacc)

                        ncore = NT if DBG.get("core", True) else 0
                        if ncore:
                            scores(0)
                        for blk in range(ncore):
                            if blk + 1 < ncore:
                                scores(blk + 1)
                            pvstep(blk)
                    if not DBG.get("norm", True):
                        continue
                    for q in range(8):
                        cs = slice(q * 512, (q + 1) * 512)
                        S.op("dve", L("reciprocal", out=accD[:, cs], in_=accD[:, cs]), reads=Bacc, writes=Bacc)
                        S.op("dve", L("tensor_tensor", out=mixT[q % 2][:], in0=accN[:, cs], in1=accD[:, cs], op=ALU.mult), reads=Bacc, writes=[Bmix[q % 2]])
                        S.dma(mixT_scr[:, hp, cs], mixT[q % 2][:], reads=[Bmix[q % 2]], writes=[BmixT[hp]])
            if not DBG.get("wo", True):
                return
            with ExitStack() as st:
                mT = sbt(st, "mT", [128, 8, T], BF16); BmT = Buf()
                for hp in range(8):
                    S.dma(mT[:, hp, :], mixT_scr[:, hp, :], reads=[BmixT[hp]], writes=[BmT])
                def attn_sample(st2):
                    qk = sbt(st2, "qkvs", [NS, 3, 3, D], F32); Bqk = Buf()
                    S.dma(qk[:], qkvs_scr[:, :, :, :], reads=[Bqkvs], writes=[Bqk])
                    Kc = sbt(st2, "Kc", [128, D], F32); Vc = sbt(st2, "Vc", [128, D], F32); BKc, BVc = Buf(), Buf()
                    prod = sbt(st2, "prod", [128, D], F32); Bprod = Buf()
                    tmpv = sbt(st2, "tmpv", [128, D], F32); Btmpv = Buf()
                    Ssc = sbt(st2, "Ssc", [128, 16], F32); Es = sbt(st2, "Es", [128, 16], F32); BSs, BEs = Buf(), Buf()
                    pself = sbt(st2, "pself", [NS, D], F32); Bps = Buf()
                    sself = sbt(st2, "sself", [NS, 16], F32); eself = sbt(st2, "eself", [NS, 16], F32); Bss, Bes = Buf(), Buf()
                    first = True
                    for gi, (win, d) in enumerate(GROUPS):
                        for s_ in range(NS):
                            S.dma(Kc[:], c_in[gi][ia, s_, 0:win:d, 0:1024], writes=[BKc])
                            S.dma(Vc[:], c_in[gi][ia, s_, 0:win:d, 1024:2048], writes=[BVc])
                            for c in range(2):
                                S.op("pe", L("matmul", P[c][:, :], lhsT=sel_t[0:NS, s_, :], rhs=qk[:, gi, 0, c * 512:(c + 1) * 512], start=True, stop=True),
                                     reads=[Bqk, Bconst], writes=[BP[c]])
                                S.op("dve", L("tensor_tensor", out=prod[:, c * 512:(c + 1) * 512], in0=Kc[:, c * 512:(c + 1) * 512], in1=P[c][:, :], op=ALU.mult), reads=[BKc, BP[c]], writes=[Bprod])
                            S.op("dve", L("tensor_reduce", out=Ssc[:], in_=prod[:].rearrange("p (h f) -> p h f", h=16), axis=AX.X, op=ALU.add), reads=[Bprod], writes=[BSs])
                            S.op("act", L("activation", out=Es[:], in_=Ssc[:], func=AF.Exp, scale=0.125), reads=[BSs], writes=[BEs])
                            S.op("dve", L("tensor_tensor", out=tmpv[:].rearrange("p (h f) -> p h f", h=16), in0=Vc[:].rearrange("p (h f) -> p h f", h=16), in1=Es[:].unsqueeze(2).to_broadcast([128, 16, 64]), op=ALU.mult), reads=[BVc, BEs], writes=[Btmpv])
                            for c in range(2):
                                S.op("pe", L("matmul", P[2 + c][0:NS, :], lhsT=oneh_t[:, s_, :], rhs=tmpv[:, c * 512:(c + 1) * 512], start=first, stop=False),
                                     reads=[Btmpv, Bconst], writes=[BP[2 + c]])
                            S.op("pe", L("matmul", P[4][0:NS, 0:16], lhsT=oneh_t[:, s_, :], rhs=Es[:], start=first, stop=False), reads=[BEs, Bconst], writes=[BP[4]])
                            first = False
                        S.op("dve", L("tensor_tensor", out=pself[:], in0=qk[:, gi, 0, :], in1=qk[:, gi, 1, :], op=ALU.mult), reads=[Bqk], writes=[Bps])
                        S.op("dve", L("tensor_reduce", out=sself[:], in_=pself[:].rearrange("p (h f) -> p h f", h=16), axis=AX.X, op=ALU.add), reads=[Bps], writes=[Bss])
                        S.op("act", L("activation", out=eself[:], in_=sself[:], func=AF.Exp, scale=0.125), reads=[Bss], writes=[Bes])
                        S.op("dve", L("tensor_tensor", out=pself[:].rearrange("p (h f) -> p h f", h=16), in0=qk[:, gi, 2, :].rearrange("p (h f) -> p h f", h=16), in1=eself[:].unsqueeze(2).to_broadcast([NS, 16, 64]), op=ALU.mult), reads=[Bqk, Bes, Bss], writes=[Bps])
                        last = (gi == len(GROUPS) - 1)
                        for c in range(2):
                            S.op("pe", L("matmul", P[2 + c][0:NS, :], lhsT=idf[0:NS, 0:NS], rhs=pself[:, c * 512:(c + 1) * 512], start=False, stop=last), reads=[Bps, Bconst], writes=[BP[2 + c]])
                        S.op("pe", L("matmul", P[4][0:NS, 0:16], lhsT=idf[0:NS, 0:NS], rhs=eself[:], start=False, stop=last), reads=[Bes, Bconst], writes=[BP[4]])
                        S.dma(kvs[gi][ia, :, win - 1, 0:1024], qk[:, gi, 1, :], reads=[Bqk])
                        S.dma(kvs[gi][ia, :, win - 1, 1024:2048], qk[:, gi, 2, :], reads=[Bqk])
                    rden = sbt(st2, "rden", [NS, 16], F32); Brd = Buf()
                    mixs = sbt(st2, "mixs", [NS, D], F32); Bmx = Buf()
                    S.op("dve", L("reciprocal", out=rden[:], in_=P[4][0:NS, 0:16]), reads=[BP[4]], writes=[Brd])
                    for c in range(2):
                        S.op("dve", L("tensor_tensor", out=mixs[:, c * 512:(c + 1) * 512].rearrange("p (h f) -> p h f", h=8), in0=P[2 + c][0:NS, :].rearrange("p (h f) -> p h f", h=8), in1=rden[:, c * 8:(c + 1) * 8].unsqueeze(2).to_broadcast([NS, 8, 64]), op=ALU.mult), reads=[BP[2 + c], Brd], writes=[Bmx])
                    return to_T(st2, mixs[:], Bmx, 8, "mixs")
                proj_ln_phase(li, 0, mT, lambda t: [BmT], 8, wo[ia], st, sample_fn=attn_sample)

        def pool_phase(li):
            with ExitStack() as st0:
                pT = sbt(st0, "pooledT", [128, 8, T], BF16); BpT = [Buf() for _ in range(8)]
                us = sbt(st0, "us", [NS, D], F32); Bus = Buf()
                with ExitStack() as st:
                    xT, BxT = load_xT(st)
                    stg = sbt(st, "wf_stg", [128, 512], F32); Bstg = Buf()
                    w, Bw = load_wfull(st, "pwin", pwin, 8, stg, Bstg, dbl=False)
                    rc = sbt(st, "rc", [128, 8, 16], F32); Brc = Buf()
                    S.dma(rc[:], prcnt[:, :, :], writes=[Brc])
                    prow = sbt(st, "prow", [15, D], F32); Bprow = Buf()
                    ub = sbt(st, "ub", [128, 16 + T], F32); sA = sbt(st, "sA", [128, 16 + T], F32); sB = sbt(st, "sB", [128, 16 + T], F32)
                    Bub, BsA, BsB = Buf(), Buf(), Buf()
                    for (bu, Bb) in ((ub, Bub), (sA, BsA), (sB, BsB)):
                        S.op("pool", L("memset", bu[:, 0:16], 0.0), writes=[Bb])
                    if with_sample:
                        xs_T, BxsT = load_xTs(st)
                        for c in range(2):
                            for k in range(8):
                                S.op("pe", L("matmul", P[4 + c][0:NS, :], lhsT=xs_T[:, k, :], rhs=w[:, k, c * 512:(c + 1) * 512], start=(k == 0), stop=(k == 7)),
                                     reads=[Bw, BxsT], writes=[BP[4 + c]], signal=(k == 7))
                            S.op("act", L("copy", out=us[:, c * 512:(c + 1) * 512], in_=P[4 + c][0:NS, :]), reads=[BP[4 + c]], writes=[Bus])
                    for j in range(8):
                        for blk in range(8):
                            pi = blk % 4
                            for k in range(8):
                                S.op("pe", L("matmul", P[pi][:, :], lhsT=w[:, k, j * 128:(j + 1) * 128], rhs=xT[:, k, blk * 512:(blk + 1) * 512], start=(k == 0), stop=(k == 7)),
                                     reads=[Bw] + BxT[4 * blk:4 * blk + 4], writes=[BP[pi]], signal=(k == 7))
                            S.op("act", L("copy", out=ub[:, 16 + blk * 512:16 + (blk + 1) * 512], in_=P[pi][:, :]), reads=[BP[pi]], writes=[Bub])
                        wj = POOL_W[j // 2]
                        cur, Bcur = ub, Bub
                        step = 1
                        bufs = [(sA, BsA), (sB, BsB)]
                        bi = 0
                        while step < wj:
                            nxt, Bn = bufs[bi % 2]
                            bi += 1
                            for hh in range(2):
                                cs = slice(16 + hh * 2048, 16 + (hh + 1) * 2048)
                                cs2 = slice(16 - step + hh * 2048, 16 - step + (hh + 1) * 2048)
                                S.op(("dve", "pool")[hh], L("tensor_tensor", out=nxt[:, cs], in0=cur[:, cs], in1=cur[:, cs2], op=ALU.add), reads=[Bcur], writes=[Bn])
                            cur, Bcur = nxt, Bn
                            step *= 2
                        S.op("dve", L("tensor_tensor", out=cur[:, 16:32], in0=cur[:, 16:32], in1=rc[:, j, :], op=ALU.mult), reads=[Bcur, Brc], writes=[Bcur])
                        S.op("dve", L("tensor_scalar", out=cur[:, 16:32], in0=cur[:, 16:32], scalar1=float(wj), scalar2=None, op0=ALU.mult), reads=[Bcur], writes=[Bcur])
                        for hh in range(2):
                            cs = slice(16 + hh * 2048, 16 + (hh + 1) * 2048)
                            co = slice(hh * 2048, (hh + 1) * 2048)
                            S.op("dve", L("scalar_tensor_tensor", out=pT[:, j, co], in0=cur[:, cs], scalar=1.0 / wj, in1=ub[:, cs], op0=ALU.mult, op1=ALU.subtract), reads=[Bcur, Bub], writes=[BpT[j]])
                        fm_block_to_rows(ub[:, 16 + T - 15:16 + T], Bub, 15, prow, Bprow, j)
                    S.dma(poolp[:, :], prow[:], reads=[Bprow])
                with ExitStack() as st:
                    zT = pT
                    wgs = sbt(st, "wgs", [128, 4, 2, 256], F32); wgb = sbt(st, "wgb", [128, 4, 2, 256], BF16); Bwg = Buf()
                    psc = sbt(st, "psc", [128, 8], F32)
                    S.dma(wgs[:], pwgrp.rearrange("g (k p) c -> p g k c", p=128), writes=[Bwg])
                    S.dma(psc[:], pscale[:, :], writes=[Bwg])
                    S.op("dve", L("tensor_copy", out=wgb[:], in_=wgs[:]), reads=[Bwg], writes=[Bwg])
                    n = 0
                    for blk in range(8):
                        for gi in range(4):
                            pis = (0, 1) if n % 2 == 0 else (2, 3)
                            n += 1
                            for mt in range(2):
                                pi = pis[mt]
                                for kt in range(2):
                                    S.op("pe", L("matmul", P[pi][:, :], lhsT=wgb[:, gi, kt, mt * 128:(mt + 1) * 128], rhs=pT[:, 2 * gi + kt, blk * 512:(blk + 1) * 512], start=(kt == 0), stop=(kt == 1)),
                                         reads=[Bwg, BpT[2 * gi], BpT[2 * gi + 1]], writes=[BP[pi]], signal=(kt == 1))
                            for mt in range(2):
                                pi = pis[mt]
                                jt = 2 * gi + mt
                                S.op("act", L("activation", out=zT[:, jt, blk * 512:(blk + 1) * 512], in_=P[pi][:, :], func=AF.Copy, scale=psc[:, jt:jt + 1]), reads=[BP[pi], Bwg], writes=[BpT[jt]])
                    def pool_sample(st2):
                        sp = sbt(st2, "sp", [NS, 15, 256], F32); Bsp = Buf()
                        ws = sbt(st2, "ws", [NS, D], F32); Bws = Buf()
                        for gi, w_ in enumerate(POOL_W):
                            cs = slice(gi * 256, (gi + 1) * 256)
                            S.dma(sp[:, 0:w_ - 1, :], st_pool[:, 15 - (w_ - 1):15, cs], writes=[Bsp])
                            S.op("dve", L("tensor_reduce", out=ws[:, cs], in_=sp[:, 0:w_ - 1, :].rearrange("p r c -> p c r"), axis=AX.X, op=ALU.add), reads=[Bsp], writes=[Bws])
                            S.op("dve", L("tensor_tensor", out=ws[:, cs], in0=ws[:, cs], in1=us[:, cs], op=ALU.add), reads=[Bws, Bus], writes=[Bws])
                            S.op("dve", L("scalar_tensor_tensor", out=ws[:, cs], in0=ws[:, cs], scalar=1.0 / w_, in1=us[:, cs], op0=ALU.mult, op1=ALU.subtract), reads=[Bws, Bus], writes=[Bws])
                        pTs, BpTs = to_T(st2, ws[:], Bws, 8, "pls")
                        zTs = sbt(st2, "zTs", [128, 8, NS], BF16); BzTs = Buf()
                        for gi in range(4):
                            for mt in range(2):
                                for kt in range(2):
                                    S.op("pe", L("matmul", P[4][:, 0:NS], lhsT=wgb[:, gi, kt, mt * 128:(mt + 1) * 128], rhs=pTs[:, 2 * gi + kt, :], start=(kt == 0), stop=(kt == 1)),
                                         reads=[Bwg, BpTs], writes=[BP[4]], signal=(kt == 1))
                                jt = 2 * gi + mt
                                S.op("act", L("activation", out=zTs[:, jt, :], in_=P[4][:, 0:NS], func=AF.Copy, scale=psc[:, jt:jt + 1]), reads=[BP[4], Bwg], writes=[BzTs])
                        S.dma(pools[:, 14, :], us[:], reads=[Bus])
                        return zTs, BzTs
                    proj_ln_phase(li, 0, zT, lambda t: BpT, 8, pwout, st, sample_fn=pool_sample)

        def sconv_phase(li):
            with ExitStack() as st0:
                yT = sbt(st0, "yT", [128, 8, T], BF16); ByT = Buf()
                gs3 = sbt(st0, "gs3", [NS, 8, 384], F32); Bgs3 = Buf()
                with ExitStack() as st:
                    xT, BxT = load_xT(st)
                    wst = sbt(st, "wst", [128, 8, 384], F32); Bwst = Buf()
                    wbf = [sbt(st, f"wbf{i}", [128, 8, 384], BF16) for i in range(2)]; Bwbf = [Buf(), Buf()]
                    skt = sbt(st, "skt", [128, 8, 3], F32); Bsk = Buf()
                    S.dma(skt[:], sk[:, :, :], writes=[Bsk])
                    pb2 = [sbt(st, f"pb2_{i}", [128, 514], F32) for i in range(2)]; Bpb2 = [Buf(), Buf()]
                    srow = sbt(st, "srow", [2, D], F32); Bsrow = Buf()
                    hb = [sbt(st, f"hb{i}", [128, 512], F32) for i in range(2)]; Bhb = [Buf(), Buf()]
                    tmp = [sbt(st, f"stmp{i}", [128, 512], F32) for i in range(2)]; Btmp = [Buf(), Buf()]
                    if with_sample:
                        xs_T, BxsT = load_xTs(st)
                    for j in range(8):
                        wi = j % 2
                        S.dma(wst[:], swin[j].rearrange("(k p) c -> p k c", p=128), writes=[Bwst])
                        cast(wbf[wi][:], wst[:], [Bwst], [Bwbf[wi]])
                        if with_sample:
                            for k in range(8):
                                S.op("pe", L("matmul", P[5][0:NS, 0:384], lhsT=xs_T[:, k, :], rhs=wbf[wi][:, k, :], start=(k == 0), stop=(k == 7)),
                                     reads=[Bwbf[wi], BxsT], writes=[BP[5]], signal=(k == 7))
                            S.op("act", L("copy", out=gs3[:, j, :], in_=P[5][0:NS, 0:384]), reads=[BP[5]], writes=[Bgs3])
                        for blk in range(8):
                            i = blk % 2
                            pis = (0, 1, 2) if i == 0 else (3, 4, 5)
                            for s_, pi in enumerate(pis):
                                for k in range(8):
                                    S.op("pe", L("matmul", P[pi][:, :], lhsT=wbf[wi][:, k, s_ * 128:(s_ + 1) * 128], rhs=xT[:, k, blk * 512:(blk + 1) * 512], start=(k == 0), stop=(k == 7)),
                                         reads=[Bwbf[wi]] + BxT[4 * blk:4 * blk + 4], writes=[BP[pi]], signal=(k == 7))
                            pb_ = pb2[i]
                            if blk == 0:
                                S.op("pool", L("memset", pb_[:, 0:2], 0.0), writes=[Bpb2[i]])
                            else:
                                S.op("pool", L("tensor_copy", out=pb_[:, 0:2], in_=pb2[1 - i][:, 512:514]), reads=[Bpb2[1 - i]], writes=[Bpb2[i]])
                            S.op("act", L("copy", out=hb[i][:], in_=P[pis[2]][:, :]), reads=[BP[pis[2]]], writes=[Bhb[i]])
                            S.op("dve", L("tensor_tensor", out=pb_[:, 2:514], in0=P[pis[1]][:, :], in1=hb[i][:], op=ALU.mult), reads=[BP[pis[1]], Bhb[i]], writes=[Bpb2[i]])
                            tm = tmp[i]
                            S.op("dve", L("tensor_scalar", out=tm[:], in0=pb_[:, 0:512], scalar1=skt[:, j, 0:1], scalar2=None, op0=ALU.mult), reads=[Bpb2[i], Bsk], writes=[Btmp[i]])
                            S.op("dve", L("scalar_tensor_tensor", out=tm[:], in0=pb_[:, 1:513], scalar=skt[:, j, 1:2], in1=tm[:], op0=ALU.mult, op1=ALU.add), reads=[Bpb2[i], Bsk, Btmp[i]], writes=[Btmp[i]])
                            S.op("dve", L("scalar_tensor_tensor", out=tm[:], in0=pb_[:, 2:514], scalar=skt[:, j, 2:3], in1=tm[:], op0=ALU.mult, op1=ALU.add), reads=[Bpb2[i], Bsk, Btmp[i]], writes=[Btmp[i]])
                            S.op("dve", L("tensor_tensor", out=yT[:, j, blk * 512:(blk + 1) * 512], in0=P[pis[0]][:, :], in1=tm[:], op=ALU.mult), reads=[BP[pis[0]], Btmp[i]], writes=[ByT])
                            if blk == 7:
                                fm_block_to_rows(pb_[:, 512:514], Bpb2[i], 2, srow, Bsrow, j)
                    S.dma(sconvp[:, :], srow[:], reads=[Bsrow])
                with ExitStack() as st:
                    def sconv_sample(st2):
                        sst = sbt(st2, "sst", [NS, 2, D], F32); Bsst = Buf()
                        S.dma(sst[:], st_sconv[:, :, :], writes=[Bsst])
                        ktm = sbt(st2, "sktm", [NS, 3, D], F32); Bktm = Buf()
                        for tap in range(3):
                            S.dma(ktm[:, tap, :], sk_tm[tap:tap + 1, :].partition_broadcast(NS), writes=[Bktm])
                        p_s = sbt(st2, "p_s", [NS, D], F32); t1 = sbt(st2, "t1", [NS, D], F32); t2 = sbt(st2, "t2", [NS, D], F32)
                        Bp_s, Bt1, Bt2 = Buf(), Buf(), Buf()
                        tt_ = lambda o, a, b, op, rd, wr: S.op("dve", L("tensor_tensor", out=o, in0=a, in1=b, op=op), reads=rd, writes=wr)
                        v3 = lambda t_: t_[:].rearrange("p (j c) -> p j c", j=8)
                        tt_(v3(p_s), gs3[:, :, 128:256], gs3[:, :, 256:384], ALU.mult, [Bgs3], [Bp_s])
                        tt_(t1[:], sst[:, 0, :], ktm[:, 0, :], ALU.mult, [Bsst, Bktm], [Bt1])
                        tt_(t2[:], sst[:, 1, :], ktm[:, 1, :], ALU.mult, [Bsst, Bktm], [Bt2])
                        tt_(t1[:], t1[:], t2[:], ALU.add, [Bt1, Bt2], [Bt1])
                        tt_(t2[:], p_s[:], ktm[:, 2, :], ALU.mult, [Bp_s, Bktm, Bt1], [Bt2])
                        tt_(t1[:], t1[:], t2[:], ALU.add, [Bt1, Bt2], [Bt1])
                        tt_(v3(t1), v3(t1), gs3[:, :, 0:128], ALU.mult, [Bt1, Bgs3], [Bt1])
                        S.dma(sconvs[:, 1, :], p_s[:], reads=[Bp_s])
                        return to_T(st2, t1[:], Bt1, 8, "scs")
                    proj_ln_phase(li, 0, yT, lambda t: [ByT], 8, swout, st, sample_fn=sconv_sample)

        on = lambda name: phases is None or name in phases
        if on("init"):
            init_phase()
        ia = 0
        for li in range(DEPTH):
            kind = li % 3
            if kind == 0:
                if on(f"mix{li}"):
                    attn_phase(li, ia)
                ia += 1
            elif kind == 1:
                if on(f"mix{li}"):
                    pool_phase(li)
            else:
                if on(f"mix{li}"):
                    sconv_phase(li)
            if on(f"ffn{li}"):
                ffn_phase(li)
        while bg:
            S.dma(*bg.pop(0))
        S.final_wait_dmas()
        print("ops recorded:", S.nops, {e: len(v) for e, v in S.ops.items()})
        S.replay()
    return nc


def _rope_tables():
    half = 32
    inv = (np.float32(10000.0) ** (-np.arange(half, dtype=np.float32) / np.float32(half))).astype(np.float32)
    out = {}
    for (_, d) in GROUPS:
        nblk = NT // d
        tab = np.zeros((128, NT, 2, 32), np.float32)
        for blk in range(NT):
            r, nb = blk // nblk, blk % nblk
            pos = (r + d * (128 * nb + np.arange(128))).astype(np.float32)
            ang = pos[:, None] * inv[None, :]
            tab[:, blk, 0, :] = np.cos(ang)
            tab[:, blk, 1, :] = np.sin(ang)
        out[d] = tab
    angs = (np.float32(PAST) * inv).astype(np.float32)
    rs = np.zeros((NS, 2, 32), np.float32)
    rs[:, 0, :] = np.cos(angs)[None]
    rs[:, 1, :] = np.sin(angs)[None]
    return out, rs


def _consts():
    j = np.arange(128)[:, None]
    c = np.arange(256)[None, :]
    band = np.where(c < 128, (c >= j), ((c - 128) <= j)).astype(np.float32)
    ident = np.eye(128, dtype=np.float32)
    prcnt = np.zeros((128, 8, 16), np.float32)
    for jt in range(8):
        w = POOL_W[jt // 2]
        cnt = np.minimum(w, np.arange(16) + 1).astype(np.float32)
        prcnt[:, jt, :] = (1.0 / cnt)[None, :] / np.float32(w) * np.float32(w) / 1.0
    return band, ident, prcnt


_NC_CACHE = {}


def kernel(x_prompt, x_sample, cache_kv_w128, cache_kv_w512, cache_kv_w2048,
           state_pool, state_sconv, state_ffn_conv,
           attn_w_qkv, attn_w_o, pool_w_in, pool_w_grp, pool_scale, pool_w_out,
           sconv_w_in, sconv_k, sconv_w_out, ffn_w_up, ffn_k, ffn_w_down, ln_g, ln_b):
    f = lambda a: np.ascontiguousarray(np.asarray(a, dtype=np.float32))
    if "nc" not in _NC_CACHE:
        _NC_CACHE["nc"] = build()
    nc = _NC_CACHE["nc"]
    ropes, rope_s = _rope_tables()
    band, ident, prcnt = _consts()
    wq = f(attn_w_qkv).reshape(2, D, 3, 3, 8, 2, 64)
    wqkv = np.ascontiguousarray(wq.transpose(0, 4, 2, 1, 3, 5, 6)).reshape(2, 8, 3, D, 384)
    wu = f(ffn_w_up).reshape(DEPTH, D, 2, NJ, 128)
    wup = np.ascontiguousarray(wu.transpose(0, 3, 1, 2, 4)).reshape(DEPTH, NJ, D, 256)
    ffk = np.ascontiguousarray(f(ffn_k).reshape(DEPTH, 3, NJ, 128).transpose(0, 3, 2, 1))
    sw = f(sconv_w_in)[0].reshape(D, 3, 8, 128)
    swin = np.ascontiguousarray(sw.transpose(2, 0, 1, 3)).reshape(8, D, 384)
    skf = np.ascontiguousarray(f(sconv_k)[0].reshape(3, 8, 128).transpose(2, 1, 0))
    psc = np.ascontiguousarray(f(pool_scale)[0].reshape(8, 128).T)
    common = {
        "wqkv": wqkv, "wo": f(attn_w_o), "wup": wup, "wdn": f(ffn_w_down), "ffk": ffk, "ffk_tm": f(ffn_k),
        "pwin": f(pool_w_in)[0], "pwgrp": f(pool_w_grp)[0], "pscale": psc, "pwout": f(pool_w_out)[0], "prcnt": prcnt,
        "swin": swin, "sk": skf, "sk_tm": f(sconv_k)[0], "swout": f(sconv_w_out)[0],
        "lng": f(ln_g), "lnb": f(ln_b), "rope_s": rope_s, "band": band, "ident": ident,
        "sel": np.ascontiguousarray(np.broadcast_to(np.eye(NS, dtype=np.float32)[:, :, None], (NS, NS, 128))),
        "oneh": np.ascontiguousarray(np.broadcast_to(np.eye(NS, dtype=np.float32)[None, :, :], (128, NS, NS))),
    }
    for (_, d) in GROUPS:
        common[f"rope{d}"] = ropes[d]
    caches = {128: f(cache_kv_w128), 512: f(cache_kv_w512), 2048: f(cache_kv_w2048)}
    in_maps = []
    for c in range(8):
        seq = c // 2
        ss = slice(NS * c, NS * (c + 1))
        m = dict(common)
        m["x"] = f(x_prompt[seq])
        m["xs"] = f(x_sample[ss, 0, :])
        for w in (128, 512, 2048):
            m[f"c{w}"] = np.ascontiguousarray(caches[w][:, ss].reshape(2, NS, w, 2048))
        m["st_pool"] = f(state_pool[0, ss])
        m["st_sconv"] = f(state_sconv[0, ss])
        m["st_ffn"] = f(state_ffn_conv[:, ss])
        in_maps.append(m)
    res = run_bass_kernel_spmd(nc, in_maps, core_ids=list(range(8)))
    R = res.results
    B = 4
    y_prompt = np.stack([R[2 * b]["y"] for b in range(B)])
    y_sample = np.concatenate([R[c]["ys"] for c in range(8)], 0).reshape(32, 1, D)
    outs = [y_prompt, y_sample]
    for (w, _) in GROUPS:
        kp = np.stack([R[2 * b][f"kv{w}p"] for b in range(B)], 1).reshape(2, B, w, 2, 16, 64)
        ks = np.concatenate([R[c][f"kv{w}s"] for c in range(8)], 1).reshape(2, 32, w, 2, 16, 64)
        outs += [kp, ks]
    outs.append(np.stack([R[2 * b]["poolp"] for b in range(B)])[None])
    outs.append(np.concatenate([R[c]["pools"] for c in range(8)], 0)[None])
    outs.append(np.stack([R[2 * b]["sconvp"] for b in range(B)])[None])
    outs.append(np.concatenate([R[c]["sconvs"] for c in range(8)], 0)[None])
    outs.append(np.stack([R[2 * b]["ffnp"] for b in range(B)], 1))
    outs.append(np.concatenate([R[c]["ffns"] for c in range(8)], 1))
    return tuple(np.ascontiguousarray(o, dtype=np.float32) for o in outs)
```
